# Optimizing a Trainium2 kernel written in Bass

```python
import math
import jax, jax.numpy as jnp
from jax import lax
import numpy as np

D_MODEL = 1024
BATCH = 8
SEQ = 4096
DEPTH = 2

GDN_HEADS = 4
GDN_HEAD_DIM = 128
GDN_WIDTH = GDN_HEADS * GDN_HEAD_DIM
CONV_WIDTH = 4
CHUNK = 64
SWA_HEADS = 8
SWA_HEAD_DIM = 64
SWA_WIDTH = SWA_HEADS * SWA_HEAD_DIM
DILATED_CONFIGS = ((128, 1), (512, 4), (2048, 16))
BAND_BLOCK = 128
MIX_WIDTH = GDN_WIDTH + SWA_WIDTH
IN_SIZES = (3 * GDN_WIDTH, GDN_WIDTH, GDN_HEADS, GDN_HEADS, SWA_WIDTH, SWA_WIDTH, SWA_WIDTH)
IN_COLS = sum(IN_SIZES)
IN_SPLITS = tuple(int(c) for c in np.cumsum(IN_SIZES)[:-1])
D_FF = 2816
FFN_CONV_WIDTH = 3
RMS_EPS = 1e-6
L2_EPS = 1e-6

kernel_name = 'hybrid_gdn_dilated_alibi_convffn'


def rmsnorm(x, gain):
    xf = x.astype(jnp.float32)
    y = xf * lax.rsqrt(jnp.mean(xf * xf, axis=-1, keepdims=True) + RMS_EPS)
    return (y * gain.astype(jnp.float32)).astype(x.dtype)


def l2norm(x):
    xf = x.astype(jnp.float32)
    return xf * lax.rsqrt(jnp.sum(xf * xf, axis=-1, keepdims=True) + L2_EPS)


def causal_dwconv(x, w):
    k_width = w.shape[0]
    s = x.shape[1]
    xp = jnp.pad(x, ((0, 0), (k_width - 1, 0), (0, 0)))
    return sum(xp[:, j:j + s] * w[j] for j in range(k_width))


def alibi_slopes(n_heads):
    return jnp.asarray(2.0 ** (-8.0 * np.arange(1, n_heads + 1) / n_heads), dtype=jnp.float32)


def chunk_gated_delta_rule(q, k, v, g, beta):
    bn, h, s, dk = q.shape
    dv = v.shape[-1]
    n = s // CHUNK
    q = q * dk ** -0.5
    q = q.reshape(bn, h, n, CHUNK, dk)
    k = k.reshape(bn, h, n, CHUNK, dk)
    v = v.reshape(bn, h, n, CHUNK, dv)
    beta = beta.reshape(bn, h, n, CHUNK, 1)
    g = jnp.cumsum(g.reshape(bn, h, n, CHUNK), axis=-1)
    idx = jnp.arange(CHUNK)
    causal = idx[:, None] >= idx[None, :]
    strict = idx[:, None] > idx[None, :]
    decay = jnp.exp(jnp.where(causal, g[..., :, None] - g[..., None, :], -jnp.inf))
    k_beta = k * beta
    a = jnp.where(strict, jnp.einsum('bhnik,bhnjk->bhnij', k_beta, k) * decay, 0.0)
    rhs = jnp.concatenate([v * beta, k_beta * jnp.exp(g)[..., None]], axis=-1)
    sol = lax.linalg.triangular_solve(a, rhs, left_side=True, lower=True, unit_diagonal=True)
    u, w = sol[..., :dv], sol[..., dv:]
    a_qk = jnp.where(causal, jnp.einsum('bhnik,bhnjk->bhnij', q, k) * decay, 0.0)
    q_dec = q * jnp.exp(g)[..., None]
    g_last = g[..., -1:]
    k_dec = k * jnp.exp(g_last - g)[..., None]
    decay_last = jnp.exp(g_last[..., 0])

    def step(state, inp):
        qd, wc, uc, kd, aqk, dl = inp
        v_new = uc - jnp.einsum('bhik,bhkv->bhiv', wc, state)
        o = jnp.einsum('bhik,bhkv->bhiv', qd, state) + jnp.einsum('bhij,bhjv->bhiv', aqk, v_new)
        state = state * dl[..., None, None] + jnp.einsum('bhik,bhiv->bhkv', kd, v_new)
        return state, o

    xs = tuple(jnp.moveaxis(t, 2, 0) for t in (q_dec, w, u, k_dec, a_qk, decay_last))
    state0 = jnp.zeros((bn, h, dk, dv), jnp.float32)
    _, o = lax.scan(step, state0, xs)
    return jnp.moveaxis(o, 0, 2).reshape(bn, h, s, dv)


def gated_deltanet(qkv, z, b_logit, a_logit, conv_w, a_log, dt_bias, norm_gain):
    bn, s, _ = qkv.shape
    out_dtype = z.dtype
    qkv = jax.nn.silu(causal_dwconv(qkv, conv_w))
    q, k, v = jnp.split(qkv, 3, axis=-1)
    heads = lambda t: t.reshape(bn, s, GDN_HEADS, GDN_HEAD_DIM).transpose(0, 2, 1, 3)
    q = l2norm(heads(q))
    k = l2norm(heads(k))
    v = heads(v).astype(jnp.float32)
    beta = jax.nn.sigmoid(b_logit.astype(jnp.float32)).transpose(0, 2, 1)
    g = (-jnp.exp(a_log.astype(jnp.float32))
         * jax.nn.softplus(a_logit.astype(jnp.float32) + dt_bias.astype(jnp.float32))).transpose(0, 2, 1)
    o = chunk_gated_delta_rule(q, k, v, g, beta).transpose(0, 2, 1, 3)
    zf = z.reshape(bn, s, GDN_HEADS, GDN_HEAD_DIM).astype(jnp.float32)
    y = (o * lax.rsqrt(jnp.mean(o * o, axis=-1, keepdims=True) + RMS_EPS)
         * norm_gain.astype(jnp.float32) * jax.nn.silu(zf))
    return y.reshape(bn, s, GDN_WIDTH).astype(out_dtype)


def banded_causal_attention(q, k, v, n_back, step, slopes):
    nb, h, l, dh = q.shape
    nblk = -(-l // BAND_BLOCK)
    lp = nblk * BAND_BLOCK
    qb = jnp.pad(q, ((0, 0), (0, 0), (0, lp - l), (0, 0))).reshape(nb, h, nblk, BAND_BLOCK, dh)

    def band(t):
        tp = jnp.pad(t, ((0, 0), (0, 0), (BAND_BLOCK, lp - l), (0, 0)))
        prev = tp[:, :, :lp].reshape(nb, h, nblk, BAND_BLOCK, dh)
        cur = tp[:, :, BAND_BLOCK:].reshape(nb, h, nblk, BAND_BLOCK, dh)
        return jnp.concatenate([prev, cur], axis=3)

    kb, vb = band(k), band(v)
    s = jnp.einsum('nhbqd,nhbkd->nhbqk', qb, kb).astype(jnp.float32) * dh ** -0.5
    qi = jnp.arange(BAND_BLOCK)[:, None]
    kj = jnp.arange(2 * BAND_BLOCK)[None, :]
    dist = qi + BAND_BLOCK - kj
    key_pos = (jnp.arange(nblk) * BAND_BLOCK - BAND_BLOCK)[:, None, None] + kj
    valid = (dist >= 0) & (dist <= n_back) & (key_pos >= 0)
    alibi = -slopes[:, None, None, None] * (dist * step).astype(jnp.float32)
    s = jnp.where(valid, s + alibi, -jnp.inf)
    m = jnp.max(s, axis=-1, keepdims=True)
    p = jnp.exp(s - m)
    den = jnp.sum(p, axis=-1, keepdims=True)
    o = jnp.einsum('nhbqk,nhbkd->nhbqd', p, vb.astype(jnp.float32)) / den
    lse = (m + jnp.log(den))[..., 0]
    return o.reshape(nb, h, lp, dh)[:, :, :l], lse.reshape(nb, h, lp)[:, :, :l]


def dilated_attention(q, k, v, slopes):
    bn, s, h, dh = q.shape
    outs, lses = [], []
    for window, dilation in DILATED_CONFIGS:
        l = s // dilation
        sub = lambda t: t.reshape(bn, l, dilation, h, dh).transpose(0, 2, 3, 1, 4).reshape(bn * dilation, h, l, dh)
        o, lse = banded_causal_attention(sub(q), sub(k), sub(v), window // dilation, dilation, slopes)
        outs.append(o.reshape(bn, dilation, h, l, dh).transpose(0, 3, 1, 2, 4).reshape(bn, s, h, dh))
        lses.append(lse.reshape(bn, dilation, h, l).transpose(0, 3, 1, 2).reshape(bn, s, h))
    weights = jax.nn.softmax(jnp.stack(lses), axis=0)
    return jnp.einsum('gbsh,gbshd->bshd', weights, jnp.stack(outs))


def setup_inputs(seed: int = 0) -> dict:
    key = jax.random.key(seed)
    ks = jax.random.split(key, 16)
    nrm = lambda kk, shape, fan_in: jax.random.normal(kk, shape, jnp.float32) * fan_in ** -0.5
    x = jax.random.normal(ks[0], (BATCH, SEQ, D_MODEL), jnp.float32)
    ln1 = 1.0 + 0.02 * jax.random.normal(ks[1], (DEPTH, D_MODEL), jnp.float32)
    w_in = nrm(ks[2], (DEPTH, D_MODEL, IN_COLS), D_MODEL)
    conv_qkv = nrm(ks[3], (DEPTH, CONV_WIDTH, 3 * GDN_WIDTH), CONV_WIDTH)
    a_log = jnp.log(jax.random.uniform(ks[4], (DEPTH, GDN_HEADS), jnp.float32, minval=1.0, maxval=16.0))
    dt = jnp.exp(jax.random.uniform(ks[5], (DEPTH, GDN_HEADS), jnp.float32,
                                    minval=math.log(1e-3), maxval=math.log(1e-1)))
    dt_bias = jnp.log(jnp.expm1(dt))
    gdn_norm = 1.0 + 0.02 * jax.random.normal(ks[6], (DEPTH, GDN_HEAD_DIM), jnp.float32)
    w_out = nrm(ks[7], (DEPTH, MIX_WIDTH, D_MODEL), MIX_WIDTH)
    ln2 = 1.0 + 0.02 * jax.random.normal(ks[8], (DEPTH, D_MODEL), jnp.float32)
    w_gate = nrm(ks[9], (DEPTH, D_MODEL, D_FF), D_MODEL)
    w_up = nrm(ks[10], (DEPTH, D_MODEL, D_FF), D_MODEL)
    ffn_conv = nrm(ks[11], (DEPTH, FFN_CONV_WIDTH, D_FF), FFN_CONV_WIDTH)
    w_down = nrm(ks[12], (DEPTH, D_FF, D_MODEL), D_FF)
    ln_f = 1.0 + 0.02 * jax.random.normal(ks[13], (D_MODEL,), jnp.float32)
    return {'x': x, 'ln1': ln1, 'w_in': w_in, 'conv_qkv': conv_qkv, 'a_log': a_log, 'dt_bias': dt_bias,
            'gdn_norm': gdn_norm, 'w_out': w_out, 'ln2': ln2, 'w_gate': w_gate, 'w_up': w_up,
            'ffn_conv': ffn_conv, 'w_down': w_down, 'ln_f': ln_f}


def reference(x, ln1, w_in, conv_qkv, a_log, dt_bias, gdn_norm, w_out, ln2, w_gate, w_up, ffn_conv, w_down, ln_f):
    bn, s, _ = x.shape
    slopes = alibi_slopes(SWA_HEADS)
    heads_b = lambda t: t.reshape(bn, s, SWA_HEADS, SWA_HEAD_DIM)
    for l in range(DEPTH):
        h = rmsnorm(x, ln1[l])
        proj = h @ w_in[l]
        qkv_a, z_a, b_a, a_a, q_b, k_b, v_b = jnp.split(proj, IN_SPLITS, axis=-1)
        y_a = gated_deltanet(qkv_a, z_a, b_a, a_a, conv_qkv[l], a_log[l], dt_bias[l], gdn_norm[l])
        y_b = dilated_attention(heads_b(q_b), heads_b(k_b), heads_b(v_b), slopes)
        y_b = y_b.reshape(bn, s, SWA_WIDTH).astype(x.dtype)
        x = x + jnp.concatenate([y_a, y_b], axis=-1) @ w_out[l]
        h = rmsnorm(x, ln2[l])
        gate = causal_dwconv(h @ w_gate[l], ffn_conv[l])
        x = x + (jax.nn.silu(gate) * (h @ w_up[l])) @ w_down[l]
    return rmsnorm(x, ln_f)
```

```python
from contextlib import ExitStack
import numpy as np
import ml_dtypes
import concourse.bass as bass
import concourse.mybir as mybir
from concourse.bass_utils import run_bass_kernel_spmd

F32 = mybir.dt.float32
BF16 = mybir.dt.bfloat16
AF = mybir.ActivationFunctionType
ALU = mybir.AluOpType

S_LEN = 4096
D = 1024
DEPTH = 2
NT = 32
IN_COLS = 3592
DFF = 2816
NFC = 22
RMS_EPS = 1e-6
L2_EPS = 1e-6
NEG = -30000.0

ENGS = ("pe", "act", "dve", "pool", "sp")


class Slot:
    __slots__ = ("name", "kind", "sem", "count")

    def __init__(self, name, kind):
        self.name = name
        self.kind = kind
        self.sem = None
        self.count = 0


class Obj:
    __slots__ = ("name", "w_ev", "r_ev", "dq")

    def __init__(self, name):
        self.name = name
        self.w_ev = {}
        self.r_ev = {}
        self.dq = {}


class Op:
    __slots__ = ("eng", "idx", "fn", "waits", "need_inc", "count", "slot", "tag")

    def __init__(self, eng, idx, fn):
        self.eng = eng
        self.idx = idx
        self.fn = fn
        self.waits = []
        self.need_inc = False
        self.count = None
        self.slot = None


def _merge(dst, src):
    for k, v in src.items():
        if dst.get(k, -1) < v:
            dst[k] = v


class Sched:
    def __init__(self, nc):
        self.nc = nc
        self.ops = {e: [] for e in ENGS}
        self.seen = {e: {} for e in ENGS}
        self.slots = []
        self.free = {"hw": [], "sw": []}
        self.phase_slots = []
        self.gen = 0
        self.bar = {}
        self.bar_gen = 0
        self.bar_applied = {e: 0 for e in ENGS}

    def keep(self):
        self.phase_slots = []

    def barrier(self):
        ev = {}
        for e in ENGS:
            n = len(self.ops[e])
            for i in range(n - 1, -1, -1):
                if self.ops[e][i].slot is None:
                    ev[("e", e)] = i
                    break
        for sl in self.slots:
            ev[("d", sl)] = sl.count
        self.bar = ev
        self.bar_gen += 1
        for sl in self.phase_slots:
            self.free[sl.kind].append(sl)
        self.phase_slots = []
        self.gen += 1

    def _slot_for(self, obj, qk):
        ent = obj.dq.get(qk)
        if ent is not None and ent[1] == self.gen:
            return ent[0]
        if self.free[qk]:
            sl = self.free[qk].pop()
        else:
            sl = Slot("%s_%d" % (qk, len(self.slots)), qk)
            self.slots.append(sl)
        self.phase_slots.append(sl)
        obj.dq[qk] = (sl, self.gen)
        return sl

    def _record(self, eng, fn, reads, writes, dma_obj=None):
        lst = self.ops[eng]
        op = Op(eng, len(lst), fn)
        op.tag = getattr(self, 'tag', '')
        need = {}
        mykey = ("e", eng)
        for o in reads:
            _merge(need, o.w_ev)
        for o in writes:
            _merge(need, o.w_ev)
            _merge(need, o.r_ev)
        if self.bar_applied[eng] != self.bar_gen:
            self.bar_applied[eng] = self.bar_gen
            for k, v in self.bar.items():
                if k == mykey:
                    continue
                if need.get(k, -1) < v:
                    need[k] = v
        seen = self.seen[eng]
        for k, v in need.items():
            if k == mykey and dma_obj is None and eng == "pe":
                continue
            if seen.get(k, -1) >= v:
                continue
            seen[k] = v
            if k[0] == "e":
                prod = self.ops[k[1]][v]
                prod.need_inc = True
                op.waits.append(("e", prod))
            else:
                op.waits.append(("d", k[1], v))
        lst.append(op)
        if dma_obj is not None:
            sl = self._slot_for(dma_obj, "sw" if eng == "pool" else "hw")
            sl.count += 1
            op.slot = sl
            ev = {("d", sl): sl.count}
        else:
            ev = {mykey: op.idx}
        for o in reads:
            _merge(o.r_ev, ev)
        for o in writes:
            if o.r_ev:
                o.w_ev = dict(ev)
                o.r_ev = {}
            else:
                _merge(o.w_ev, ev)
        return op

    def op(self, eng, fn, reads=(), writes=()):
        return self._record(eng, fn, list(reads), list(writes))

    def dma(self, eng, out, in_, reads, writes, sb_obj, **kw):
        def fn(e, out=out, in_=in_, kw=kw):
            return e.dma_start(out=out, in_=in_, **kw)
        return self._record(eng, fn, list(reads), list(writes), dma_obj=sb_obj)

    def emit(self, stack, final_wait_objs=()):
        nc = self.nc
        esem = {}
        for e in ENGS:
            if e != "sp":
                esem[e] = stack.enter_context(nc.semaphore("s_" + e))
        for sl in self.slots:
            sl.sem = stack.enter_context(nc.semaphore("d_" + sl.name))
        for e in ENGS:
            c = 0
            for op in self.ops[e]:
                if op.slot is None and op.need_inc:
                    c += 1
                    op.count = c
        block = stack.enter_context(nc.Block())

        def run(engname, e):
            for op in self.ops[engname]:
                for w in op.waits:
                    if w[0] == "e":
                        e.wait_ge(esem[w[1].eng], w[1].count)
                    else:
                        e.wait_ge(w[1].sem, 16 * w[2])
                ins = op.fn(e)
                if op.slot is not None:
                    ins.then_inc(op.slot.sem, 16)
                elif op.need_inc:
                    ins.then_inc(esem[engname], 1)
            if engname == "sp":
                for sl in self.slots:
                    e.wait_ge(sl.sem, 16 * sl.count)

        @block.tensor
        def _(e):
            run("pe", e)

        @block.scalar
        def _(e):
            run("act", e)

        @block.vector
        def _(e):
            run("dve", e)

        @block.gpsimd
        def _(e):
            run("pool", e)

        @block.sync
        def _(e):
            run("sp", e)


CST_NAMES = ["ident", "ones", "mbcT", "msT", "lmT", "bones", "sel0", "sel1"]


def _host_consts():
    i = np.arange(128)
    same = (i[:, None] // 64) == (i[None, :] // 64)
    c = {}
    c["ident"] = np.eye(128, dtype=np.float32)
    c["ones"] = np.ones((128, 128), np.float32)
    c["mbcT"] = np.where(same & (i[:, None] <= i[None, :]), 0.0, NEG).astype(np.float32)
    c["msT"] = (same & (i[:, None] < i[None, :])).astype(np.float32)
    c["lmT"] = (same & (i[:, None] <= i[None, :])).astype(np.float32)
    c["bones"] = same.astype(np.float32)
    s0 = np.zeros((128, 128), np.float32); s0[0:64, :] = 1.0
    s1 = np.zeros((128, 128), np.float32); s1[64:128, :] = 1.0
    c["sel0"] = s0
    c["sel1"] = s1
    cst = np.concatenate([c[n] for n in CST_NAMES], axis=1)
    ki = np.arange(128)[:, None, None]
    o = np.arange(17)[None, :, None]
    qi = np.arange(128)[None, None, :]
    dl = o * 128 + qi - ki
    cnt = ((dl >= 0) & (dl <= 128)).astype(np.int64) + ((dl >= 0) & (dl % 4 == 0) & (dl <= 512)) \
        + ((dl >= 0) & (dl % 16 == 0) & (dl <= 2048))
    bm = np.where(cnt > 0, np.log(np.maximum(cnt, 1).astype(np.float64)), NEG).astype(np.float32)
    bm = bm.reshape(128, 17 * 128)
    t = np.arange(S_LEN)
    slopes = 2.0 ** (-8.0 * np.arange(1, 9) / 8)
    kaug = np.zeros((8, 4, S_LEN), np.float32)
    qaug = np.zeros((8, 4, S_LEN), np.float32)
    for h in range(8):
        kaug[h, 0] = slopes[h] * (t % 128)
        kaug[h, 1] = slopes[h] * 128 * (t // 128)
        kaug[h, 2] = 1.0
        kaug[h, 3] = 1.0
        qaug[h, 0] = 1.0
        qaug[h, 1] = 1.0
        qaug[h, 2] = -slopes[h] * (t % 128)
        qaug[h, 3] = -slopes[h] * 128 * (t // 128)
    return cst, bm, kaug, qaug


class KB:
    ARENA_WORDS = 53200

    def __init__(self, nc, st, cfg):
        self.nc = nc
        self.st = st
        self.cfg = cfg
        self.S = Sched(nc)
        self.arena = st.enter_context(nc.sbuf_tensor("arena", [128, self.ARENA_WORDS], F32))
        self.top = 0
        self.pbank = [st.enter_context(nc.psum_tensor("pb%d" % i, [128, 512], F32)) for i in range(8)]
        self.pobj = [Obj("pb%d" % i) for i in range(8)]
        self.final_objs = []

    def f32(self, name, n):
        off = self.top
        self.top += n + (n & 1)
        assert self.top <= self.ARENA_WORDS, (name, self.top)
        return self.arena[:, off:off + n], Obj(name)

    def bf(self, name, n):
        assert n % 2 == 0
        off = self.top
        self.top += n // 2 + ((n // 2) & 1)
        assert self.top <= self.ARENA_WORDS, (name, self.top)
        return self.arena[:, off:off + n // 2].bitcast(BF16), Obj(name)

    def mark(self):
        return self.top

    def release(self, m):
        self.top = m
        self.S.barrier()

    def dram(self, name, shape, dt, out=False):
        out = out or (name in self.cfg.get("outs", ()))
        kind = "ExternalOutput" if out else "Internal"
        return self.nc.dram_tensor(name, list(shape), dt, kind=kind).ap()


def build(cfg):
    nc = bass.Bass("TRN2", target_bir_lowering=False)
    dbg = cfg.get("dbg", False)
    layers = cfg.get("layers", [0, 1])
    phases = cfg.get("phases", {"p1", "p2", "p3", "p4a", "p4b"})
    final_norm = cfg.get("final", True)

    def inp(name, shape, dt=F32):
        return nc.dram_tensor(name, list(shape), dt, kind="ExternalInput").ap()

    x_in = inp("x", [S_LEN, D])
    ln1 = inp("ln1", [DEPTH, D]); ln2 = inp("ln2", [DEPTH, D]); lnf = inp("ln_f", [1, D])
    w_in = inp("w_in", [DEPTH, D, IN_COLS])
    convq = inp("conv_qkv_r", [DEPTH, 128, 12 * 4])
    alog = inp("a_log_r", [DEPTH, 1, 128]); dtb = inp("dt_bias_r", [DEPTH, 1, 128])
    gdnn = inp("gdn_norm", [DEPTH, 128])
    w_out = inp("w_out", [DEPTH, D, D])
    w_gate = inp("w_gate", [DEPTH, D, DFF]); w_up = inp("w_up", [DEPTH, D, DFF])
    ffnc = inp("ffn_conv_r", [DEPTH, 128, NFC * 3])
    w_down = inp("w_down", [DEPTH, DFF, D])
    cst_d = inp("cst", [128, 8 * 128]); cstb_d = inp("cstb", [128, 2 * 128], BF16); bm_d = inp("bm", [128, 17 * 128], BF16)
    kaug_d = inp("kaug", [8, 4, S_LEN], BF16); qaug_d = inp("qaug", [8, 4, S_LEN], BF16)
    out_d = nc.dram_tensor("out", [S_LEN, D], F32, kind="ExternalOutput").ap()

    with ExitStack() as st:
        kb = KB(nc, st, cfg)
        S = kb.S
        xs = kb.dram("xs", [S_LEN, D], F32, out=(dbg or cfg.get("xs_out", False)))
        gqkv = kb.dram("gqkv", [12, 128, S_LEN], BF16, out=dbg)
        aqk = kb.dram("aqk", [8, 128, S_LEN], BF16, out=dbg)
        zs = kb.dram("zs", [S_LEN, 512], BF16, out=dbg)
        avs = kb.dram("avs", [S_LEN, 512], BF16, out=dbg)
        glog_d = kb.dram("glog", [S_LEN, 8], F32, out=dbg)
        ydbg = kb.dram("ydbg", [S_LEN, D], BF16, out=True) if (dbg or cfg.get("ydump")) else None
        o_xs = [Obj("xs%d" % i) for i in range(NT)]
        o_gqkv = Obj("gqkv"); o_aqk = Obj("aqk"); o_zs = Obj("zs"); o_avs = Obj("avs")

        cst32, o_cst32 = kb.f32("cst32", 8 * 128)
        cstbf, o_cstbf = kb.bf("cstbf", 2 * 128)
        S.dma("sp", cst32, cst_d, [], [o_cst32], o_cst32)
        S.dma("sp", cstbf, cstb_d, [], [o_cstbf], o_cstbf)
        C32 = {n: cst32[:, i * 128:(i + 1) * 128] for i, n in enumerate(CST_NAMES)}
        ident_bf = cstbf[:, 0:128]
        ones_bf = cstbf[:, 128:256]
        kb.C32 = C32; kb.o_cst32 = o_cst32; kb.ident_bf = ident_bf; kb.ones_bf = ones_bf; kb.o_cstbf = o_cstbf
        glog, o_glog = kb.f32("glog", NT * 8)
        kb.glog = glog; kb.o_glog = o_glog

        S.keep()
        base_mark = kb.mark()
        phases_all = phases
        for l in layers:
            phases = cfg.get('phases_by_layer', {}).get(l, phases_all)
            kb.cur_phases = phases
            x_src = x_in if l == layers[0] else xs
            if "p1" in phases:
                m = kb.mark()
                S.tag = 'phase1_' + str(l)
                phase1(kb, l, x_src, ln1, w_in, convq, gqkv, aqk, zs, avs, o_xs, o_gqkv, o_aqk, o_zs, o_avs, glog_d)
                kb.release(m)
            m_y = kb.mark()
            ybuf, _ = kb.bf("ybuf", NT * 1024)
            yv = ybuf.rearrange("p (t c) -> p t c", t=NT)
            o_y = [Obj("y%d" % i) for i in range(NT)]
            if "p2" in phases and "p3" in phases and cfg.get("interleave", False):
                m = kb.mark()
                g2 = phase2_gen(kb, l, gqkv, zs, o_gqkv, o_zs, alog, dtb, gdnn, yv, o_y, banks=(4, 5, 6, 7))
                g3 = phase3_gen(kb, l, aqk, avs, o_aqk, o_avs, bm_d, kaug_d, qaug_d, yv, o_y, ps_banks=(0, 1, 2), pacc_banks=(3,), look=2)
                d2 = d3 = False
                ratio = cfg.get("ratio", 4)
                while not (d2 and d3):
                    if not d2:
                        S.tag = 'phase2_' + str(l)
                        try:
                            next(g2)
                        except StopIteration:
                            d2 = True
                    for _ in range(ratio):
                        if d3:
                            break
                        S.tag = 'phase3_' + str(l)
                        try:
                            next(g3)
                        except StopIteration:
                            d3 = True
                kb.release(m)
            else:
                if "p2" in phases:
                    m = kb.mark()
                    S.tag = 'phase2_' + str(l)
                    phase2(kb, l, gqkv, zs, o_gqkv, o_zs, alog, dtb, gdnn, yv, o_y, banks=cfg.get('p2banks', range(8)))
                    kb.release(m)
                if "p3" in phases:
                    m = kb.mark()
                    S.tag = 'phase3_' + str(l)
                    phase3(kb, l, aqk, avs, o_aqk, o_avs, bm_d, kaug_d, qaug_d, yv, o_y)
                    kb.release(m)
            if (dbg and not cfg.get("lim") and ("p2" in phases or "p3" in phases) and "p4a" not in phases) or cfg.get("ydump"):
                for t in range(NT):
                    S.dma("sp", ydbg[t * 128:(t + 1) * 128, :], yv[:, t, :], [o_y[t]], [], o_y[t])
                    kb.final_objs.append(o_y[t])
            if "p4a" in phases:
                m = kb.mark()
                S.tag = 'phase4a_' + str(l)
                phase4a(kb, l, x_src, xs, o_xs, w_out, yv, o_y)
                kb.release(m)
            kb.release(m_y)
            if "p4b" in phases:
                m = kb.mark()
                last = (l == layers[-1]) and final_norm
                x4 = xs if ("p4a" in phases or l != layers[0]) else x_in
                S.tag = 'phase4b_' + str(l)
                phase4b(kb, l, x4, out_d if last else xs, o_xs, ln2, w_gate, w_up, ffnc, w_down, lnf if last else None)
                kb.release(m)
                if dbg and cfg.get("snap") and l == 0:
                    m = kb.mark()
                    xsnap = kb.dram("xsnap", [S_LEN, D], F32, out=True)
                    bufs = [kb.f32("snap%d" % i, D) for i in range(2)]
                    for t in range(NT):
                        bt, ob = bufs[t % 2]
                        S.dma("sp", bt, xs[t * 128:(t + 1) * 128, :], [o_xs[t]], [ob], ob)
                        S.dma("sp", xsnap[t * 128:(t + 1) * 128, :], bt, [ob], [], ob)
                    kb.release(m)
        npad = cfg.get("pad", 0)
        if npad:
            padt, o_pad = kb.f32("padt", 8)
            for i in range(npad):
                S.op("dve", lambda e: e.memset(padt, 0.0), [], [o_pad])
                S.op("act", lambda e: e.activation(out=padt, in_=padt, func=AF.Copy), [o_pad], [o_pad])
        S.emit(st, final_wait_objs=kb.final_objs)
    return nc


def make_stages(kb, n=3, cap=2048):
    return [kb.f32("wstage%d" % i, cap) for i in range(n)]


def load_weight(kb, name, src2d, kchunks, ncols, stages, eng="sp"):
    S = kb.S
    w, o = kb.bf(name, kchunks * ncols)
    wv = w.rearrange("p (k c) -> p k c", k=kchunks)
    srcv = src2d.rearrange("(k p) c -> p k c", p=128)
    n = getattr(kb, "wstage_n", 0)
    for k in range(kchunks):
        c0 = 0
        while c0 < ncols:
            stg, o_stg = stages[n % len(stages)]
            cap = stg.shape[-1]
            c1 = min(ncols, c0 + cap)
            S.dma(eng, stg[:, 0:c1 - c0], srcv[:, k, c0:c1], [], [o_stg], o_stg)
            dst = wv[:, k, c0:c1]
            ce = ("act", "dve", "pool")[n % 3]
            if ce == "act":
                S.op("act", lambda e, dst=dst, stg=stg, m=c1 - c0: e.activation(out=dst, in_=stg[:, 0:m], func=AF.Copy), [o_stg], [o])
            else:
                S.op(ce, lambda e, dst=dst, stg=stg, m=c1 - c0: e.tensor_copy(out=dst, in_=stg[:, 0:m]), [o_stg], [o])
            n += 1
            c0 = c1
    kb.wstage_n = n
    return wv, o


def load_weight_blocks(kb, wv, src2d, kchunks, blocks, stages, eng="sp"):
    S = kb.S
    srcv = src2d.rearrange("(k p) c -> p k c", p=128)
    n = getattr(kb, "wstage_n", 0)
    objs = []
    for (c0, c1) in blocks:
        o = Obj("wblk%d" % c0)
        objs.append(o)
        width = c1 - c0
        cap = stages[0][0].shape[-1]
        kstep = max(1, min(kchunks, cap // width))
        for k0 in range(0, kchunks, kstep):
            k1 = min(kchunks, k0 + kstep)
            stg, o_stg = stages[n % len(stages)]
            stv = stg[:, 0:(k1 - k0) * width].rearrange("p (k c) -> p k c", k=k1 - k0)
            S.dma(eng, stv, srcv[:, k0:k1, c0:c1], [], [o_stg], o_stg)
            dst = wv[:, k0:k1, c0:c1]
            ce = ("act", "dve", "pool")[n % 3]
            if ce == "act":
                S.op("act", lambda e, dst=dst, stv=stv: e.activation(out=dst, in_=stv, func=AF.Copy), [o_stg], [o])
            else:
                S.op(ce, lambda e, dst=dst, stv=stv: e.tensor_copy(out=dst, in_=stv), [o_stg], [o])
            n += 1
    kb.wstage_n = n
    return objs


def rms_stats(kb, ss, o_ss, n, inv_n, eps):
    S = kb.S
    S.op("dve", lambda e: e.tensor_scalar(out=ss, in0=ss, scalar1=inv_n, scalar2=eps, op0=ALU.mult, op1=ALU.add), [o_ss], [o_ss])
    S.op("act", lambda e: e.activation(out=ss, in_=ss, func=AF.Sqrt), [o_ss], [o_ss])
    S.op("dve", lambda e: e.reciprocal(out=ss, in_=ss), [o_ss], [o_ss])


def norm_transpose(kb, xt, o_xt, j, lnrep, o_ln, ss, o_ss, hb, o_hb, junk, o_junk, pT, o_pT, hT, o_hT, col0, evac_eng):
    S = kb.S
    xj = xt[:, j, :]
    S.op("dve", lambda e: e.scalar_tensor_tensor(out=hb, in0=xj, scalar=ss[:, j:j + 1], in1=lnrep, op0=ALU.mult, op1=ALU.mult),
         [o_xt, o_ss, o_ln], [o_hb])
    pTb = pT.bitcast(BF16)
    for k in range(8):
        S.op("pe", lambda e, k=k: e.transpose(pTb[:, k * 128:(k + 1) * 128], hb[:, k * 128:(k + 1) * 128], kb.ident_bf),
             [o_hb, kb.o_cstbf], [o_pT])
    src = pTb.rearrange("p (k t) -> p k t", k=8)
    dst = hT[:, :, col0:col0 + 128]
    if evac_eng == "act":
        S.op("act", lambda e: e.activation(out=dst, in_=src, func=AF.Copy), [o_pT], [o_hT])
    else:
        S.op("dve", lambda e: e.tensor_copy(out=dst, in_=src), [o_pT], [o_hT])


def phase4b(kb, l, x_src, x_dst, o_xs, ln2, w_gate, w_up, ffnc, w_down, lnf):
    S = kb.S
    nc = kb.nc
    actreg, _ = kb.f32("actreg", NFC * 256)
    stages = [(actreg[:, i * 2816:(i + 1) * 2816], Obj("wst%d" % i)) for i in range(2)]
    o_stages = [o for _, o in stages]
    wg_a, _ = kb.bf("wg", 8 * DFF)
    wu_a, _ = kb.bf("wu", 8 * DFF)
    wg = wg_a.rearrange("p (k c) -> p k c", k=8)
    wu = wu_a.rearrange("p (k c) -> p k c", k=8)
    fblocks = [(c0, min(DFF, c0 + 512)) for c0 in range(0, DFF, 512)]
    o_wgb, o_wub = [], []
    for blk in fblocks:
        o_wgb += load_weight_blocks(kb, wg, w_gate[l], 8, [blk], stages)
        o_wub += load_weight_blocks(kb, wu, w_up[l], 8, [blk], stages)
    wd, o_wd = load_weight(kb, "wd", w_down[l], NFC, D, stages)
    lnrep, o_ln = kb.f32("ln2rep", D)
    S.dma("sp", lnrep, ln2[l:l + 1, :].partition_broadcast(128), [], [o_ln], o_ln)
    if lnf is not None:
        lnfrep, o_lnf = kb.f32("lnfrep", D)
        S.dma("sp", lnfrep, lnf[0:1, :].partition_broadcast(128), [], [o_lnf], o_lnf)
    cw, o_cw = kb.f32("ffncw", NFC * 3)
    S.dma("sp", cw, ffnc[l], [], [o_cw], o_cw)
    cwv = cw.rearrange("p (c j) -> p c j", j=3)
    halo, o_halo = kb.f32("halo", NFC * 2)
    halov = halo.rearrange("p (c j) -> p c j", j=2)
    S.op("pool", lambda e: e.memset(halo, 0.0), [], [o_halo])
    xt_a, o_xt = kb.f32("xt", 4 * D)
    xt = xt_a.rearrange("p (j d) -> p j d", j=4)
    hb, o_hb = kb.bf("hb", D)
    junk, o_junk = kb.bf("junk", D)
    ss, o_ss = kb.f32("ss", 4)
    hT_a, o_hT = kb.bf("hT", 8 * 512)
    hT = hT_a.rearrange("p (k t) -> p k t", k=8)
    actT_a, o_actT = actreg.bitcast(BF16), Obj("actT")
    actT = actT_a.rearrange("p (c t) -> p c t", c=NFC)
    pres = [kb.f32("pre%d" % i, 514) for i in range(2)]
    cvs = [kb.f32("cv%d" % i, 512) for i in range(3)]
    xo_a, o_xo, xo = xt_a, o_xt, xt
    ss2, o_ss2 = kb.f32("ss2", 4)
    pb = kb.pbank; po = kb.pobj
    for T in range(8):
        tiles = o_xs[T * 4:(T + 1) * 4]
        src = x_src[T * 512:(T + 1) * 512, :].rearrange("(j p) d -> p j d", p=128)
        S.dma("sp", xt, src, tiles, [o_xt], o_xt)
        for j in range(4):
            S.op("act", lambda e, j=j: e.activation(out=junk, in_=xt[:, j, :], func=AF.Square, accum_out=ss[:, j:j + 1]),
                 [o_xt], [o_junk, o_ss])
        rms_stats(kb, ss, o_ss, 4, 1.0 / D, RMS_EPS)
        for j in range(4):
            norm_transpose(kb, xt, o_xt, j, lnrep, o_ln, ss, o_ss, hb, o_hb, junk, o_junk, pb[6 + (j % 2)], po[6 + (j % 2)],
                           hT, o_hT, j * 128, "act" if j % 2 == 0 else "dve")
        def ffn_a(fc):
            pg, opg = pb[(2 * fc) % 6], po[(2 * fc) % 6]
            pu, opu = pb[(2 * fc + 1) % 6], po[(2 * fc + 1) % 6]
            for k in range(8):
                S.op("pe", lambda e, k=k, fc=fc, pg=pg: e.matmul(pg[:, :], lhsT=wg[:, k, fc * 128:(fc + 1) * 128], rhs=hT[:, k, :],
                                                                 start=(k == 0), stop=(k == 7)), [o_wgb[fc // 4], o_hT], [opg])
            for k in range(8):
                S.op("pe", lambda e, k=k, fc=fc, pu=pu: e.matmul(pu[:, :], lhsT=wu[:, k, fc * 128:(fc + 1) * 128], rhs=hT[:, k, :],
                                                                 start=(k == 0), stop=(k == 7)), [o_wub[fc // 4], o_hT], [opu])
            pre, o_pre = pres[fc % 2]
            cv, o_cv = cvs[fc % 3]
            S.op("pool", lambda e, pre=pre, fc=fc: e.tensor_copy(out=pre[:, 0:2], in_=halov[:, fc, :]), [o_halo], [o_pre])
            S.op("act", lambda e, pre=pre, pg=pg: e.activation(out=pre[:, 2:514], in_=pg[:, :], func=AF.Copy), [opg], [o_pre])
            S.op("pool", lambda e, pre=pre, fc=fc: e.tensor_copy(out=halov[:, fc, :], in_=pre[:, 512:514]), [o_pre], [o_halo])
            S.op("dve", lambda e, pre=pre, cv=cv, fc=fc: e.tensor_scalar(out=cv, in0=pre[:, 2:514], scalar1=cwv[:, fc, 2:3], scalar2=None,
                                                                         op0=ALU.mult), [o_pre, o_cw], [o_cv])
            S.op("dve", lambda e, pre=pre, cv=cv, fc=fc: e.scalar_tensor_tensor(out=cv, in0=pre[:, 1:513], scalar=cwv[:, fc, 1:2], in1=cv,
                                                                                op0=ALU.mult, op1=ALU.add), [o_pre, o_cw, o_cv], [o_cv])
            S.op("dve", lambda e, pre=pre, cv=cv, fc=fc: e.scalar_tensor_tensor(out=cv, in0=pre[:, 0:512], scalar=cwv[:, fc, 0:1], in1=cv,
                                                                                op0=ALU.mult, op1=ALU.add), [o_pre, o_cw, o_cv], [o_cv])

        def ffn_b(fc):
            pu, opu = pb[(2 * fc + 1) % 6], po[(2 * fc + 1) % 6]
            cv, o_cv = cvs[fc % 3]
            S.op("act", lambda e, cv=cv: e.activation(out=cv, in_=cv, func=AF.Silu), [o_cv], [o_cv])
            S.op("dve", lambda e, cv=cv, pu=pu, fc=fc: e.tensor_tensor(out=actT[:, fc, :], in0=cv, in1=pu[:, :], op=ALU.mult),
                 [o_cv, opu], [o_actT] + (o_stages if T == 0 else []))

        for s_ in range(NFC + 1):
            if s_ < NFC:
                ffn_a(s_)
            if s_ >= 1:
                ffn_b(s_ - 1)
        for j in range(4):
            for nh in range(2):
                pi = (j * 2 + nh) % 6
                pd, opd = pb[pi], po[pi]
                for fc in range(NFC):
                    S.op("pe", lambda e, fc=fc, j=j, nh=nh, pd=pd: e.matmul(pd[:, :], lhsT=actT[:, fc, j * 128:(j + 1) * 128],
                                                                          rhs=wd[:, fc, nh * 512:(nh + 1) * 512],
                                                                          start=(fc == 0), stop=(fc == NFC - 1)), [o_actT, o_wd], [opd])
                S.op("dve", lambda e, j=j, nh=nh, pd=pd: e.tensor_tensor(out=xo[:, j, nh * 512:(nh + 1) * 512],
                                                                       in0=xt[:, j, nh * 512:(nh + 1) * 512], in1=pd[:, :], op=ALU.add),
                     [o_xt, opd], [o_xo])
        dst = x_dst[T * 512:(T + 1) * 512, :].rearrange("(j p) d -> p j d", p=128)
        if lnf is not None:
            for j in range(4):
                S.op("act", lambda e, j=j: e.activation(out=junk, in_=xo[:, j, :], func=AF.Square, accum_out=ss2[:, j:j + 1]),
                     [o_xo], [o_junk, o_ss2])
            rms_stats(kb, ss2, o_ss2, 4, 1.0 / D, RMS_EPS)
            for j in range(4):
                S.op("dve", lambda e, j=j: e.scalar_tensor_tensor(out=xo[:, j, :], in0=xo[:, j, :], scalar=ss2[:, j:j + 1], in1=lnfrep,
                                                                  op0=ALU.mult, op1=ALU.mult), [o_xo, o_ss2, o_lnf], [o_xo])
            S.dma("sp", dst, xo_a.rearrange("p (j d) -> p j d", j=4), [o_xo], [], o_xo)
        else:
            S.dma("sp", dst, xo_a.rearrange("p (j d) -> p j d", j=4), [o_xo], tiles, o_xo)
        if o_xo not in kb.final_objs:
            kb.final_objs.append(o_xo)


def phase1(kb, l, x_src, ln1, w_in, convq, gqkv, aqk, zs, avs, o_xs, o_gqkv, o_aqk, o_zs, o_avs, glog_d):
    S = kb.S
    w_a, _ = kb.bf("win", 8 * IN_COLS)
    w = w_a.rearrange("p (k c) -> p k c", k=8)
    wblocks = [(0, 512), (512, 1024), (1024, 1536), (2056, 2568), (2568, 3080), (1536, 2048), (3080, 3592), (2048, 2056)]
    wobjs = load_weight_blocks(kb, w, w_in[l], 8, wblocks, make_stages(kb))
    o_wcol = {}
    for (c0, c1), o in zip(wblocks, wobjs):
        for c in range(c0, c1, 8):
            o_wcol[c] = o

    def o_wc(c0):
        return o_wcol[c0 - (c0 % 8)]
    lnrep, o_ln = kb.f32("ln1rep", D)
    S.dma("sp", lnrep, ln1[l:l + 1, :].partition_broadcast(128), [], [o_ln], o_ln)
    cw, o_cw = kb.f32("qkvcw", 48)
    S.dma("sp", cw, convq[l], [], [o_cw], o_cw)
    cwv = cw.rearrange("p (c j) -> p c j", j=4)
    pre_a, _ = kb.f32("qkvpre", 12 * 516)
    prev = pre_a.rearrange("p (c t) -> p c t", c=12)
    o_pre = [Obj("pre%d" % c) for c in range(12)]
    S.op("pool", lambda e: e.memset(pre_a, 0.0), [], o_pre)
    xts = [kb.f32("xt%d" % i, 4 * D) for i in range(2)]
    hb, o_hb = kb.bf("hb", D)
    junk, o_junk = kb.bf("junk", D)
    ss, o_ss = kb.f32("ss", 4)
    hTs = [kb.bf("hT%d" % i, 8 * 512) for i in range(2)]
    cvs = [kb.f32("cv%d" % i, 512) for i in range(3)]
    sqs = [kb.bf("sq%d" % i, 512) for i in range(2)]
    rns = [kb.f32("rn%d" % i, 512) for i in range(2)]
    stg = [kb.bf("stg%d" % i, 512) for i in range(4)]
    zst_a, o_zst = kb.bf("zst", 4 * 512)
    zst = zst_a.rearrange("p (j c) -> p j c", j=4)
    avst_a, o_avst = kb.bf("avst", 4 * 512)
    avst = avst_a.rearrange("p (j c) -> p j c", j=4)
    glv = kb.glog.rearrange("p (t c) -> p t c", c=8)
    pb = kb.pbank; po = kb.pobj
    nstg = 0
    npb = 0
    def load_x(T):
        xt_a, o_xt = xts[T % 2]
        src = x_src[T * 512:(T + 1) * 512, :].rearrange("(j p) d -> p j d", p=128)
        S.dma("sp", xt_a.rearrange("p (j d) -> p j d", j=4), src, o_xs[T * 4:(T + 1) * 4], [o_xt], o_xt)

    load_x(0)
    for T in range(8):
        tiles = o_xs[T * 4:(T + 1) * 4]
        xt_a, o_xt = xts[T % 2]
        xt = xt_a.rearrange("p (j d) -> p j d", j=4)
        hT_a, o_hT = hTs[T % 2]
        hT = hT_a.rearrange("p (k t) -> p k t", k=8)
        if T + 1 < 8:
            load_x(T + 1)
        for j in range(4):
            S.op("act", lambda e, j=j, xt=xt: e.activation(out=junk, in_=xt[:, j, :], func=AF.Square, accum_out=ss[:, j:j + 1]),
                 [o_xt], [o_junk, o_ss])
        rms_stats(kb, ss, o_ss, 4, 1.0 / D, RMS_EPS)
        for j in range(4):
            norm_transpose(kb, xt, o_xt, j, lnrep, o_ln, ss, o_ss, hb, o_hb, junk, o_junk, pb[6 + (j % 2)], po[6 + (j % 2)],
                           hT, o_hT, j * 128, "act" if j % 2 == 0 else "dve")
        tsl = slice(T * 512, (T + 1) * 512)
        def stage_a(c):
            pg, opg = pb[c % 4], po[c % 4]
            for k in range(8):
                S.op("pe", lambda e, k=k, c=c, pg=pg, hT=hT: e.matmul(pg[:, :], lhsT=w[:, k, c * 128:(c + 1) * 128], rhs=hT[:, k, :],
                                                                      start=(k == 0), stop=(k == 7)), [o_wc(c * 128), o_hT], [opg])
            opre = o_pre[c]
            S.op("act", lambda e, c=c, pg=pg: e.activation(out=prev[:, c, 3:515], in_=pg[:, :], func=AF.Copy), [opg], [opre])
            cv, o_cv = cvs[c % 3]
            S.op("act", lambda e, c=c, cv=cv, pg=pg: e.activation(out=cv, in_=pg[:, :], func=AF.Copy, scale=cwv[:, c, 3:4]), [opg, o_cw], [o_cv])
            for jt in (2, 1, 0):
                S.op("dve", lambda e, c=c, cv=cv, jt=jt: e.scalar_tensor_tensor(out=cv, in0=prev[:, c, jt:jt + 512], scalar=cwv[:, c, jt:jt + 1],
                                                                                in1=cv, op0=ALU.mult, op1=ALU.add), [opre, o_cw, o_cv], [o_cv])
            S.op("pool", lambda e, c=c: e.tensor_copy(out=prev[:, c, 0:3], in_=prev[:, c, 512:515]), [opre], [opre])

        def stage_b(c):
            cv, o_cv = cvs[c % 3]
            if c >= 8:
                st_t, o_st = stg[c % 4]
                S.op("act", lambda e, cv=cv, st_t=st_t: e.activation(out=st_t, in_=cv, func=AF.Silu), [o_cv], [o_st])
                S.dma("sp", gqkv[c][:, tsl], st_t, [o_st], [o_gqkv], o_st)
            else:
                sq, o_sq = sqs[c % 2]
                rn, o_rn = rns[c % 2]
                pn, opn = pb[4 + (c % 2)], po[4 + (c % 2)]
                S.op("act", lambda e, cv=cv: e.activation(out=cv, in_=cv, func=AF.Silu), [o_cv], [o_cv])
                S.op("pool", lambda e, cv=cv, sq=sq: e.tensor_tensor(out=sq, in0=cv, in1=cv, op=ALU.mult), [o_cv], [o_sq])
                S.op("pe", lambda e, sq=sq, pn=pn: e.matmul(pn[:, :], lhsT=kb.ones_bf, rhs=sq, start=True, stop=True), [kb.o_cstbf, o_sq], [opn])
                S.op("dve", lambda e, rn=rn, pn=pn: e.tensor_scalar(out=rn, in0=pn[:, :], scalar1=L2_EPS, scalar2=None, op0=ALU.add), [opn], [o_rn])

        def stage_c(c):
            if c >= 8:
                return
            cv, o_cv = cvs[c % 3]
            rn, o_rn = rns[c % 2]
            st_t, o_st = stg[c % 4]
            S.op("act", lambda e, rn=rn: e.activation(out=rn, in_=rn, func=AF.Sqrt), [o_rn], [o_rn])
            S.op("dve", lambda e, rn=rn: e.reciprocal(out=rn, in_=rn), [o_rn], [o_rn])
            sc = (128 ** -0.5) if c < 4 else 1.0
            S.op("dve", lambda e, cv=cv, rn=rn, st_t=st_t, sc=sc: e.scalar_tensor_tensor(out=st_t, in0=cv, scalar=sc, in1=rn,
                                                                                         op0=ALU.mult, op1=ALU.mult), [o_cv, o_rn], [o_st])
            S.dma("sp", gqkv[c][:, tsl], st_t, [o_st], [o_gqkv], o_st)

        for s_ in range(12 + 2):
            if s_ < 12:
                stage_a(s_)
            if 0 <= s_ - 1 < 12:
                stage_b(s_ - 1)
            if 0 <= s_ - 2 < 12:
                stage_c(s_ - 2)
        npb = 0
        nstg = 0
        for c in range(8):
            pg, opg = pb[npb % 4], po[npb % 4]; npb += 1
            col = 2056 + c * 128
            for k in range(8):
                S.op("pe", lambda e, k=k, col=col, pg=pg, hT=hT: e.matmul(pg[:, :], lhsT=w[:, k, col:col + 128], rhs=hT[:, k, :],
                                                                   start=(k == 0), stop=(k == 7)), [o_wc(col), o_hT], [opg])
            st_t, o_st = stg[nstg % 4]; nstg += 1
            sc = 0.125 if c < 4 else 1.0
            if c % 2 == 0:
                S.op("act", lambda e, pg=pg, st_t=st_t, sc=sc: e.activation(out=st_t, in_=pg[:, :], func=AF.Copy, scale=sc), [opg], [o_st])
            else:
                S.op("dve", lambda e, pg=pg, st_t=st_t, sc=sc: e.tensor_scalar(out=st_t, in0=pg[:, :], scalar1=sc, scalar2=None, op0=ALU.mult),
                     [opg], [o_st])
            S.dma("sp", aqk[c][:, tsl], st_t, [o_st], [o_aqk], o_st)
        for j in range(4):
            for which, col, dst_t, o_dst in ((0, 1536, zst, o_zst), (1, 3080, avst, o_avst)):
                pg, opg = pb[npb % 4], po[npb % 4]; npb += 1
                for k in range(8):
                    S.op("pe", lambda e, k=k, j=j, col=col, pg=pg, hT=hT: e.matmul(pg[:, :], lhsT=hT[:, k, j * 128:(j + 1) * 128], rhs=w[:, k, col:col + 512],
                                                                          start=(k == 0), stop=(k == 7)), [o_wc(col), o_hT], [opg])
                if which == 0:
                    S.op("act", lambda e, pg=pg, dst_t=dst_t, j=j: e.activation(out=dst_t[:, j, :], in_=pg[:, :], func=AF.Copy), [opg], [o_dst])
                else:
                    S.op("dve", lambda e, pg=pg, dst_t=dst_t, j=j: e.tensor_copy(out=dst_t[:, j, :], in_=pg[:, :]), [opg], [o_dst])
            pl, opl = pb[4 + (j % 2)], po[4 + (j % 2)]
            for k in range(8):
                S.op("pe", lambda e, k=k, j=j, pl=pl, hT=hT: e.matmul(pl[:, 0:8], lhsT=hT[:, k, j * 128:(j + 1) * 128], rhs=w[:, k, 2048:2056],
                                                               start=(k == 0), stop=(k == 7)), [o_wc(2048), o_hT], [opl])
            S.op("dve", lambda e, pl=pl, j=j, T=T: e.tensor_copy(out=glv[:, T * 4 + j, :], in_=pl[:, 0:8]), [opl], [kb.o_glog])
        S.dma("sp", zs[tsl, :].rearrange("(j p) c -> p j c", p=128), zst, [o_zst], [o_zs], o_zst)
        S.dma("sp", avs[tsl, :].rearrange("(j p) c -> p j c", p=128), avst, [o_avst], [o_avs], o_avst)
    if kb.cfg.get("dbg"):
        S.dma("sp", glog_d.rearrange("(t p) c -> p t c", p=128), glv, [kb.o_glog], [], kb.o_glog)
        kb.final_objs.append(kb.o_glog)


def phase4a(kb, l, x_src, xs, o_xs, w_out, yv, o_y):
    S = kb.S
    wo, o_wo = load_weight(kb, "wo", w_out[l], 8, D, make_stages(kb))
    xts = [kb.f32("xa%d" % i, D) for i in range(2)]
    yTs = [kb.bf("yT%d" % i, 8 * 128) for i in range(2)]
    pb = kb.pbank; po = kb.pobj
    S.dma("sp", xts[0][0], x_src[0:128, :], [o_xs[0]], [xts[0][1]], xts[0][1])
    for t in range(NT):
        xt, o_xt = xts[t % 2]
        yT_a, o_yT = yTs[t % 2]
        yT = yT_a.rearrange("p (k t) -> p k t", k=8)
        if t + 1 < NT:
            xn, o_xn = xts[(t + 1) % 2]
            S.dma("sp", xn, x_src[(t + 1) * 128:(t + 2) * 128, :], [o_xs[t + 1]], [o_xn], o_xn)
        pT, o_pT = pb[6 + (t % 2)], po[6 + (t % 2)]
        pTb = pT.bitcast(BF16)
        for k in range(8):
            S.op("pe", lambda e, k=k, t=t, pTb=pTb: e.transpose(pTb[:, k * 128:(k + 1) * 128], yv[:, t, k * 128:(k + 1) * 128], kb.ident_bf),
                 [o_y[t], kb.o_cstbf], [o_pT])
        if t % 2 == 0:
            S.op("act", lambda e, pTb=pTb, yT_a=yT_a: e.activation(out=yT_a, in_=pTb[:, :], func=AF.Copy), [o_pT], [o_yT])
        else:
            S.op("dve", lambda e, pTb=pTb, yT_a=yT_a: e.tensor_copy(out=yT_a, in_=pTb[:, :]), [o_pT], [o_yT])
        for nh in range(2):
            pi = (t * 2 + nh) % 6
            pd, opd = pb[pi], po[pi]
            for k in range(8):
                S.op("pe", lambda e, k=k, nh=nh, pd=pd, yT=yT: e.matmul(pd[:, :], lhsT=yT[:, k, :], rhs=wo[:, k, nh * 512:(nh + 1) * 512],
                                                                      start=(k == 0), stop=(k == 7)), [o_yT, o_wo], [opd])
            S.op("dve", lambda e, nh=nh, pd=pd, xt=xt: e.tensor_tensor(out=xt[:, nh * 512:(nh + 1) * 512], in0=xt[:, nh * 512:(nh + 1) * 512],
                                                                     in1=pd[:, :], op=ALU.add), [o_xt, opd], [o_xt])
        S.dma("sp", xs[t * 128:(t + 1) * 128, :], xt, [o_xt], [o_xs[t]], o_xt)
        if o_xt not in kb.final_objs:
            kb.final_objs.append(o_xt)


def phase3(*a, **k):
    for _ in phase3_gen(*a, **k):
        pass


def phase3_gen(kb, l, aqk, avs, o_aqk, o_avs, bm_d, kaug_d, qaug_d, yv, o_y, ps_banks=(0, 1, 2, 3, 4), pacc_banks=(5, 6, 7), look=2):
    S = kb.S
    bm, o_bm = kb.bf("bm", 17 * 128)
    S.dma("sp", bm, bm_d, [], [o_bm], o_bm)
    KTs = [kb.bf("KT%d" % i, S_LEN) for i in range(2)]
    QTs = [kb.bf("QT%d" % i, S_LEN) for i in range(2)]
    VAs = [kb.bf("VA%d" % i, NT * 66) for i in range(2)]
    PTs = [kb.bf("PT%d" % i, 512) for i in range(3)]
    rd, o_rd = kb.f32("rden", 2)
    for i in range(2):
        S.op("pool", lambda e, i=i: e.memset(KTs[i][0][64:128, :], 0.0), [], [KTs[i][1]])
        S.op("pool", lambda e, i=i: e.memset(QTs[i][0][64:128, :], 0.0), [], [QTs[i][1]])
        S.op("pool", lambda e, i=i: e.memset(VAs[i][0], 1.0), [], [VAs[i][1]])
    pb = kb.pbank; po = kb.pobj
    lim = kb.cfg.get('lim', False)
    PTs = PTs + [kb.bf("PT%d" % i, 512) for i in range(3, 5)]
    groups = []
    for h in range(2 if lim else 8):
        for qb in (list(range(3)) + [20] if lim else range(NT)):
            nkb = min(qb, 16) + 1
            for o0 in range(0, nkb, 4):
                groups.append((h, qb, o0, min(4, nkb - o0), nkb))
    loaded = set()
    state = {}

    def load_head(h):
        KT, o_KT = KTs[h % 2]
        QT, o_QT = QTs[h % 2]
        VA_a, o_VA = VAs[h % 2]
        VA = VA_a.rearrange("p (t c) -> p t c", c=66)
        r0 = (h % 2) * 64
        S.dma("sp", KT[0:64, :], aqk[4 + h // 2][r0:r0 + 64, :], [o_aqk], [o_KT], o_KT)
        S.dma("sp", QT[0:64, :], aqk[h // 2][r0:r0 + 64, :], [o_aqk], [o_QT], o_QT)
        S.dma("sp", KT[64:68, :], kaug_d[h], [], [o_KT], o_KT)
        S.dma("sp", QT[64:68, :], qaug_d[h], [], [o_QT], o_QT)
        S.dma("sp", VA[:, :, 0:64], avs[:, h * 64:(h + 1) * 64].rearrange("(t p) c -> p t c", p=128), [o_avs], [o_VA], o_VA)

    def emit_qk(gi):
        h, qb, o0, n, nkb = groups[gi]
        if h not in loaded:
            loaded.add(h)
            load_head(h)
        KT, o_KT = KTs[h % 2]
        QT, o_QT = QTs[h % 2]
        ps, ops_ = pb[ps_banks[gi % len(ps_banks)]], po[ps_banks[gi % len(ps_banks)]]
        PT, o_PT = PTs[gi % 5]
        S.op("pe", lambda e, ps=ps, o0=o0, n=n: e.matmul(ps[:, 0:n * 128], lhsT=kb.ident_bf, rhs=bm[:, o0 * 128:(o0 + n) * 128],
                                                         start=True, stop=False), [kb.o_cstbf, o_bm], [ops_])
        for o in range(o0, o0 + n):
            kbk = qb - o
            S.op("pe", lambda e, ps=ps, o=o, o0=o0, n=n, kbk=kbk, qb=qb, KT=KT, QT=QT: e.matmul(
                ps[:, (o - o0) * 128:(o - o0 + 1) * 128], lhsT=KT[:, kbk * 128:(kbk + 1) * 128], rhs=QT[:, qb * 128:(qb + 1) * 128],
                start=False, stop=(o == o0 + n - 1)), [o_KT, o_QT], [ops_])
        S.op("act", lambda e, ps=ps, PT=PT, n=n: e.activation(out=PT[:, 0:n * 128], in_=ps[:, 0:n * 128], func=AF.Exp), [ops_], [o_PT])

    def emit_pv(gi):
        h, qb, o0, n, nkb = groups[gi]
        VA_a, o_VA = VAs[h % 2]
        VA = VA_a.rearrange("p (t c) -> p t c", c=66)
        PT, o_PT = PTs[gi % 5]
        if o0 == 0:
            state["npo"] = state.get("npo", 0) + 1
        pi = pacc_banks[state["npo"] % len(pacc_banks)]
        pacc, opacc = pb[pi], po[pi]
        for o in range(o0, o0 + n):
            kbk = qb - o
            S.op("pe", lambda e, pacc=pacc, PT=PT, o=o, o0=o0, kbk=kbk, VA=VA, nkb=nkb: e.matmul(
                pacc[:, 0:65], lhsT=PT[:, (o - o0) * 128:(o - o0 + 1) * 128], rhs=VA[:, kbk, 0:65],
                start=(o == 0), stop=(o == nkb - 1)), [o_PT, o_VA], [opacc])
        if o0 + n == nkb:
            rdc = rd[:, (qb % 2):(qb % 2) + 1]
            S.op("dve", lambda e, pacc=pacc, rdc=rdc: e.reciprocal(out=rdc, in_=pacc[:, 64:65]), [opacc], [o_rd])
            S.op("dve", lambda e, pacc=pacc, rdc=rdc, qb=qb, h=h: e.tensor_scalar(out=yv[:, qb, 512 + h * 64:512 + (h + 1) * 64], in0=pacc[:, 0:64],
                                                                                  scalar1=rdc, scalar2=None, op0=ALU.mult), [opacc, o_rd], [o_y[qb]])

    LOOK = look
    G = len(groups)
    for gi in range(G + LOOK):
        if gi < G:
            emit_qk(gi)
        if gi - LOOK >= 0:
            emit_pv(gi - LOOK)
        yield


class SlotPool:
    def __init__(self, kb, banks=range(8)):
        self.banks = [(kb.pbank[b], kb.pobj[b]) for b in banks]
        self.n = 0

    def bank(self):
        b = self.banks[self.n % len(self.banks)]
        self.n += 1
        return b


def _slot(bank, h):
    return bank[0][:, h * 128:(h + 1) * 128], bank[1]


def phase2(*a, **k):
    for _ in phase2_gen(*a, **k):
        pass


def phase2_gen(kb, l, gqkv, zs, o_gqkv, o_zs, alog, dtb, gdnn, yv, o_y, banks=range(8)):
    from itertools import zip_longest
    S = kb.S
    C32 = kb.C32
    o_c32 = kb.o_cst32
    ident32 = C32["ident"]; ones32 = C32["ones"]; mbcT = C32["mbcT"]; msT = C32["msT"]; lmT = C32["lmT"]
    sel = [C32["sel0"], C32["sel1"]]
    ident_bf = kb.ident_bf; o_cbf = kb.o_cstbf
    sp = SlotPool(kb, banks)
    lim = kb.cfg.get("lim", False)
    import os
    ntile = int(os.environ.get('P2NT', '3')) if lim else NT

    def t128(name):
        return kb.f32(name, 128)

    dtbr, o_dtbr = t128("dtbr"); algr, o_algr = t128("algr"); gnrep, o_gn = t128("gnrep")
    S.dma("sp", dtbr, dtb[l].partition_broadcast(128), [], [o_dtbr], o_dtbr)
    S.dma("sp", algr, alog[l].partition_broadcast(128), [], [o_algr], o_algr)
    S.dma("sp", gnrep, gdnn[l:l + 1, :].partition_broadcast(128), [], [o_gn], o_gn)
    glv = kb.glog.rearrange("p (t c) -> p t c", c=8)
    o_gl = kb.o_glog
    g, o_g = t128("g"); bet, o_bet = t128("bet"); nbet, o_nbet = t128("nbet")
    if 'p1' not in kb.cur_phases:
        S.op("pool", lambda e: e.memset(kb.glog, 0.1), [], [o_gl])
    gc, o_gc = t128("gc"); ngc, o_ngc = t128("ngc"); egc, o_egc = t128("egc"); negc, o_negc = t128("negc")
    edl, o_edl = t128("edl"); tmp, o_tmp = t128("tmp")
    dlr = [t128("dlr0"), t128("dlr1")]
    v3 = lambda ap: ap.rearrange("p (t h) -> p t h", h=4)
    S.op("dve", lambda e: e.tensor_tensor(out=v3(tmp), in0=glv[:, :, 4:8], in1=v3(dtbr), op=ALU.add), [o_gl, o_dtbr], [o_tmp])
    S.op("act", lambda e: e.activation(out=tmp, in_=tmp, func=AF.Exp), [o_tmp], [o_tmp])
    S.op("dve", lambda e: e.tensor_scalar(out=tmp, in0=tmp, scalar1=1.0, scalar2=None, op0=ALU.add), [o_tmp], [o_tmp])
    S.op("act", lambda e: e.activation(out=tmp, in_=tmp, func=AF.Ln), [o_tmp], [o_tmp])
    S.op("act", lambda e: e.activation(out=algr, in_=algr, func=AF.Exp), [o_algr], [o_algr])
    S.op("dve", lambda e: e.scalar_tensor_tensor(out=g, in0=tmp, scalar=-1.0, in1=algr, op0=ALU.mult, op1=ALU.mult), [o_tmp, o_algr], [o_g])
    S.op("act", lambda e: e.activation(out=v3(bet), in_=glv[:, :, 0:4], func=AF.Sigmoid), [o_gl], [o_bet])
    S.op("dve", lambda e: e.tensor_scalar(out=nbet, in0=bet, scalar1=-1.0, scalar2=None, op0=ALU.mult), [o_bet], [o_nbet])
    pgc, opgc = _slot(sp.bank(), 0)
    S.op("pe", lambda e: e.matmul(pgc, lhsT=lmT, rhs=g, start=True, stop=True), [o_c32, o_g], [opgc])
    S.op("dve", lambda e: e.tensor_copy(out=gc, in_=pgc), [opgc], [o_gc])
    S.op("dve", lambda e: e.tensor_scalar(out=ngc, in0=gc, scalar1=-1.0, scalar2=None, op0=ALU.mult), [o_gc], [o_ngc])
    S.op("act", lambda e: e.activation(out=egc, in_=gc, func=AF.Exp), [o_gc], [o_egc])
    S.op("dve", lambda e: e.tensor_scalar(out=negc, in0=egc, scalar1=-1.0, scalar2=None, op0=ALU.mult), [o_egc], [o_negc])
    for c in range(2):
        pd, opd = _slot(sp.bank(), 0)
        dl_t, o_dl = dlr[c]
        rc = slice(c * 64, c * 64 + 64)
        S.op("pe", lambda e, pd=pd, c=c: e.matmul(pd, lhsT=sel[c], rhs=g, start=True, stop=True), [o_c32, o_g], [opd])
        S.op("dve", lambda e, pd=pd, rc=rc: e.tensor_tensor(out=edl[rc, :], in0=pd[rc, :], in1=gc[rc, :], op=ALU.subtract), [opd, o_gc], [o_edl])
        S.op("act", lambda e, pd=pd, dl_t=dl_t: e.activation(out=dl_t, in_=pd, func=AF.Exp), [opd], [o_dl])
    S.op("act", lambda e: e.activation(out=edl, in_=edl, func=AF.Exp), [o_edl], [o_edl])

    def bf128(name):
        return kb.bf(name, 128)
    ld = [[kb.bf("ld%d_%d" % (par, c), 512) for c in range(12)] for par in range(2)]
    zt = [kb.bf("zt%d" % i, 512) for i in range(3)]
    PB = [[{nm: [bf128("%s%d%d_%d" % (nm, par, h, i)) for i in range(2)] for nm in ("B", "Bt", "Q")} for h in range(4)] for par in range(3)]
    PX = [[{nm: bf128("%s%d%d" % (nm, par, h)) for nm in ("aqkT", "kdec", "vtok")} for h in range(4)] for par in range(3)]
    W32x = [[{nm: t128("%s%d_%d" % (nm, h, i)) for nm in ("dg", "dgn", "E3", "E3s")} for h in range(4)] for i in range(2)]
    Sst = [t128("S%d" % h) for h in range(4)]
    Sbf = [bf128("Sbf%d" % h) for h in range(4)]
    rp = [bf128("rp%d" % h) for h in range(4)]
    vnew = [bf128("vnew%d" % h) for h in range(4)]
    qs = [t128("qs%d" % h) for h in range(4)]
    osb = [t128("osb%d" % h) for h in range(4)]
    szb = [t128("sz%d" % h) for h in range(4)]
    t1b = [t128("t1%d" % h) for h in range(4)]
    junk, o_junk = t128("junk2")
    ssn = [kb.f32("ssn%d" % h, 2) for h in range(4)]
    for h in range(4):
        S.op("pool", lambda e, h=h: e.memset(Sst[h][0], 0.0), [], [Sst[h][1]])
        S.op("pool", lambda e, h=h: e.memset(Sbf[h][0], 0.0), [], [Sbf[h][1]])

    def loads(t):
        if t % 4 == 0:
            par = (t // 4) % 2
            for c in range(12):
                buf, o_b = ld[par][c]
                S.dma("sp", buf, gqkv[c][:, t * 128:t * 128 + 512], [o_gqkv], [o_b], o_b)
        zb, o_zb = zt[t % 3]
        S.dma("sp", zb, zs[t * 128:(t + 1) * 128, :], [o_zs], [o_zb], o_zb)

    def opnd(t, h):
        par = (t // 4) % 2
        off = (t % 4) * 128
        q = (ld[par][h][0][:, off:off + 128], ld[par][h][1])
        k = (ld[par][4 + h][0][:, off:off + 128], ld[par][4 + h][1])
        v = (ld[par][8 + h][0][:, off:off + 128], ld[par][8 + h][1])
        return q, k, v

    def pre_gen(t):
        par = t % 3
        W32 = W32x[t % 2]
        loads(t)
        hs = range(4)
        pkk = {}; pkq = {}
        for h in hs:
            (qT, o_q), (kT, o_k), (vT, o_v) = opnd(t, h)
            n = t * 4 + h
            dg, o_dg = W32[h]["dg"]; dgn, o_dgn = W32[h]["dgn"]
            S.op("dve", lambda e, dg=dg, n=n: e.tensor_scalar(out=dg, in0=ident32, scalar1=gc[:, n:n + 1], scalar2=None, op0=ALU.mult),
                 [o_c32, o_gc], [o_dg])
            S.op("act", lambda e, dgn=dgn, n=n: e.activation(out=dgn, in_=ident32, func=AF.Copy, scale=ngc[:, n:n + 1]),
                 [o_c32, o_ngc], [o_dgn])
        bt1 = sp.bank(); bt2 = sp.bank()
        for h in hs:
            n = t * 4 + h
            (qT, o_q), (kT, o_k), (vT, o_v) = opnd(t, h)
            kdec, o_kdec = PX[par][h]["kdec"]; vtok, o_vtok = PX[par][h]["vtok"]
            p, op_ = _slot(bt1, h); pb16 = p.bitcast(BF16)[:, 0:128]
            S.op("pe", lambda e, pb16=pb16, kT=kT: e.transpose(pb16, kT, ident_bf), [o_k, o_cbf], [op_])
            S.op("act", lambda e, pb16=pb16, kdec=kdec, n=n: e.activation(out=kdec, in_=pb16, func=AF.Copy, scale=edl[:, n:n + 1]), [op_, o_edl], [o_kdec])
            p2, op2 = _slot(bt2, h); p2b = p2.bitcast(BF16)[:, 0:128]
            S.op("pe", lambda e, p2b=p2b, vT=vT: e.transpose(p2b, vT, ident_bf), [o_v, o_cbf], [op2])
            S.op("dve", lambda e, p2b=p2b, vtok=vtok: e.tensor_copy(out=vtok, in_=p2b), [op2], [o_vtok])
        yield
        pdl = {}
        bdl = sp.bank()
        for h in hs:
            dg, o_dg = W32[h]["dg"]; dgn, o_dgn = W32[h]["dgn"]
            pdl[h] = _slot(bdl, h)
            p, op_ = pdl[h]
            S.op("pe", lambda e, p=p, dg=dg: e.matmul(p, lhsT=ones32, rhs=dg, start=True, stop=False), [o_c32, o_dg], [op_])
            S.op("pe", lambda e, p=p, dgn=dgn: e.matmul(p, lhsT=dgn, rhs=ones32, start=False, stop=False), [o_c32, o_dgn], [op_])
            S.op("pe", lambda e, p=p: e.matmul(p, lhsT=ident32, rhs=mbcT, start=False, stop=True), [o_c32], [op_])
            E3, o_E3 = W32[h]["E3"]
            S.op("act", lambda e, p=p, E3=E3: e.activation(out=E3, in_=p, func=AF.Exp), [op_], [o_E3])
        yield
        bkk = sp.bank(); bkq = sp.bank()
        for h in hs:
            (qT, o_q), (kT, o_k), (vT, o_v) = opnd(t, h)
            pkk[h] = _slot(bkk, h); pkq[h] = _slot(bkq, h)
            S.op("pe", lambda e, kT=kT, p=pkk[h][0]: e.matmul(p, lhsT=kT, rhs=kT, start=True, stop=True), [o_k], [pkk[h][1]])
            S.op("pe", lambda e, kT=kT, qT=qT, p=pkq[h][0]: e.matmul(p, lhsT=kT, rhs=qT, start=True, stop=True), [o_k, o_q], [pkq[h][1]])
        for h in hs:
            n = t * 4 + h
            E3, o_E3 = W32[h]["E3"]; E3s, o_E3s = W32[h]["E3s"]
            aqkT, o_aqkT = PX[par][h]["aqkT"]
            S.op("dve", lambda e, p=pkq[h][0], E3=E3, aqkT=aqkT: e.tensor_tensor(out=aqkT, in0=p, in1=E3, op=ALU.mult), [pkq[h][1], o_E3], [o_aqkT])
            S.op("pool", lambda e, E3=E3, E3s=E3s: e.tensor_tensor(out=E3s, in0=E3, in1=msT, op=ALU.mult), [o_E3, o_c32], [o_E3s])
            Bt0, o_Bt0 = PB[par][h]["Bt"][0]
            S.op("dve", lambda e, p=pkk[h][0], E3s=E3s, Bt0=Bt0, n=n: e.scalar_tensor_tensor(out=Bt0, in0=p, scalar=nbet[:, n:n + 1], in1=E3s,
                                                                                            op0=ALU.mult, op1=ALU.mult), [pkk[h][1], o_nbet, o_E3s], [o_Bt0])
        yield
        btr = sp.bank()
        for h in hs:
            Bt0, o_Bt0 = PB[par][h]["Bt"][0]
            B0, o_B0 = PB[par][h]["B"][0]
            Q0, o_Q0 = PB[par][h]["Q"][0]
            p, op_ = _slot(btr, h)
            pb16 = p.bitcast(BF16)[:, 0:128]
            S.op("pe", lambda e, pb16=pb16, Bt0=Bt0: e.transpose(pb16, Bt0, ident_bf), [o_Bt0, o_cbf], [op_])
            S.op("act", lambda e, pb16=pb16, B0=B0: e.activation(out=B0, in_=pb16, func=AF.Copy), [op_], [o_B0])
            S.op("pool", lambda e, Bt0=Bt0, Q0=Q0: e.tensor_tensor(out=Q0, in0=Bt0, in1=ident_bf, op=ALU.add), [o_Bt0, o_cbf], [o_Q0])
        yield
        bB = sp.bank(); bBt = sp.bank()
        for h in hs:
            B0, o_B0 = PB[par][h]["B"][0]
            Bt0, o_Bt0 = PB[par][h]["Bt"][0]
            p1, op1 = _slot(bB, h); p2, op2 = _slot(bBt, h)
            S.op("pe", lambda e, p=p1, Bt0=Bt0, B0=B0: e.matmul(p, lhsT=Bt0, rhs=B0, start=True, stop=True), [o_Bt0, o_B0], [op1])
            S.op("pe", lambda e, p=p2, Bt0=Bt0, B0=B0: e.matmul(p, lhsT=B0, rhs=Bt0, start=True, stop=True), [o_Bt0, o_B0], [op2])
        for h in hs:
            B1, o_B1 = PB[par][h]["B"][1]
            Bt1, o_Bt1 = PB[par][h]["Bt"][1]
            p1, op1 = _slot(bB, h); p2, op2 = _slot(bBt, h)
            S.op("act", lambda e, p=p1, B1=B1: e.activation(out=B1, in_=p, func=AF.Copy), [op1], [o_B1])
            S.op("dve", lambda e, p=p2, Bt1=Bt1: e.tensor_copy(out=Bt1, in_=p), [op2], [o_Bt1])
        yield
        for it in range(1, 6):
            bQ = sp.bank()
            bB = sp.bank() if it <= 4 else None
            bBt = sp.bank() if it <= 3 else None
            for h in hs:
                Bk, o_Bk = PB[par][h]["B"][it % 2]
                Btk, o_Btk = PB[par][h]["Bt"][it % 2]
                Qp, o_Qp = PB[par][h]["Q"][(it - 1) % 2]
                p, op_ = _slot(bQ, h)
                S.op("pe", lambda e, p=p, Bk=Bk, Qp=Qp: e.matmul(p, lhsT=Bk, rhs=Qp, start=True, stop=True), [o_Bk, o_Qp], [op_])
                if it <= 4:
                    p1, op1 = _slot(bB, h)
                    S.op("pe", lambda e, p=p1, Btk=Btk, Bk=Bk: e.matmul(p, lhsT=Btk, rhs=Bk, start=True, stop=True), [o_Btk, o_Bk], [op1])
                if it <= 3:
                    p2, op2 = _slot(bBt, h)
                    S.op("pe", lambda e, p=p2, Btk=Btk, Bk=Bk: e.matmul(p, lhsT=Bk, rhs=Btk, start=True, stop=True), [o_Btk, o_Bk], [op2])
            for h in hs:
                Qp, o_Qp = PB[par][h]["Q"][(it - 1) % 2]
                Qn, o_Qn = PB[par][h]["Q"][it % 2]
                p, op_ = _slot(bQ, h)
                S.op("dve", lambda e, p=p, Qp=Qp, Qn=Qn: e.tensor_tensor(out=Qn, in0=Qp, in1=p, op=ALU.add), [op_, o_Qp], [o_Qn])
                if it <= 4:
                    Bn, o_Bn = PB[par][h]["B"][(it + 1) % 2]
                    p1, op1 = _slot(bB, h)
                    S.op("act", lambda e, p=p1, Bn=Bn: e.activation(out=Bn, in_=p, func=AF.Copy), [op1], [o_Bn])
                if it <= 3:
                    Btn, o_Btn = PB[par][h]["Bt"][(it + 1) % 2]
                    p2, op2 = _slot(bBt, h)
                    S.op("dve", lambda e, p=p2, Btn=Btn: e.tensor_copy(out=Btn, in_=p), [op2], [o_Btn])
            yield

    def rec_gen(t):
        par = t % 3
        hs = range(4)
        zb, o_zb = zt[t % 3]
        for c in range(2):
            rc = slice(c * 64, c * 64 + 64)
            pk = {}; pq = {}
            bpk = sp.bank(); bpq = sp.bank()
            for h in hs:
                (qT, o_q), (kT, o_k), (vT, o_v) = opnd(t, h)
                pk[h] = _slot(bpk, h); pq[h] = _slot(bpq, h)
                S.op("pe", lambda e, p=pk[h][0], kT=kT, h=h: e.matmul(p, lhsT=kT, rhs=Sbf[h][0], start=True, stop=True), [o_k, Sbf[h][1]], [pk[h][1]])
                S.op("pe", lambda e, p=pq[h][0], qT=qT, h=h: e.matmul(p, lhsT=qT, rhs=Sbf[h][0], start=True, stop=True), [o_q, Sbf[h][1]], [pq[h][1]])
            for h in hs:
                n = t * 4 + h
                vtok, o_vtok = PX[par][h]["vtok"]
                S.op("dve", lambda e, p=pk[h][0], h=h, n=n, vtok=vtok, rc=rc: e.scalar_tensor_tensor(out=rp[h][0][rc, :], in0=p[rc, :], scalar=negc[rc, n:n + 1],
                                                                                             in1=vtok[rc, :], op0=ALU.mult, op1=ALU.add),
                     [pk[h][1], o_negc, o_vtok], [rp[h][1]])
                S.op("act", lambda e, p=pq[h][0], h=h, n=n, rc=rc: e.activation(out=qs[h][0][rc, :], in_=p[rc, :], func=AF.Copy, scale=egc[rc, n:n + 1]),
                     [pq[h][1], o_egc], [qs[h][1]])
            yield
            pv = {}
            bpv = sp.bank()
            for h in hs:
                Q5, o_Q5 = PB[par][h]["Q"][1]
                pv[h] = _slot(bpv, h)
                S.op("pe", lambda e, p=pv[h][0], Q5=Q5, h=h, rc=rc: e.matmul(p, lhsT=Q5[rc, :], rhs=rp[h][0][rc, :], start=True, stop=True), [o_Q5, rp[h][1]], [pv[h][1]])
            for h in hs:
                n = t * 4 + h
                S.op("act", lambda e, p=pv[h][0], h=h, n=n, rc=rc: e.activation(out=vnew[h][0][rc, :], in_=p[rc, :], func=AF.Copy, scale=bet[rc, n:n + 1]),
                     [pv[h][1], o_bet], [vnew[h][1]])
            yield
            pS = {}; po_ = {}
            bpS = sp.bank(); bpo = sp.bank()
            for h in hs:
                kdec, o_kdec = PX[par][h]["kdec"]; aqkT, o_aqkT = PX[par][h]["aqkT"]
                pS[h] = _slot(bpS, h); po_[h] = _slot(bpo, h)
                S.op("pe", lambda e, p=pS[h][0], kdec=kdec, h=h, rc=rc: e.matmul(p, lhsT=kdec[rc, :], rhs=vnew[h][0][rc, :], start=True, stop=True),
                     [o_kdec, vnew[h][1]], [pS[h][1]])
                S.op("pe", lambda e, p=po_[h][0], aqkT=aqkT, h=h, rc=rc: e.matmul(p, lhsT=aqkT[rc, :], rhs=vnew[h][0][rc, :], start=True, stop=True),
                     [o_aqkT, vnew[h][1]], [po_[h][1]])
            for h in hs:
                n = t * 4 + h
                dl_t, o_dl = dlr[c]
                S.op("dve", lambda e, p=pS[h][0], h=h, n=n, dl_t=dl_t: e.scalar_tensor_tensor(out=Sst[h][0], in0=Sst[h][0], scalar=dl_t[:, n:n + 1], in1=p,
                                                                                             op0=ALU.mult, op1=ALU.add), [pS[h][1], o_dl, Sst[h][1]], [Sst[h][1]])
                S.op("act", lambda e, h=h: e.activation(out=Sbf[h][0], in_=Sst[h][0], func=AF.Copy), [Sst[h][1]], [Sbf[h][1]])
                S.op("dve", lambda e, p=po_[h][0], h=h, rc=rc: e.tensor_tensor(out=osb[h][0][rc, :], in0=p[rc, :], in1=qs[h][0][rc, :], op=ALU.add),
                     [po_[h][1], qs[h][1]], [osb[h][1]])
            yield
        for h in hs:
            ss_t, o_ss = ssn[h]
            S.op("act", lambda e, h=h, ss_t=ss_t: e.activation(out=junk, in_=osb[h][0], func=AF.Square, accum_out=ss_t[:, 0:1]), [osb[h][1]], [o_junk, o_ss])
            S.op("act", lambda e, ss_t=ss_t: e.activation(out=ss_t[:, 0:1], in_=ss_t[:, 0:1], func=AF.Ln, scale=1.0 / 128, bias=RMS_EPS), [o_ss], [o_ss])
            S.op("act", lambda e, ss_t=ss_t: e.activation(out=ss_t[:, 0:1], in_=ss_t[:, 0:1], func=AF.Exp, scale=-0.5), [o_ss], [o_ss])
            S.op("act", lambda e, h=h: e.activation(out=szb[h][0], in_=zb[:, h * 128:(h + 1) * 128], func=AF.Exp, scale=-1.0), [o_zb], [szb[h][1]])
        yield
        for h in hs:
            S.op("act", lambda e, h=h: e.activation(out=szb[h][0], in_=szb[h][0], func=AF.Ln, bias=1.0), [szb[h][1]], [szb[h][1]])
            S.op("act", lambda e, h=h: e.activation(out=szb[h][0], in_=szb[h][0], func=AF.Exp, scale=-1.0), [szb[h][1]], [szb[h][1]])
        yield
        for h in hs:
            ss_t, o_ss = ssn[h]
            S.op("dve", lambda e, h=h, ss_t=ss_t: e.scalar_tensor_tensor(out=t1b[h][0], in0=osb[h][0], scalar=ss_t[:, 0:1], in1=gnrep, op0=ALU.mult, op1=ALU.mult),
                 [osb[h][1], o_ss, o_gn], [t1b[h][1]])
            S.op("pool", lambda e, h=h: e.tensor_tensor(out=t1b[h][0], in0=t1b[h][0], in1=zb[:, h * 128:(h + 1) * 128], op=ALU.mult), [t1b[h][1], o_zb], [t1b[h][1]])
            S.op("pool", lambda e, h=h: e.tensor_tensor(out=yv[:, t, h * 128:(h + 1) * 128], in0=t1b[h][0], in1=szb[h][0], op=ALU.mult),
                 [t1b[h][1], szb[h][1]], [o_y[t]])
        yield

    gens = {}

    def adv(tt):
        try:
            next(gens[tt])
        except StopIteration:
            del gens[tt]

    gens[0] = pre_gen(0)
    while 0 in gens:
        adv(0)
        yield
    if ntile > 1:
        gens[1] = pre_gen(1)
        for _ in range(5):
            adv(1)
            yield
    for t in range(ntile):
        if t + 2 < ntile:
            gens[t + 2] = pre_gen(t + 2)
        gr = rec_gen(t)
        rec_done = False
        step = 0
        while True:
            if not rec_done:
                try:
                    next(gr)
                except StopIteration:
                    rec_done = True
            order = (t + 1, t + 2) if step % 2 == 0 else (t + 2, t + 1)
            for tt in order:
                if tt in gens:
                    adv(tt)
                    break
            step += 1
            yield
            if rec_done and (t + 1) not in gens:
                break


def host_inputs(inputs, b):
    cst, bm, kaug, qaug = _host_consts()
    f = lambda a: np.ascontiguousarray(np.asarray(a, dtype=np.float32))
    m = {
        "x": f(inputs["x"][b]),
        "ln1": f(inputs["ln1"]), "ln2": f(inputs["ln2"]), "ln_f": f(inputs["ln_f"]).reshape(1, D),
        "w_in": f(inputs["w_in"]),
        "conv_qkv_r": f(np.asarray(inputs["conv_qkv"]).reshape(DEPTH, 4, 12, 128).transpose(0, 3, 2, 1).reshape(DEPTH, 128, 48)),
        "a_log_r": f(np.tile(np.asarray(inputs["a_log"]), (1, 32)).reshape(DEPTH, 1, 128)),
        "dt_bias_r": f(np.tile(np.asarray(inputs["dt_bias"]), (1, 32)).reshape(DEPTH, 1, 128)),
        "gdn_norm": f(inputs["gdn_norm"]),
        "w_out": f(inputs["w_out"]),
        "w_gate": f(inputs["w_gate"]), "w_up": f(inputs["w_up"]),
        "ffn_conv_r": f(np.asarray(inputs["ffn_conv"]).reshape(DEPTH, 3, NFC, 128).transpose(0, 3, 2, 1).reshape(DEPTH, 128, NFC * 3)),
        "w_down": f(inputs["w_down"]),
        "cst": cst, "cstb": cst[:, 0:256].astype(ml_dtypes.bfloat16), "bm": bm.astype(ml_dtypes.bfloat16),
        "kaug": kaug.astype(ml_dtypes.bfloat16), "qaug": qaug.astype(ml_dtypes.bfloat16),
    }
    return m


_NC_CACHE = {}


def kernel(**inputs):
    n = 8
    if "full" not in _NC_CACHE:
        _NC_CACHE["full"] = build(dict())
    in_maps = [host_inputs(inputs, b) for b in range(n)]
    res = run_bass_kernel_spmd(_NC_CACHE["full"], in_maps, core_ids=list(range(n)))
    return np.stack([r["out"] for r in res.results], axis=0).astype(np.float32)
```

```python
from contextlib import ExitStack
import numpy as np
import ml_dtypes
import concourse.bass as bass
import concourse.mybir as mybir
from concourse.bass_utils import run_bass_kernel_spmd

F32 = mybir.dt.float32
BF16 = mybir.dt.bfloat16
AF = mybir.ActivationFunctionType
ALU = mybir.AluOpType

S_LEN = 4096
D = 1024
DEPTH = 2
NT = 32
IN_COLS = 3592
DFF = 2816
NFC = 22
RMS_EPS = 1e-6
L2_EPS = 1e-6
NEG = -30000.0

ENGS = ("pe", "act", "dve", "pool", "sp")


class Slot:
    __slots__ = ("name", "kind", "sem", "count")

    def __init__(self, name, kind):
        self.name = name
        self.kind = kind
        self.sem = None
        self.count = 0


class Obj:
    __slots__ = ("name", "w_ev", "r_ev", "dq")

    def __init__(self, name):
        self.name = name
        self.w_ev = {}
        self.r_ev = {}
        self.dq = {}


class Op:
    __slots__ = ("eng", "idx", "fn", "waits", "need_inc", "count", "slot", "tag")

    def __init__(self, eng, idx, fn):
        self.eng = eng
        self.idx = idx
        self.fn = fn
        self.waits = []
        self.need_inc = False
        self.count = None
        self.slot = None


def _merge(dst, src):
    for k, v in src.items():
        if dst.get(k, -1) < v:
            dst[k] = v


class Sched:
    def __init__(self, nc):
        self.nc = nc
        self.ops = {e: [] for e in ENGS}
        self.seen = {e: {} for e in ENGS}
        self.slots = []
        self.free = {"hw": [], "sw": []}
        self.phase_slots = []
        self.gen = 0
        self.bar = {}
        self.bar_gen = 0
        self.bar_applied = {e: 0 for e in ENGS}

    def keep(self):
        self.phase_slots = []

    def barrier(self):
        ev = {}
        for e in ENGS:
            n = len(self.ops[e])
            for i in range(n - 1, -1, -1):
                if self.ops[e][i].slot is None:
                    ev[("e", e)] = i
                    break
        for sl in self.slots:
            ev[("d", sl)] = sl.count
        self.bar = ev
        self.bar_gen += 1
        for sl in self.phase_slots:
            self.free[sl.kind].append(sl)
        self.phase_slots = []
        self.gen += 1

    def _slot_for(self, obj, qk):
        ent = obj.dq.get(qk)
        if ent is not None and ent[1] == self.gen:
            return ent[0]
        if self.free[qk]:
            sl = self.free[qk].pop()
        else:
            sl = Slot("%s_%d" % (qk, len(self.slots)), qk)
            self.slots.append(sl)
        self.phase_slots.append(sl)
        obj.dq[qk] = (sl, self.gen)
        return sl

    def _record(self, eng, fn, reads, writes, dma_obj=None):
        lst = self.ops[eng]
        op = Op(eng, len(lst), fn)
        op.tag = getattr(self, 'tag', '')
        need = {}
        mykey = ("e", eng)
        for o in reads:
            _merge(need, o.w_ev)
        for o in writes:
            _merge(need, o.w_ev)
            _merge(need, o.r_ev)
        if self.bar_applied[eng] != self.bar_gen:
            self.bar_applied[eng] = self.bar_gen
            for k, v in self.bar.items():
                if k == mykey:
                    continue
                if need.get(k, -1) < v:
                    need[k] = v
        seen = self.seen[eng]
        for k, v in need.items():
            if k == mykey and dma_obj is None and eng == "pe":
                continue
            if seen.get(k, -1) >= v:
                continue
            seen[k] = v
            if k[0] == "e":
                prod = self.ops[k[1]][v]
                prod.need_inc = True
                op.waits.append(("e", prod))
            else:
                op.waits.append(("d", k[1], v))
        lst.append(op)
        if dma_obj is not None:
            sl = self._slot_for(dma_obj, "sw" if eng == "pool" else "hw")
            sl.count += 1
            op.slot = sl
            ev = {("d", sl): sl.count}
        else:
            ev = {mykey: op.idx}
        for o in reads:
            _merge(o.r_ev, ev)
        for o in writes:
            if o.r_ev:
                o.w_ev = dict(ev)
                o.r_ev = {}
            else:
                _merge(o.w_ev, ev)
        return op

    def op(self, eng, fn, reads=(), writes=()):
        return self._record(eng, fn, list(reads), list(writes))

    def dma(self, eng, out, in_, reads, writes, sb_obj, **kw):
        def fn(e, out=out, in_=in_, kw=kw):
            return e.dma_start(out=out, in_=in_, **kw)
        return self._record(eng, fn, list(reads), list(writes), dma_obj=sb_obj)

    def emit(self, stack, final_wait_objs=()):
        nc = self.nc
        esem = {}
        for e in ENGS:
            if e != "sp":
                esem[e] = stack.enter_context(nc.semaphore("s_" + e))
        for sl in self.slots:
            sl.sem = stack.enter_context(nc.semaphore("d_" + sl.name))
        for e in ENGS:
            c = 0
            for op in self.ops[e]:
                if op.slot is None and op.need_inc:
                    c += 1
                    op.count = c
        block = stack.enter_context(nc.Block())

        def run(engname, e):
            for op in self.ops[engname]:
                for w in op.waits:
                    if w[0] == "e":
                        e.wait_ge(esem[w[1].eng], w[1].count)
                    else:
                        e.wait_ge(w[1].sem, 16 * w[2])
                ins = op.fn(e)
                if op.slot is not None:
                    ins.then_inc(op.slot.sem, 16)
                elif op.need_inc:
                    ins.then_inc(esem[engname], 1)
            if engname == "sp":
                for sl in self.slots:
                    e.wait_ge(sl.sem, 16 * sl.count)

        @block.tensor
        def _(e):
            run("pe", e)

        @block.scalar
        def _(e):
            run("act", e)

        @block.vector
        def _(e):
            run("dve", e)

        @block.gpsimd
        def _(e):
            run("pool", e)

        @block.sync
        def _(e):
            run("sp", e)


CST_NAMES = ["ident", "ones", "mbcT", "msT", "lmT", "bones", "sel0", "sel1"]


def _host_consts():
    i = np.arange(128)
    same = (i[:, None] // 64) == (i[None, :] // 64)
    c = {}
    c["ident"] = np.eye(128, dtype=np.float32)
    c["ones"] = np.ones((128, 128), np.float32)
    c["mbcT"] = np.where(same & (i[:, None] <= i[None, :]), 0.0, NEG).astype(np.float32)
    c["msT"] = (same & (i[:, None] < i[None, :])).astype(np.float32)
    c["lmT"] = (same & (i[:, None] <= i[None, :])).astype(np.float32)
    c["bones"] = same.astype(np.float32)
    s0 = np.zeros((128, 128), np.float32); s0[0:64, :] = 1.0
    s1 = np.zeros((128, 128), np.float32); s1[64:128, :] = 1.0
    c["sel0"] = s0
    c["sel1"] = s1
    cst = np.concatenate([c[n] for n in CST_NAMES], axis=1)
    ki = np.arange(128)[:, None, None]
    o = np.arange(17)[None, :, None]
    qi = np.arange(128)[None, None, :]
    dl = o * 128 + qi - ki
    cnt = ((dl >= 0) & (dl <= 128)).astype(np.int64) + ((dl >= 0) & (dl % 4 == 0) & (dl <= 512)) \
        + ((dl >= 0) & (dl % 16 == 0) & (dl <= 2048))
    bm = np.where(cnt > 0, np.log(np.maximum(cnt, 1).astype(np.float64)), NEG).astype(np.float32)
    bm = bm.reshape(128, 17 * 128)
    t = np.arange(S_LEN)
    slopes = 2.0 ** (-8.0 * np.arange(1, 9) / 8)
    kaug = np.zeros((8, 4, S_LEN), np.float32)
    qaug = np.zeros((8, 4, S_LEN), np.float32)
    for h in range(8):
        kaug[h, 0] = slopes[h] * (t % 128)
        kaug[h, 1] = slopes[h] * 128 * (t // 128)
        kaug[h, 2] = 1.0
        kaug[h, 3] = 1.0
        qaug[h, 0] = 1.0
        qaug[h, 1] = 1.0
        qaug[h, 2] = -slopes[h] * (t % 128)
        qaug[h, 3] = -slopes[h] * 128 * (t // 128)
    return cst, bm, kaug, qaug


class KB:
    ARENA_WORDS = 53200

    def __init__(self, nc, st, cfg):
        self.nc = nc
        self.st = st
        self.cfg = cfg
        self.S = Sched(nc)
        self.arena = st.enter_context(nc.sbuf_tensor("arena", [128, self.ARENA_WORDS], F32))
        self.top = 0
        self.pbank = [st.enter_context(nc.psum_tensor("pb%d" % i, [128, 512], F32)) for i in range(8)]
        self.pobj = [Obj("pb%d" % i) for i in range(8)]
        self.final_objs = []

    def f32(self, name, n):
        off = self.top
        self.top += n + (n & 1)
        assert self.top <= self.ARENA_WORDS, (name, self.top)
        return self.arena[:, off:off + n], Obj(name)

    def bf(self, name, n):
        assert n % 2 == 0
        off = self.top
        self.top += n // 2 + ((n // 2) & 1)
        assert self.top <= self.ARENA_WORDS, (name, self.top)
        return self.arena[:, off:off + n // 2].bitcast(BF16), Obj(name)

    def mark(self):
        return self.top

    def release(self, m):
        self.top = m
        self.S.barrier()

    def dram(self, name, shape, dt, out=False):
        out = out or (name in self.cfg.get("outs", ()))
        kind = "ExternalOutput" if out else "Internal"
        return self.nc.dram_tensor(name, list(shape), dt, kind=kind).ap()


def build(cfg):
    nc = bass.Bass("TRN2", target_bir_lowering=False)
    dbg = cfg.get("dbg", False)
    layers = cfg.get("layers", [0, 1])
    phases = cfg.get("phases", {"p1", "p2", "p3", "p4a", "p4b"})
    final_norm = cfg.get("final", True)

    def inp(name, shape, dt=F32):
        return nc.dram_tensor(name, list(shape), dt, kind="ExternalInput").ap()

    x_in = inp("x", [S_LEN, D])
    ln1 = inp("ln1", [DEPTH, D]); ln2 = inp("ln2", [DEPTH, D]); lnf = inp("ln_f", [1, D])
    w_in = inp("w_in", [DEPTH, D, IN_COLS])
    convq = inp("conv_qkv_r", [DEPTH, 128, 12 * 4])
    alog = inp("a_log_r", [DEPTH, 1, 128]); dtb = inp("dt_bias_r", [DEPTH, 1, 128])
    gdnn = inp("gdn_norm", [DEPTH, 128])
    w_out = inp("w_out", [DEPTH, D, D])
    w_gate = inp("w_gate", [DEPTH, D, DFF]); w_up = inp("w_up", [DEPTH, D, DFF])
    ffnc = inp("ffn_conv_r", [DEPTH, 128, NFC * 3])
    w_down = inp("w_down", [DEPTH, DFF, D])
    cst_d = inp("cst", [128, 8 * 128]); cstb_d = inp("cstb", [128, 2 * 128], BF16); bm_d = inp("bm", [128, 17 * 128], BF16)
    kaug_d = inp("kaug", [8, 4, S_LEN], BF16); qaug_d = inp("qaug", [8, 4, S_LEN], BF16)
    out_d = nc.dram_tensor("out", [S_LEN, D], F32, kind="ExternalOutput").ap()

    with ExitStack() as st:
        kb = KB(nc, st, cfg)
        S = kb.S
        xs = kb.dram("xs", [S_LEN, D], F32, out=(dbg or cfg.get("xs_out", False)))
        gqkv = kb.dram("gqkv", [12, 128, S_LEN], BF16, out=dbg)
        aqk = kb.dram("aqk", [8, 128, S_LEN], BF16, out=dbg)
        zs = kb.dram("zs", [S_LEN, 512], BF16, out=dbg)
        avs = kb.dram("avs", [S_LEN, 512], BF16, out=dbg)
        glog_d = kb.dram("glog", [S_LEN, 8], F32, out=dbg)
        ydbg = kb.dram("ydbg", [S_LEN, D], BF16, out=True) if (dbg or cfg.get("ydump")) else None
        o_xs = [Obj("xs%d" % i) for i in range(NT)]
        o_gqkv = Obj("gqkv"); o_aqk = Obj("aqk"); o_zs = Obj("zs"); o_avs = Obj("avs")

        cst32, o_cst32 = kb.f32("cst32", 8 * 128)
        cstbf, o_cstbf = kb.bf("cstbf", 2 * 128)
        S.dma("sp", cst32, cst_d, [], [o_cst32], o_cst32)
        S.dma("sp", cstbf, cstb_d, [], [o_cstbf], o_cstbf)
        C32 = {n: cst32[:, i * 128:(i + 1) * 128] for i, n in enumerate(CST_NAMES)}
        ident_bf = cstbf[:, 0:128]
        ones_bf = cstbf[:, 128:256]
        kb.C32 = C32; kb.o_cst32 = o_cst32; kb.ident_bf = ident_bf; kb.ones_bf = ones_bf; kb.o_cstbf = o_cstbf
        glog, o_glog = kb.f32("glog", NT * 8)
        kb.glog = glog; kb.o_glog = o_glog

        S.keep()
        base_mark = kb.mark()
        phases_all = phases
        for l in layers:
            phases = cfg.get('phases_by_layer', {}).get(l, phases_all)
            kb.cur_phases = phases
            x_src = x_in if l == layers[0] else xs
            if "p1" in phases:
                m = kb.mark()
                S.tag = 'phase1_' + str(l)
                phase1(kb, l, x_src, ln1, w_in, convq, gqkv, aqk, zs, avs, o_xs, o_gqkv, o_aqk, o_zs, o_avs, glog_d)
                kb.release(m)
            m_y = kb.mark()
            ybuf, _ = kb.bf("ybuf", NT * 1024)
            yv = ybuf.rearrange("p (t c) -> p t c", t=NT)
            o_y = [Obj("y%d" % i) for i in range(NT)]
            if "p2" in phases and "p3" in phases and cfg.get("interleave", False):
                m = kb.mark()
                g2 = phase2_gen(kb, l, gqkv, zs, o_gqkv, o_zs, alog, dtb, gdnn, yv, o_y, banks=(4, 5, 6, 7))
                g3 = phase3_gen(kb, l, aqk, avs, o_aqk, o_avs, bm_d, kaug_d, qaug_d, yv, o_y, ps_banks=(0, 1, 2), pacc_banks=(3,), look=2)
                d2 = d3 = False
                ratio = cfg.get("ratio", 4)
                while not (d2 and d3):
                    if not d2:
                        S.tag = 'phase2_' + str(l)
                        try:
                            next(g2)
                        except StopIteration:
                            d2 = True
                    for _ in range(ratio):
                        if d3:
                            break
                        S.tag = 'phase3_' + str(l)
                        try:
                            next(g3)
                        except StopIteration:
                            d3 = True
                kb.release(m)
            else:
                if "p2" in phases:
                    m = kb.mark()
                    S.tag = 'phase2_' + str(l)
                    phase2(kb, l, gqkv, zs, o_gqkv, o_zs, alog, dtb, gdnn, yv, o_y, banks=cfg.get('p2banks', range(8)))
                    kb.release(m)
                if "p3" in phases:
                    m = kb.mark()
                    S.tag = 'phase3_' + str(l)
                    phase3(kb, l, aqk, avs, o_aqk, o_avs, bm_d, kaug_d, qaug_d, yv, o_y)
                    kb.release(m)
            if (dbg and not cfg.get("lim") and ("p2" in phases or "p3" in phases) and "p4a" not in phases) or cfg.get("ydump"):
                for t in range(NT):
                    S.dma("sp", ydbg[t * 128:(t + 1) * 128, :], yv[:, t, :], [o_y[t]], [], o_y[t])
                    kb.final_objs.append(o_y[t])
            if "p4a" in phases:
                m = kb.mark()
                S.tag = 'phase4a_' + str(l)
                phase4a(kb, l, x_src, xs, o_xs, w_out, yv, o_y)
                kb.release(m)
            kb.release(m_y)
            if "p4b" in phases:
                m = kb.mark()
                last = (l == layers[-1]) and final_norm
                x4 = xs if ("p4a" in phases or l != layers[0]) else x_in
                S.tag = 'phase4b_' + str(l)
                phase4b(kb, l, x4, out_d if last else xs, o_xs, ln2, w_gate, w_up, ffnc, w_down, lnf if last else None)
                kb.release(m)
                if dbg and cfg.get("snap") and l == 0:
                    m = kb.mark()
                    xsnap = kb.dram("xsnap", [S_LEN, D], F32, out=True)
                    bufs = [kb.f32("snap%d" % i, D) for i in range(2)]
                    for t in range(NT):
                        bt, ob = bufs[t % 2]
                        S.dma("sp", bt, xs[t * 128:(t + 1) * 128, :], [o_xs[t]], [ob], ob)
                        S.dma("sp", xsnap[t * 128:(t + 1) * 128, :], bt, [ob], [], ob)
                    kb.release(m)
        npad = cfg.get("pad", 0)
        if npad:
            padt, o_pad = kb.f32("padt", 8)
            for i in range(npad):
                S.op("dve", lambda e: e.memset(padt, 0.0), [], [o_pad])
                S.op("act", lambda e: e.activation(out=padt, in_=padt, func=AF.Copy), [o_pad], [o_pad])
        S.emit(st, final_wait_objs=kb.final_objs)
    return nc


def make_stages(kb, n=3, cap=2048):
    return [kb.f32("wstage%d" % i, cap) for i in range(n)]


def load_weight(kb, name, src2d, kchunks, ncols, stages, eng="sp"):
    S = kb.S
    w, o = kb.bf(name, kchunks * ncols)
    wv = w.rearrange("p (k c) -> p k c", k=kchunks)
    srcv = src2d.rearrange("(k p) c -> p k c", p=128)
    n = getattr(kb, "wstage_n", 0)
    for k in range(kchunks):
        c0 = 0
        while c0 < ncols:
            stg, o_stg = stages[n % len(stages)]
            cap = stg.shape[-1]
            c1 = min(ncols, c0 + cap)
            S.dma(eng, stg[:, 0:c1 - c0], srcv[:, k, c0:c1], [], [o_stg], o_stg)
            dst = wv[:, k, c0:c1]
            ce = ("act", "dve", "pool")[n % 3]
            if ce == "act":
                S.op("act", lambda e, dst=dst, stg=stg, m=c1 - c0: e.activation(out=dst, in_=stg[:, 0:m], func=AF.Copy), [o_stg], [o])
            else:
                S.op(ce, lambda e, dst=dst, stg=stg, m=c1 - c0: e.tensor_copy(out=dst, in_=stg[:, 0:m]), [o_stg], [o])
            n += 1
            c0 = c1
    kb.wstage_n = n
    return wv, o


def load_weight_blocks(kb, wv, src2d, kchunks, blocks, stages, eng="sp"):
    S = kb.S
    srcv = src2d.rearrange("(k p) c -> p k c", p=128)
    n = getattr(kb, "wstage_n", 0)
    objs = []
    for (c0, c1) in blocks:
        o = Obj("wblk%d" % c0)
        objs.append(o)
        width = c1 - c0
        cap = stages[0][0].shape[-1]
        kstep = max(1, min(kchunks, cap // width))
        for k0 in range(0, kchunks, kstep):
            k1 = min(kchunks, k0 + kstep)
            stg, o_stg = stages[n % len(stages)]
            stv = stg[:, 0:(k1 - k0) * width].rearrange("p (k c) -> p k c", k=k1 - k0)
            S.dma(eng, stv, srcv[:, k0:k1, c0:c1], [], [o_stg], o_stg)
            dst = wv[:, k0:k1, c0:c1]
            ce = ("act", "dve", "pool")[n % 3]
            if ce == "act":
                S.op("act", lambda e, dst=dst, stv=stv: e.activation(out=dst, in_=stv, func=AF.Copy), [o_stg], [o])
            else:
                S.op(ce, lambda e, dst=dst, stv=stv: e.tensor_copy(out=dst, in_=stv), [o_stg], [o])
            n += 1
    kb.wstage_n = n
    return objs


def rms_stats(kb, ss, o_ss, n, inv_n, eps):
    S = kb.S
    S.op("dve", lambda e: e.tensor_scalar(out=ss, in0=ss, scalar1=inv_n, scalar2=eps, op0=ALU.mult, op1=ALU.add), [o_ss], [o_ss])
    S.op("act", lambda e: e.activation(out=ss, in_=ss, func=AF.Sqrt), [o_ss], [o_ss])
    S.op("dve", lambda e: e.reciprocal(out=ss, in_=ss), [o_ss], [o_ss])


def norm_transpose(kb, xt, o_xt, j, lnrep, o_ln, ss, o_ss, hb, o_hb, junk, o_junk, pT, o_pT, hT, o_hT, col0, evac_eng):
    S = kb.S
    xj = xt[:, j, :]
    S.op("dve", lambda e: e.scalar_tensor_tensor(out=hb, in0=xj, scalar=ss[:, j:j + 1], in1=lnrep, op0=ALU.mult, op1=ALU.mult),
         [o_xt, o_ss, o_ln], [o_hb])
    pTb = pT.bitcast(BF16)
    for k in range(8):
        S.op("pe", lambda e, k=k: e.transpose(pTb[:, k * 128:(k + 1) * 128], hb[:, k * 128:(k + 1) * 128], kb.ident_bf),
             [o_hb, kb.o_cstbf], [o_pT])
    src = pTb.rearrange("p (k t) -> p k t", k=8)
    dst = hT[:, :, col0:col0 + 128]
    if evac_eng == "act":
        S.op("act", lambda e: e.activation(out=dst, in_=src, func=AF.Copy), [o_pT], [o_hT])
    else:
        S.op("dve", lambda e: e.tensor_copy(out=dst, in_=src), [o_pT], [o_hT])


def phase4b(kb, l, x_src, x_dst, o_xs, ln2, w_gate, w_up, ffnc, w_down, lnf):
    S = kb.S
    nc = kb.nc
    actreg, _ = kb.f32("actreg", NFC * 256)
    stages = [(actreg[:, i * 2816:(i + 1) * 2816], Obj("wst%d" % i)) for i in range(2)]
    o_stages = [o for _, o in stages]
    wg_a, _ = kb.bf("wg", 8 * DFF)
    wu_a, _ = kb.bf("wu", 8 * DFF)
    wg = wg_a.rearrange("p (k c) -> p k c", k=8)
    wu = wu_a.rearrange("p (k c) -> p k c", k=8)
    lnrep, o_ln = kb.f32("ln2rep", D)
    S.dma("sp", lnrep, ln2[l:l + 1, :].partition_broadcast(128), [], [o_ln], o_ln)
    if lnf is not None:
        lnfrep, o_lnf = kb.f32("lnfrep", D)
        S.dma("sp", lnfrep, lnf[0:1, :].partition_broadcast(128), [], [o_lnf], o_lnf)
    cw, o_cw = kb.f32("ffncw", NFC * 3)
    S.dma("sp", cw, ffnc[l], [], [o_cw], o_cw)
    cwv = cw.rearrange("p (c j) -> p c j", j=3)
    halo, o_halo = kb.f32("halo", NFC * 2)
    halov = halo.rearrange("p (c j) -> p c j", j=2)
    S.op("pool", lambda e: e.memset(halo, 0.0), [], [o_halo])
    xt_a, o_xt = kb.f32("xt", 4 * D)
    xt = xt_a.rearrange("p (j d) -> p j d", j=4)
    hb, o_hb = kb.bf("hb", D)
    junk, o_junk = kb.bf("junk", D)
    ss, o_ss = kb.f32("ss", 4)
    hT_a, o_hT = kb.bf("hT", 8 * 512)
    hT = hT_a.rearrange("p (k t) -> p k t", k=8)
    actT_a, o_actT = actreg.bitcast(BF16), Obj("actT")
    actT = actT_a.rearrange("p (c t) -> p c t", c=NFC)
    pres = [kb.f32("pre%d" % i, 514) for i in range(2)]
    cvs = [kb.f32("cv%d" % i, 512) for i in range(3)]
    xo_a, o_xo, xo = xt_a, o_xt, xt
    ss2, o_ss2 = kb.f32("ss2", 4)
    pb = kb.pbank; po = kb.pobj
    S.dma("sp", xt, x_src[0:512, :].rearrange("(j p) d -> p j d", p=128), o_xs[0:4], [o_xt], o_xt)
    fblocks = [(c0, min(DFF, c0 + 512)) for c0 in range(0, DFF, 512)]
    o_wgb, o_wub = [], []
    for blk in fblocks:
        o_wgb += load_weight_blocks(kb, wg, w_gate[l], 8, [blk], stages)
        o_wub += load_weight_blocks(kb, wu, w_up[l], 8, [blk], stages)
    wd, o_wd = load_weight(kb, "wd", w_down[l], NFC, D, stages)
    for T in range(8):
        tiles = o_xs[T * 4:(T + 1) * 4]
        src = x_src[T * 512:(T + 1) * 512, :].rearrange("(j p) d -> p j d", p=128)
        if T > 0:
            S.dma("sp", xt, src, tiles, [o_xt], o_xt)
        for j in range(4):
            S.op("act", lambda e, j=j: e.activation(out=junk, in_=xt[:, j, :], func=AF.Square, accum_out=ss[:, j:j + 1]),
                 [o_xt], [o_junk, o_ss])
        rms_stats(kb, ss, o_ss, 4, 1.0 / D, RMS_EPS)
        for j in range(4):
            norm_transpose(kb, xt, o_xt, j, lnrep, o_ln, ss, o_ss, hb, o_hb, junk, o_junk, pb[6 + (j % 2)], po[6 + (j % 2)],
                           hT, o_hT, j * 128, "act" if j % 2 == 0 else "dve")
        def ffn_a(fc):
            pg, opg = pb[(2 * fc) % 6], po[(2 * fc) % 6]
            pu, opu = pb[(2 * fc + 1) % 6], po[(2 * fc + 1) % 6]
            for k in range(8):
                S.op("pe", lambda e, k=k, fc=fc, pg=pg: e.matmul(pg[:, :], lhsT=wg[:, k, fc * 128:(fc + 1) * 128], rhs=hT[:, k, :],
                                                                 start=(k == 0), stop=(k == 7)), [o_wgb[fc // 4], o_hT], [opg])
            for k in range(8):
                S.op("pe", lambda e, k=k, fc=fc, pu=pu: e.matmul(pu[:, :], lhsT=wu[:, k, fc * 128:(fc + 1) * 128], rhs=hT[:, k, :],
                                                                 start=(k == 0), stop=(k == 7)), [o_wub[fc // 4], o_hT], [opu])
            pre, o_pre = pres[fc % 2]
            cv, o_cv = cvs[fc % 3]
            S.op("pool", lambda e, pre=pre, fc=fc: e.tensor_copy(out=pre[:, 0:2], in_=halov[:, fc, :]), [o_halo], [o_pre])
            S.op("act", lambda e, pre=pre, pg=pg: e.activation(out=pre[:, 2:514], in_=pg[:, :], func=AF.Copy), [opg], [o_pre])
            S.op("pool", lambda e, pre=pre, fc=fc: e.tensor_copy(out=halov[:, fc, :], in_=pre[:, 512:514]), [o_pre], [o_halo])
            S.op("dve", lambda e, pre=pre, cv=cv, fc=fc: e.tensor_scalar(out=cv, in0=pre[:, 2:514], scalar1=cwv[:, fc, 2:3], scalar2=None,
                                                                         op0=ALU.mult), [o_pre, o_cw], [o_cv])
            S.op("dve", lambda e, pre=pre, cv=cv, fc=fc: e.scalar_tensor_tensor(out=cv, in0=pre[:, 1:513], scalar=cwv[:, fc, 1:2], in1=cv,
                                                                                op0=ALU.mult, op1=ALU.add), [o_pre, o_cw, o_cv], [o_cv])
            S.op("dve", lambda e, pre=pre, cv=cv, fc=fc: e.scalar_tensor_tensor(out=cv, in0=pre[:, 0:512], scalar=cwv[:, fc, 0:1], in1=cv,
                                                                                op0=ALU.mult, op1=ALU.add), [o_pre, o_cw, o_cv], [o_cv])

        def ffn_b(fc):
            pu, opu = pb[(2 * fc + 1) % 6], po[(2 * fc + 1) % 6]
            cv, o_cv = cvs[fc % 3]
            S.op("act", lambda e, cv=cv: e.activation(out=cv, in_=cv, func=AF.Silu), [o_cv], [o_cv])
            S.op("dve", lambda e, cv=cv, pu=pu, fc=fc: e.tensor_tensor(out=actT[:, fc, :], in0=cv, in1=pu[:, :], op=ALU.mult),
                 [o_cv, opu], [o_actT] + (o_stages if T == 0 else []))

        for s_ in range(NFC + 1):
            if s_ < NFC:
                ffn_a(s_)
            if s_ >= 1:
                ffn_b(s_ - 1)
        for j in range(4):
            for nh in range(2):
                pi = (j * 2 + nh) % 6
                pd, opd = pb[pi], po[pi]
                for fc in range(NFC):
                    S.op("pe", lambda e, fc=fc, j=j, nh=nh, pd=pd: e.matmul(pd[:, :], lhsT=actT[:, fc, j * 128:(j + 1) * 128],
                                                                          rhs=wd[:, fc, nh * 512:(nh + 1) * 512],
                                                                          start=(fc == 0), stop=(fc == NFC - 1)), [o_actT, o_wd], [opd])
                S.op("dve", lambda e, j=j, nh=nh, pd=pd: e.tensor_tensor(out=xo[:, j, nh * 512:(nh + 1) * 512],
                                                                       in0=xt[:, j, nh * 512:(nh + 1) * 512], in1=pd[:, :], op=ALU.add),
                     [o_xt, opd], [o_xo])
        dst = x_dst[T * 512:(T + 1) * 512, :].rearrange("(j p) d -> p j d", p=128)
        if lnf is not None:
            for j in range(4):
                S.op("act", lambda e, j=j: e.activation(out=junk, in_=xo[:, j, :], func=AF.Square, accum_out=ss2[:, j:j + 1]),
                     [o_xo], [o_junk, o_ss2])
            rms_stats(kb, ss2, o_ss2, 4, 1.0 / D, RMS_EPS)
            for j in range(4):
                S.op("dve", lambda e, j=j: e.scalar_tensor_tensor(out=xo[:, j, :], in0=xo[:, j, :], scalar=ss2[:, j:j + 1], in1=lnfrep,
                                                                  op0=ALU.mult, op1=ALU.mult), [o_xo, o_ss2, o_lnf], [o_xo])
            S.dma("sp", dst, xo_a.rearrange("p (j d) -> p j d", j=4), [o_xo], [], o_xo)
        else:
            S.dma("sp", dst, xo_a.rearrange("p (j d) -> p j d", j=4), [o_xo], tiles, o_xo)
        if o_xo not in kb.final_objs:
            kb.final_objs.append(o_xo)


def phase1(kb, l, x_src, ln1, w_in, convq, gqkv, aqk, zs, avs, o_xs, o_gqkv, o_aqk, o_zs, o_avs, glog_d):
    S = kb.S
    w_a, _ = kb.bf("win", 8 * IN_COLS)
    w = w_a.rearrange("p (k c) -> p k c", k=8)
    lnrep, o_ln = kb.f32("ln1rep", D)
    S.dma("sp", lnrep, ln1[l:l + 1, :].partition_broadcast(128), [], [o_ln], o_ln)
    cw, o_cw = kb.f32("qkvcw", 48)
    S.dma("sp", cw, convq[l], [], [o_cw], o_cw)
    cwv = cw.rearrange("p (c j) -> p c j", j=4)
    pre_a, _ = kb.f32("qkvpre", 12 * 516)
    prev = pre_a.rearrange("p (c t) -> p c t", c=12)
    o_pre = [Obj("pre%d" % c) for c in range(12)]
    S.op("pool", lambda e: e.memset(pre_a, 0.0), [], o_pre)
    xts = [kb.f32("xt%d" % i, 4 * D) for i in range(2)]
    hb, o_hb = kb.bf("hb", D)
    junk, o_junk = kb.bf("junk", D)
    ss, o_ss = kb.f32("ss", 4)
    hTs = [kb.bf("hT%d" % i, 8 * 512) for i in range(2)]
    cvs = [kb.f32("cv%d" % i, 512) for i in range(3)]
    sqs = [kb.bf("sq%d" % i, 512) for i in range(2)]
    rns = [kb.f32("rn%d" % i, 512) for i in range(2)]
    stg = [kb.bf("stg%d" % i, 512) for i in range(4)]
    zst_a, o_zst = kb.bf("zst", 4 * 512)
    zst = zst_a.rearrange("p (j c) -> p j c", j=4)
    avst_a, o_avst = kb.bf("avst", 4 * 512)
    avst = avst_a.rearrange("p (j c) -> p j c", j=4)
    glv = kb.glog.rearrange("p (t c) -> p t c", c=8)
    pb = kb.pbank; po = kb.pobj
    nstg = 0
    npb = 0
    def load_x(T):
        xt_a, o_xt = xts[T % 2]
        src = x_src[T * 512:(T + 1) * 512, :].rearrange("(j p) d -> p j d", p=128)
        S.dma("sp", xt_a.rearrange("p (j d) -> p j d", j=4), src, o_xs[T * 4:(T + 1) * 4], [o_xt], o_xt)

    load_x(0)
    wblocks = [(0, 512), (512, 1024), (1024, 1536), (2056, 2568), (2568, 3080), (1536, 2048), (3080, 3592), (2048, 2056)]
    wobjs = load_weight_blocks(kb, w, w_in[l], 8, wblocks, make_stages(kb))
    o_wcol = {}
    for (c0, c1), o in zip(wblocks, wobjs):
        for c in range(c0, c1, 8):
            o_wcol[c] = o

    def o_wc(c0):
        return o_wcol[c0 - (c0 % 8)]
    for T in range(8):
        tiles = o_xs[T * 4:(T + 1) * 4]
        xt_a, o_xt = xts[T % 2]
        xt = xt_a.rearrange("p (j d) -> p j d", j=4)
        hT_a, o_hT = hTs[T % 2]
        hT = hT_a.rearrange("p (k t) -> p k t", k=8)
        if T + 1 < 8:
            load_x(T + 1)
        for j in range(4):
            S.op("act", lambda e, j=j, xt=xt: e.activation(out=junk, in_=xt[:, j, :], func=AF.Square, accum_out=ss[:, j:j + 1]),
                 [o_xt], [o_junk, o_ss])
        rms_stats(kb, ss, o_ss, 4, 1.0 / D, RMS_EPS)
        for j in range(4):
            norm_transpose(kb, xt, o_xt, j, lnrep, o_ln, ss, o_ss, hb, o_hb, junk, o_junk, pb[6 + (j % 2)], po[6 + (j % 2)],
                           hT, o_hT, j * 128, "act" if j % 2 == 0 else "dve")
        tsl = slice(T * 512, (T + 1) * 512)
        def stage_a(c):
            pg, opg = pb[c % 4], po[c % 4]
            for k in range(8):
                S.op("pe", lambda e, k=k, c=c, pg=pg, hT=hT: e.matmul(pg[:, :], lhsT=w[:, k, c * 128:(c + 1) * 128], rhs=hT[:, k, :],
                                                                      start=(k == 0), stop=(k == 7)), [o_wc(c * 128), o_hT], [opg])
            opre = o_pre[c]
            S.op("act", lambda e, c=c, pg=pg: e.activation(out=prev[:, c, 3:515], in_=pg[:, :], func=AF.Copy), [opg], [opre])
            cv, o_cv = cvs[c % 3]
            S.op("act", lambda e, c=c, cv=cv, pg=pg: e.activation(out=cv, in_=pg[:, :], func=AF.Copy, scale=cwv[:, c, 3:4]), [opg, o_cw], [o_cv])
            for jt in (2, 1, 0):
                S.op("dve", lambda e, c=c, cv=cv, jt=jt: e.scalar_tensor_tensor(out=cv, in0=prev[:, c, jt:jt + 512], scalar=cwv[:, c, jt:jt + 1],
                                                                                in1=cv, op0=ALU.mult, op1=ALU.add), [opre, o_cw, o_cv], [o_cv])
            S.op("pool", lambda e, c=c: e.tensor_copy(out=prev[:, c, 0:3], in_=prev[:, c, 512:515]), [opre], [opre])

        def stage_b(c):
            cv, o_cv = cvs[c % 3]
            if c >= 8:
                st_t, o_st = stg[c % 4]
                S.op("act", lambda e, cv=cv, st_t=st_t: e.activation(out=st_t, in_=cv, func=AF.Silu), [o_cv], [o_st])
                S.dma("sp", gqkv[c][:, tsl], st_t, [o_st], [o_gqkv], o_st)
            else:
                sq, o_sq = sqs[c % 2]
                rn, o_rn = rns[c % 2]
                pn, opn = pb[4 + (c % 2)], po[4 + (c % 2)]
                S.op("act", lambda e, cv=cv: e.activation(out=cv, in_=cv, func=AF.Silu), [o_cv], [o_cv])
                S.op("pool", lambda e, cv=cv, sq=sq: e.tensor_tensor(out=sq, in0=cv, in1=cv, op=ALU.mult), [o_cv], [o_sq])
                S.op("pe", lambda e, sq=sq, pn=pn: e.matmul(pn[:, :], lhsT=kb.ones_bf, rhs=sq, start=True, stop=True), [kb.o_cstbf, o_sq], [opn])

        def stage_c(c):
            if c >= 8:
                return
            cv, o_cv = cvs[c % 3]
            rn, o_rn = rns[c % 2]
            st_t, o_st = stg[c % 4]
            pn, opn = pb[4 + (c % 2)], po[4 + (c % 2)]
            S.op("act", lambda e, rn=rn, pn=pn: e.activation(out=rn, in_=pn[:, :], func=AF.Sqrt, bias=L2_EPS), [opn], [o_rn])
            S.op("dve", lambda e, rn=rn: e.reciprocal(out=rn, in_=rn), [o_rn], [o_rn])
            sc = (128 ** -0.5) if c < 4 else 1.0
            S.op("dve", lambda e, cv=cv, rn=rn, st_t=st_t, sc=sc: e.scalar_tensor_tensor(out=st_t, in0=cv, scalar=sc, in1=rn,
                                                                                         op0=ALU.mult, op1=ALU.mult), [o_cv, o_rn], [o_st])
            S.dma("sp", gqkv[c][:, tsl], st_t, [o_st], [o_gqkv], o_st)

        for s_ in range(12 + 2):
            if s_ < 12:
                stage_a(s_)
            if 0 <= s_ - 1 < 12:
                stage_b(s_ - 1)
            if 0 <= s_ - 2 < 12:
                stage_c(s_ - 2)
        npb = 0
        nstg = 0
        for c in range(8):
            pg, opg = pb[npb % 4], po[npb % 4]; npb += 1
            col = 2056 + c * 128
            for k in range(8):
                S.op("pe", lambda e, k=k, col=col, pg=pg, hT=hT: e.matmul(pg[:, :], lhsT=w[:, k, col:col + 128], rhs=hT[:, k, :],
                                                                   start=(k == 0), stop=(k == 7)), [o_wc(col), o_hT], [opg])
            st_t, o_st = stg[nstg % 4]; nstg += 1
            sc = 0.125 if c < 4 else 1.0
            if c % 2 == 0:
                S.op("act", lambda e, pg=pg, st_t=st_t, sc=sc: e.activation(out=st_t, in_=pg[:, :], func=AF.Copy, scale=sc), [opg], [o_st])
            else:
                S.op("dve", lambda e, pg=pg, st_t=st_t, sc=sc: e.tensor_scalar(out=st_t, in0=pg[:, :], scalar1=sc, scalar2=None, op0=ALU.mult),
                     [opg], [o_st])
            S.dma("sp", aqk[c][:, tsl], st_t, [o_st], [o_aqk], o_st)
        for j in range(4):
            for which, col, dst_t, o_dst in ((0, 1536, zst, o_zst), (1, 3080, avst, o_avst)):
                pg, opg = pb[npb % 4], po[npb % 4]; npb += 1
                for k in range(8):
                    S.op("pe", lambda e, k=k, j=j, col=col, pg=pg, hT=hT: e.matmul(pg[:, :], lhsT=hT[:, k, j * 128:(j + 1) * 128], rhs=w[:, k, col:col + 512],
                                                                          start=(k == 0), stop=(k == 7)), [o_wc(col), o_hT], [opg])
                if which == 0:
                    S.op("act", lambda e, pg=pg, dst_t=dst_t, j=j: e.activation(out=dst_t[:, j, :], in_=pg[:, :], func=AF.Copy), [opg], [o_dst])
                else:
                    S.op("dve", lambda e, pg=pg, dst_t=dst_t, j=j: e.tensor_copy(out=dst_t[:, j, :], in_=pg[:, :]), [opg], [o_dst])
            pl, opl = pb[4 + (j % 2)], po[4 + (j % 2)]
            for k in range(8):
                S.op("pe", lambda e, k=k, j=j, pl=pl, hT=hT: e.matmul(pl[:, 0:8], lhsT=hT[:, k, j * 128:(j + 1) * 128], rhs=w[:, k, 2048:2056],
                                                               start=(k == 0), stop=(k == 7)), [o_wc(2048), o_hT], [opl])
            S.op("dve", lambda e, pl=pl, j=j, T=T: e.tensor_copy(out=glv[:, T * 4 + j, :], in_=pl[:, 0:8]), [opl], [kb.o_glog])
        S.dma("sp", zs[tsl, :].rearrange("(j p) c -> p j c", p=128), zst, [o_zst], [o_zs], o_zst)
        S.dma("sp", avs[tsl, :].rearrange("(j p) c -> p j c", p=128), avst, [o_avst], [o_avs], o_avst)
    if kb.cfg.get("dbg"):
        S.dma("sp", glog_d.rearrange("(t p) c -> p t c", p=128), glv, [kb.o_glog], [], kb.o_glog)
        kb.final_objs.append(kb.o_glog)


def phase4a(kb, l, x_src, xs, o_xs, w_out, yv, o_y):
    S = kb.S
    wo, o_wo = load_weight(kb, "wo", w_out[l], 8, D, make_stages(kb))
    xts = [kb.f32("xa%d" % i, D) for i in range(2)]
    yTs = [kb.bf("yT%d" % i, 8 * 128) for i in range(2)]
    pb = kb.pbank; po = kb.pobj
    S.dma("sp", xts[0][0], x_src[0:128, :], [o_xs[0]], [xts[0][1]], xts[0][1])
    for t in range(NT):
        xt, o_xt = xts[t % 2]
        yT_a, o_yT = yTs[t % 2]
        yT = yT_a.rearrange("p (k t) -> p k t", k=8)
        if t + 1 < NT:
            xn, o_xn = xts[(t + 1) % 2]
            S.dma("sp", xn, x_src[(t + 1) * 128:(t + 2) * 128, :], [o_xs[t + 1]], [o_xn], o_xn)
        pT, o_pT = pb[6 + (t % 2)], po[6 + (t % 2)]
        pTb = pT.bitcast(BF16)
        for k in range(8):
            S.op("pe", lambda e, k=k, t=t, pTb=pTb: e.transpose(pTb[:, k * 128:(k + 1) * 128], yv[:, t, k * 128:(k + 1) * 128], kb.ident_bf),
                 [o_y[t], kb.o_cstbf], [o_pT])
        if t % 2 == 0:
            S.op("act", lambda e, pTb=pTb, yT_a=yT_a: e.activation(out=yT_a, in_=pTb[:, :], func=AF.Copy), [o_pT], [o_yT])
        else:
            S.op("dve", lambda e, pTb=pTb, yT_a=yT_a: e.tensor_copy(out=yT_a, in_=pTb[:, :]), [o_pT], [o_yT])
        for nh in range(2):
            pi = (t * 2 + nh) % 6
            pd, opd = pb[pi], po[pi]
            for k in range(8):
                S.op("pe", lambda e, k=k, nh=nh, pd=pd, yT=yT: e.matmul(pd[:, :], lhsT=yT[:, k, :], rhs=wo[:, k, nh * 512:(nh + 1) * 512],
                                                                      start=(k == 0), stop=(k == 7)), [o_yT, o_wo], [opd])
            S.op("dve", lambda e, nh=nh, pd=pd, xt=xt: e.tensor_tensor(out=xt[:, nh * 512:(nh + 1) * 512], in0=xt[:, nh * 512:(nh + 1) * 512],
                                                                     in1=pd[:, :], op=ALU.add), [o_xt, opd], [o_xt])
        S.dma("sp", xs[t * 128:(t + 1) * 128, :], xt, [o_xt], [o_xs[t]], o_xt)
        if o_xt not in kb.final_objs:
            kb.final_objs.append(o_xt)


def phase3(*a, **k):
    for _ in phase3_gen(*a, **k):
        pass


def phase3_gen(kb, l, aqk, avs, o_aqk, o_avs, bm_d, kaug_d, qaug_d, yv, o_y, ps_banks=(0, 1, 2, 3, 4), pacc_banks=(5, 6, 7), look=2):
    S = kb.S
    bm, o_bm = kb.bf("bm", 17 * 128)
    S.dma("sp", bm, bm_d, [], [o_bm], o_bm)
    KTs = [kb.bf("KT%d" % i, S_LEN) for i in range(2)]
    QTs = [kb.bf("QT%d" % i, S_LEN) for i in range(2)]
    VAs = [kb.bf("VA%d" % i, NT * 66) for i in range(2)]
    PTs = [kb.bf("PT%d" % i, 512) for i in range(3)]
    rd, o_rd = kb.f32("rden", 2)
    for i in range(2):
        S.op("pool", lambda e, i=i: e.memset(KTs[i][0][64:128, :], 0.0), [], [KTs[i][1]])
        S.op("pool", lambda e, i=i: e.memset(QTs[i][0][64:128, :], 0.0), [], [QTs[i][1]])
        S.op("pool", lambda e, i=i: e.memset(VAs[i][0], 1.0), [], [VAs[i][1]])
    pb = kb.pbank; po = kb.pobj
    lim = kb.cfg.get('lim', False)
    PTs = PTs + [kb.bf("PT%d" % i, 512) for i in range(3, 5)]
    groups = []
    for h in range(2 if lim else 8):
        for qb in (list(range(3)) + [20] if lim else range(NT)):
            nkb = min(qb, 16) + 1
            for o0 in range(0, nkb, 4):
                groups.append((h, qb, o0, min(4, nkb - o0), nkb))
    loaded = set()
    state = {}

    def load_head(h):
        KT, o_KT = KTs[h % 2]
        QT, o_QT = QTs[h % 2]
        VA_a, o_VA = VAs[h % 2]
        VA = VA_a.rearrange("p (t c) -> p t c", c=66)
        r0 = (h % 2) * 64
        S.dma("sp", KT[0:64, :], aqk[4 + h // 2][r0:r0 + 64, :], [o_aqk], [o_KT], o_KT)
        S.dma("sp", QT[0:64, :], aqk[h // 2][r0:r0 + 64, :], [o_aqk], [o_QT], o_QT)
        S.dma("sp", KT[64:68, :], kaug_d[h], [], [o_KT], o_KT)
        S.dma("sp", QT[64:68, :], qaug_d[h], [], [o_QT], o_QT)
        S.dma("sp", VA[:, :, 0:64], avs[:, h * 64:(h + 1) * 64].rearrange("(t p) c -> p t c", p=128), [o_avs], [o_VA], o_VA)

    def emit_qk(gi):
        h, qb, o0, n, nkb = groups[gi]
        if h not in loaded:
            loaded.add(h)
            load_head(h)
        KT, o_KT = KTs[h % 2]
        QT, o_QT = QTs[h % 2]
        ps, ops_ = pb[ps_banks[gi % len(ps_banks)]], po[ps_banks[gi % len(ps_banks)]]
        PT, o_PT = PTs[gi % 5]
        S.op("pe", lambda e, ps=ps, o0=o0, n=n: e.matmul(ps[:, 0:n * 128], lhsT=kb.ident_bf, rhs=bm[:, o0 * 128:(o0 + n) * 128],
                                                         start=True, stop=False), [kb.o_cstbf, o_bm], [ops_])
        for o in range(o0, o0 + n):
            kbk = qb - o
            S.op("pe", lambda e, ps=ps, o=o, o0=o0, n=n, kbk=kbk, qb=qb, KT=KT, QT=QT: e.matmul(
                ps[:, (o - o0) * 128:(o - o0 + 1) * 128], lhsT=KT[:, kbk * 128:(kbk + 1) * 128], rhs=QT[:, qb * 128:(qb + 1) * 128],
                start=False, stop=(o == o0 + n - 1)), [o_KT, o_QT], [ops_])
        S.op("act", lambda e, ps=ps, PT=PT, n=n: e.activation(out=PT[:, 0:n * 128], in_=ps[:, 0:n * 128], func=AF.Exp), [ops_], [o_PT])

    def emit_pv(gi):
        h, qb, o0, n, nkb = groups[gi]
        VA_a, o_VA = VAs[h % 2]
        VA = VA_a.rearrange("p (t c) -> p t c", c=66)
        PT, o_PT = PTs[gi % 5]
        if o0 == 0:
            state["npo"] = state.get("npo", 0) + 1
        pi = pacc_banks[state["npo"] % len(pacc_banks)]
        pacc, opacc = pb[pi], po[pi]
        for o in range(o0, o0 + n):
            kbk = qb - o
            S.op("pe", lambda e, pacc=pacc, PT=PT, o=o, o0=o0, kbk=kbk, VA=VA, nkb=nkb: e.matmul(
                pacc[:, 0:65], lhsT=PT[:, (o - o0) * 128:(o - o0 + 1) * 128], rhs=VA[:, kbk, 0:65],
                start=(o == 0), stop=(o == nkb - 1)), [o_PT, o_VA], [opacc])
        if o0 + n == nkb:
            rdc = rd[:, (qb % 2):(qb % 2) + 1]
            S.op("dve", lambda e, pacc=pacc, rdc=rdc: e.reciprocal(out=rdc, in_=pacc[:, 64:65]), [opacc], [o_rd])
            S.op("dve", lambda e, pacc=pacc, rdc=rdc, qb=qb, h=h: e.tensor_scalar(out=yv[:, qb, 512 + h * 64:512 + (h + 1) * 64], in0=pacc[:, 0:64],
                                                                                  scalar1=rdc, scalar2=None, op0=ALU.mult), [opacc, o_rd], [o_y[qb]])

    LOOK = look
    G = len(groups)
    for gi in range(G + LOOK):
        if gi < G:
            emit_qk(gi)
        if gi - LOOK >= 0:
            emit_pv(gi - LOOK)
        yield


class SlotPool:
    def __init__(self, kb, banks=range(8)):
        self.banks = [(kb.pbank[b], kb.pobj[b]) for b in banks]
        self.n = 0

    def bank(self):
        b = self.banks[self.n % len(self.banks)]
        self.n += 1
        return b


def _slot(bank, h):
    return bank[0][:, h * 128:(h + 1) * 128], bank[1]


def phase2(*a, **k):
    for _ in phase2_gen(*a, **k):
        pass


def phase2_gen(kb, l, gqkv, zs, o_gqkv, o_zs, alog, dtb, gdnn, yv, o_y, banks=range(8)):
    from itertools import zip_longest
    S = kb.S
    C32 = kb.C32
    o_c32 = kb.o_cst32
    ident32 = C32["ident"]; ones32 = C32["ones"]; mbcT = C32["mbcT"]; msT = C32["msT"]; lmT = C32["lmT"]
    sel = [C32["sel0"], C32["sel1"]]
    ident_bf = kb.ident_bf; o_cbf = kb.o_cstbf
    sp = SlotPool(kb, banks)
    lim = kb.cfg.get("lim", False)
    import os
    ntile = int(os.environ.get('P2NT', '3')) if lim else NT

    def t128(name):
        return kb.f32(name, 128)

    dtbr, o_dtbr = t128("dtbr"); algr, o_algr = t128("algr"); gnrep, o_gn = t128("gnrep")
    S.dma("sp", dtbr, dtb[l].partition_broadcast(128), [], [o_dtbr], o_dtbr)
    S.dma("sp", algr, alog[l].partition_broadcast(128), [], [o_algr], o_algr)
    S.dma("sp", gnrep, gdnn[l:l + 1, :].partition_broadcast(128), [], [o_gn], o_gn)
    glv = kb.glog.rearrange("p (t c) -> p t c", c=8)
    o_gl = kb.o_glog
    g, o_g = t128("g"); bet, o_bet = t128("bet"); nbet, o_nbet = t128("nbet")
    if 'p1' not in kb.cur_phases:
        S.op("pool", lambda e: e.memset(kb.glog, 0.1), [], [o_gl])
    gc, o_gc = t128("gc"); ngc, o_ngc = t128("ngc"); egc, o_egc = t128("egc"); negc, o_negc = t128("negc")
    edl, o_edl = t128("edl"); tmp, o_tmp = t128("tmp")
    dlr = [t128("dlr0"), t128("dlr1")]
    v3 = lambda ap: ap.rearrange("p (t h) -> p t h", h=4)
    S.op("dve", lambda e: e.tensor_tensor(out=v3(tmp), in0=glv[:, :, 4:8], in1=v3(dtbr), op=ALU.add), [o_gl, o_dtbr], [o_tmp])
    S.op("act", lambda e: e.activation(out=tmp, in_=tmp, func=AF.Exp), [o_tmp], [o_tmp])
    S.op("dve", lambda e: e.tensor_scalar(out=tmp, in0=tmp, scalar1=1.0, scalar2=None, op0=ALU.add), [o_tmp], [o_tmp])
    S.op("act", lambda e: e.activation(out=tmp, in_=tmp, func=AF.Ln), [o_tmp], [o_tmp])
    S.op("act", lambda e: e.activation(out=algr, in_=algr, func=AF.Exp), [o_algr], [o_algr])
    S.op("dve", lambda e: e.scalar_tensor_tensor(out=g, in0=tmp, scalar=-1.0, in1=algr, op0=ALU.mult, op1=ALU.mult), [o_tmp, o_algr], [o_g])
    S.op("act", lambda e: e.activation(out=v3(bet), in_=glv[:, :, 0:4], func=AF.Sigmoid), [o_gl], [o_bet])
    S.op("dve", lambda e: e.tensor_scalar(out=nbet, in0=bet, scalar1=-1.0, scalar2=None, op0=ALU.mult), [o_bet], [o_nbet])
    pgc, opgc = _slot(sp.bank(), 0)
    S.op("pe", lambda e: e.matmul(pgc, lhsT=lmT, rhs=g, start=True, stop=True), [o_c32, o_g], [opgc])
    S.op("dve", lambda e: e.tensor_copy(out=gc, in_=pgc), [opgc], [o_gc])
    S.op("dve", lambda e: e.tensor_scalar(out=ngc, in0=gc, scalar1=-1.0, scalar2=None, op0=ALU.mult), [o_gc], [o_ngc])
    S.op("act", lambda e: e.activation(out=egc, in_=gc, func=AF.Exp), [o_gc], [o_egc])
    S.op("dve", lambda e: e.tensor_scalar(out=negc, in0=egc, scalar1=-1.0, scalar2=None, op0=ALU.mult), [o_egc], [o_negc])
    for c in range(2):
        pd, opd = _slot(sp.bank(), 0)
        dl_t, o_dl = dlr[c]
        rc = slice(c * 64, c * 64 + 64)
        S.op("pe", lambda e, pd=pd, c=c: e.matmul(pd, lhsT=sel[c], rhs=g, start=True, stop=True), [o_c32, o_g], [opd])
        S.op("dve", lambda e, pd=pd, rc=rc: e.tensor_tensor(out=edl[rc, :], in0=pd[rc, :], in1=gc[rc, :], op=ALU.subtract), [opd, o_gc], [o_edl])
        S.op("act", lambda e, pd=pd, dl_t=dl_t: e.activation(out=dl_t, in_=pd, func=AF.Exp), [opd], [o_dl])
    S.op("act", lambda e: e.activation(out=edl, in_=edl, func=AF.Exp), [o_edl], [o_edl])

    def bf128(name):
        return kb.bf(name, 128)
    ld = [[kb.bf("ld%d_%d" % (par, c), 512) for c in range(12)] for par in range(2)]
    zt = [kb.bf("zt%d" % i, 512) for i in range(3)]
    PB = [[{nm: [bf128("%s%d%d_%d" % (nm, par, h, i)) for i in range(2)] for nm in ("B", "Bt", "Q")} for h in range(4)] for par in range(3)]
    PX = [[{nm: bf128("%s%d%d" % (nm, par, h)) for nm in ("aqkT", "kdec", "vtok")} for h in range(4)] for par in range(3)]
    W32x = [[{nm: t128("%s%d_%d" % (nm, h, i)) for nm in ("dg", "dgn", "E3", "E3s")} for h in range(4)] for i in range(2)]
    Sst = [t128("S%d" % h) for h in range(4)]
    Sbf = [bf128("Sbf%d" % h) for h in range(4)]
    rp = [bf128("rp%d" % h) for h in range(4)]
    vnew = [bf128("vnew%d" % h) for h in range(4)]
    qs = [t128("qs%d" % h) for h in range(4)]
    osb = [t128("osb%d" % h) for h in range(4)]
    szb = [t128("sz%d" % h) for h in range(4)]
    t1b = [t128("t1%d" % h) for h in range(4)]
    junk, o_junk = t128("junk2")
    ssn = [kb.f32("ssn%d" % h, 2) for h in range(4)]
    for h in range(4):
        S.op("pool", lambda e, h=h: e.memset(Sst[h][0], 0.0), [], [Sst[h][1]])
        S.op("pool", lambda e, h=h: e.memset(Sbf[h][0], 0.0), [], [Sbf[h][1]])

    def loads(t):
        if t % 4 == 0:
            par = (t // 4) % 2
            for c in range(12):
                buf, o_b = ld[par][c]
                S.dma("sp", buf, gqkv[c][:, t * 128:t * 128 + 512], [o_gqkv], [o_b], o_b)
        zb, o_zb = zt[t % 3]
        S.dma("sp", zb, zs[t * 128:(t + 1) * 128, :], [o_zs], [o_zb], o_zb)

    def opnd(t, h):
        par = (t // 4) % 2
        off = (t % 4) * 128
        q = (ld[par][h][0][:, off:off + 128], ld[par][h][1])
        k = (ld[par][4 + h][0][:, off:off + 128], ld[par][4 + h][1])
        v = (ld[par][8 + h][0][:, off:off + 128], ld[par][8 + h][1])
        return q, k, v

    def pre_gen(t):
        par = t % 3
        W32 = W32x[t % 2]
        loads(t)
        hs = range(4)
        pkk = {}; pkq = {}
        for h in hs:
            (qT, o_q), (kT, o_k), (vT, o_v) = opnd(t, h)
            n = t * 4 + h
            dg, o_dg = W32[h]["dg"]; dgn, o_dgn = W32[h]["dgn"]
            S.op("dve", lambda e, dg=dg, n=n: e.tensor_scalar(out=dg, in0=ident32, scalar1=gc[:, n:n + 1], scalar2=None, op0=ALU.mult),
                 [o_c32, o_gc], [o_dg])
            S.op("act", lambda e, dgn=dgn, n=n: e.activation(out=dgn, in_=ident32, func=AF.Copy, scale=ngc[:, n:n + 1]),
                 [o_c32, o_ngc], [o_dgn])
        bt1 = sp.bank(); bt2 = sp.bank()
        for h in hs:
            n = t * 4 + h
            (qT, o_q), (kT, o_k), (vT, o_v) = opnd(t, h)
            kdec, o_kdec = PX[par][h]["kdec"]; vtok, o_vtok = PX[par][h]["vtok"]
            p, op_ = _slot(bt1, h); pb16 = p.bitcast(BF16)[:, 0:128]
            S.op("pe", lambda e, pb16=pb16, kT=kT: e.transpose(pb16, kT, ident_bf), [o_k, o_cbf], [op_])
            S.op("act", lambda e, pb16=pb16, kdec=kdec, n=n: e.activation(out=kdec, in_=pb16, func=AF.Copy, scale=edl[:, n:n + 1]), [op_, o_edl], [o_kdec])
            p2, op2 = _slot(bt2, h); p2b = p2.bitcast(BF16)[:, 0:128]
            S.op("pe", lambda e, p2b=p2b, vT=vT: e.transpose(p2b, vT, ident_bf), [o_v, o_cbf], [op2])
            S.op("dve", lambda e, p2b=p2b, vtok=vtok: e.tensor_copy(out=vtok, in_=p2b), [op2], [o_vtok])
        yield
        pdl = {}
        bdl = sp.bank()
        for h in hs:
            dg, o_dg = W32[h]["dg"]; dgn, o_dgn = W32[h]["dgn"]
            pdl[h] = _slot(bdl, h)
            p, op_ = pdl[h]
            S.op("pe", lambda e, p=p, dg=dg: e.matmul(p, lhsT=ones32, rhs=dg, start=True, stop=False), [o_c32, o_dg], [op_])
            S.op("pe", lambda e, p=p, dgn=dgn: e.matmul(p, lhsT=dgn, rhs=ones32, start=False, stop=False), [o_c32, o_dgn], [op_])
            S.op("pe", lambda e, p=p: e.matmul(p, lhsT=ident32, rhs=mbcT, start=False, stop=True), [o_c32], [op_])
            E3, o_E3 = W32[h]["E3"]
            S.op("act", lambda e, p=p, E3=E3: e.activation(out=E3, in_=p, func=AF.Exp), [op_], [o_E3])
        yield
        bkk = sp.bank(); bkq = sp.bank()
        for h in hs:
            (qT, o_q), (kT, o_k), (vT, o_v) = opnd(t, h)
            pkk[h] = _slot(bkk, h); pkq[h] = _slot(bkq, h)
            S.op("pe", lambda e, kT=kT, p=pkk[h][0]: e.matmul(p, lhsT=kT, rhs=kT, start=True, stop=True), [o_k], [pkk[h][1]])
            S.op("pe", lambda e, kT=kT, qT=qT, p=pkq[h][0]: e.matmul(p, lhsT=kT, rhs=qT, start=True, stop=True), [o_k, o_q], [pkq[h][1]])
        for h in hs:
            n = t * 4 + h
            E3, o_E3 = W32[h]["E3"]; E3s, o_E3s = W32[h]["E3s"]
            aqkT, o_aqkT = PX[par][h]["aqkT"]
            S.op("dve", lambda e, p=pkq[h][0], E3=E3, aqkT=aqkT: e.tensor_tensor(out=aqkT, in0=p, in1=E3, op=ALU.mult), [pkq[h][1], o_E3], [o_aqkT])
            S.op("pool", lambda e, E3=E3, E3s=E3s: e.tensor_tensor(out=E3s, in0=E3, in1=msT, op=ALU.mult), [o_E3, o_c32], [o_E3s])
            Bt0, o_Bt0 = PB[par][h]["Bt"][0]
            S.op("dve", lambda e, p=pkk[h][0], E3s=E3s, Bt0=Bt0, n=n: e.scalar_tensor_tensor(out=Bt0, in0=p, scalar=nbet[:, n:n + 1], in1=E3s,
                                                                                            op0=ALU.mult, op1=ALU.mult), [pkk[h][1], o_nbet, o_E3s], [o_Bt0])
        yield
        btr = sp.bank()
        for h in hs:
            Bt0, o_Bt0 = PB[par][h]["Bt"][0]
            B0, o_B0 = PB[par][h]["B"][0]
            Q0, o_Q0 = PB[par][h]["Q"][0]
            p, op_ = _slot(btr, h)
            pb16 = p.bitcast(BF16)[:, 0:128]
            S.op("pe", lambda e, pb16=pb16, Bt0=Bt0: e.transpose(pb16, Bt0, ident_bf), [o_Bt0, o_cbf], [op_])
            S.op("act", lambda e, pb16=pb16, B0=B0: e.activation(out=B0, in_=pb16, func=AF.Copy), [op_], [o_B0])
            S.op("pool", lambda e, Bt0=Bt0, Q0=Q0: e.tensor_tensor(out=Q0, in0=Bt0, in1=ident_bf, op=ALU.add), [o_Bt0, o_cbf], [o_Q0])
        yield
        bB = sp.bank(); bBt = sp.bank()
        for h in hs:
            B0, o_B0 = PB[par][h]["B"][0]
            Bt0, o_Bt0 = PB[par][h]["Bt"][0]
            p1, op1 = _slot(bB, h); p2, op2 = _slot(bBt, h)
            S.op("pe", lambda e, p=p1, Bt0=Bt0, B0=B0: e.matmul(p, lhsT=Bt0, rhs=B0, start=True, stop=True), [o_Bt0, o_B0], [op1])
            S.op("pe", lambda e, p=p2, Bt0=Bt0, B0=B0: e.matmul(p, lhsT=B0, rhs=Bt0, start=True, stop=True), [o_Bt0, o_B0], [op2])
        for h in hs:
            B1, o_B1 = PB[par][h]["B"][1]
            Bt1, o_Bt1 = PB[par][h]["Bt"][1]
            p1, op1 = _slot(bB, h); p2, op2 = _slot(bBt, h)
            S.op("act", lambda e, p=p1, B1=B1: e.activation(out=B1, in_=p, func=AF.Copy), [op1], [o_B1])
            S.op("dve", lambda e, p=p2, Bt1=Bt1: e.tensor_copy(out=Bt1, in_=p), [op2], [o_Bt1])
        yield
        for it in range(1, 6):
            bQ = sp.bank()
            bB = sp.bank() if it <= 4 else None
            bBt = sp.bank() if it <= 3 else None
            for h in hs:
                Bk, o_Bk = PB[par][h]["B"][it % 2]
                Btk, o_Btk = PB[par][h]["Bt"][it % 2]
                Qp, o_Qp = PB[par][h]["Q"][(it - 1) % 2]
                p, op_ = _slot(bQ, h)
                S.op("pe", lambda e, p=p, Bk=Bk, Qp=Qp: e.matmul(p, lhsT=Bk, rhs=Qp, start=True, stop=True), [o_Bk, o_Qp], [op_])
                if it <= 4:
                    p1, op1 = _slot(bB, h)
                    S.op("pe", lambda e, p=p1, Btk=Btk, Bk=Bk: e.matmul(p, lhsT=Btk, rhs=Bk, start=True, stop=True), [o_Btk, o_Bk], [op1])
                if it <= 3:
                    p2, op2 = _slot(bBt, h)
                    S.op("pe", lambda e, p=p2, Btk=Btk, Bk=Bk: e.matmul(p, lhsT=Bk, rhs=Btk, start=True, stop=True), [o_Btk, o_Bk], [op2])
            for h in hs:
                Qp, o_Qp = PB[par][h]["Q"][(it - 1) % 2]
                Qn, o_Qn = PB[par][h]["Q"][it % 2]
                p, op_ = _slot(bQ, h)
                S.op("dve", lambda e, p=p, Qp=Qp, Qn=Qn: e.tensor_tensor(out=Qn, in0=Qp, in1=p, op=ALU.add), [op_, o_Qp], [o_Qn])
                if it <= 4:
                    Bn, o_Bn = PB[par][h]["B"][(it + 1) % 2]
                    p1, op1 = _slot(bB, h)
                    if it % 2 == 0:
                        S.op("dve", lambda e, p=p1, Bn=Bn: e.tensor_copy(out=Bn, in_=p), [op1], [o_Bn])
                    else:
                        S.op("act", lambda e, p=p1, Bn=Bn: e.activation(out=Bn, in_=p, func=AF.Copy), [op1], [o_Bn])
                if it <= 3:
                    Btn, o_Btn = PB[par][h]["Bt"][(it + 1) % 2]
                    p2, op2 = _slot(bBt, h)
                    S.op("dve", lambda e, p=p2, Btn=Btn: e.tensor_copy(out=Btn, in_=p), [op2], [o_Btn])
            yield

    def rec_gen(t):
        par = t % 3
        hs = range(4)
        zb, o_zb = zt[t % 3]
        for c in range(2):
            rc = slice(c * 64, c * 64 + 64)
            pk = {}; pq = {}
            bpk = sp.bank(); bpq = sp.bank()
            for h in hs:
                (qT, o_q), (kT, o_k), (vT, o_v) = opnd(t, h)
                pk[h] = _slot(bpk, h); pq[h] = _slot(bpq, h)
                S.op("pe", lambda e, p=pk[h][0], kT=kT, h=h: e.matmul(p, lhsT=kT, rhs=Sbf[h][0], start=True, stop=True), [o_k, Sbf[h][1]], [pk[h][1]])
                S.op("pe", lambda e, p=pq[h][0], qT=qT, h=h: e.matmul(p, lhsT=qT, rhs=Sbf[h][0], start=True, stop=True), [o_q, Sbf[h][1]], [pq[h][1]])
            for h in hs:
                n = t * 4 + h
                vtok, o_vtok = PX[par][h]["vtok"]
                S.op("dve", lambda e, p=pk[h][0], h=h, n=n, vtok=vtok, rc=rc: e.scalar_tensor_tensor(out=rp[h][0][rc, :], in0=p[rc, :], scalar=negc[rc, n:n + 1],
                                                                                             in1=vtok[rc, :], op0=ALU.mult, op1=ALU.add),
                     [pk[h][1], o_negc, o_vtok], [rp[h][1]])
                S.op("dve", lambda e, p=pq[h][0], h=h, n=n, rc=rc: e.tensor_scalar(out=qs[h][0][rc, :], in0=p[rc, :], scalar1=egc[rc, n:n + 1], scalar2=None, op0=ALU.mult),
                     [pq[h][1], o_egc], [qs[h][1]])
            yield
            pv = {}
            bpv = sp.bank()
            for h in hs:
                Q5, o_Q5 = PB[par][h]["Q"][1]
                pv[h] = _slot(bpv, h)
                S.op("pe", lambda e, p=pv[h][0], Q5=Q5, h=h, rc=rc: e.matmul(p, lhsT=Q5[rc, :], rhs=rp[h][0][rc, :], start=True, stop=True), [o_Q5, rp[h][1]], [pv[h][1]])
            for h in hs:
                n = t * 4 + h
                S.op("act", lambda e, p=pv[h][0], h=h, n=n, rc=rc: e.activation(out=vnew[h][0][rc, :], in_=p[rc, :], func=AF.Copy, scale=bet[rc, n:n + 1]),
                     [pv[h][1], o_bet], [vnew[h][1]])
            yield
            pS = {}; po_ = {}
            bpS = sp.bank(); bpo = sp.bank()
            for h in hs:
                kdec, o_kdec = PX[par][h]["kdec"]; aqkT, o_aqkT = PX[par][h]["aqkT"]
                pS[h] = _slot(bpS, h); po_[h] = _slot(bpo, h)
                S.op("pe", lambda e, p=pS[h][0], kdec=kdec, h=h, rc=rc: e.matmul(p, lhsT=kdec[rc, :], rhs=vnew[h][0][rc, :], start=True, stop=True),
                     [o_kdec, vnew[h][1]], [pS[h][1]])
                S.op("pe", lambda e, p=po_[h][0], aqkT=aqkT, h=h, rc=rc: e.matmul(p, lhsT=aqkT[rc, :], rhs=vnew[h][0][rc, :], start=True, stop=True),
                     [o_aqkT, vnew[h][1]], [po_[h][1]])
            for h in hs:
                n = t * 4 + h
                dl_t, o_dl = dlr[c]
                S.op("dve", lambda e, p=pS[h][0], h=h, n=n, dl_t=dl_t: e.scalar_tensor_tensor(out=Sst[h][0], in0=Sst[h][0], scalar=dl_t[:, n:n + 1], in1=p,
                                                                                             op0=ALU.mult, op1=ALU.add), [pS[h][1], o_dl, Sst[h][1]], [Sst[h][1]])
                S.op("act", lambda e, h=h: e.activation(out=Sbf[h][0], in_=Sst[h][0], func=AF.Copy), [Sst[h][1]], [Sbf[h][1]])
                S.op("dve", lambda e, p=po_[h][0], h=h, rc=rc: e.tensor_tensor(out=osb[h][0][rc, :], in0=p[rc, :], in1=qs[h][0][rc, :], op=ALU.add),
                     [po_[h][1], qs[h][1]], [osb[h][1]])
            yield
        for h in hs:
            ss_t, o_ss = ssn[h]
            S.op("act", lambda e, h=h, ss_t=ss_t: e.activation(out=junk, in_=osb[h][0], func=AF.Square, accum_out=ss_t[:, 0:1]), [osb[h][1]], [o_junk, o_ss])
            S.op("act", lambda e, ss_t=ss_t: e.activation(out=ss_t[:, 0:1], in_=ss_t[:, 0:1], func=AF.Ln, scale=1.0 / 128, bias=RMS_EPS), [o_ss], [o_ss])
            S.op("act", lambda e, ss_t=ss_t: e.activation(out=ss_t[:, 0:1], in_=ss_t[:, 0:1], func=AF.Exp, scale=-0.5), [o_ss], [o_ss])
            S.op("act", lambda e, h=h: e.activation(out=szb[h][0], in_=zb[:, h * 128:(h + 1) * 128], func=AF.Exp, scale=-1.0), [o_zb], [szb[h][1]])
        yield
        for h in hs:
            S.op("act", lambda e, h=h: e.activation(out=szb[h][0], in_=szb[h][0], func=AF.Ln, bias=1.0), [szb[h][1]], [szb[h][1]])
            S.op("act", lambda e, h=h: e.activation(out=szb[h][0], in_=szb[h][0], func=AF.Exp, scale=-1.0), [szb[h][1]], [szb[h][1]])
        yield
        for h in hs:
            ss_t, o_ss = ssn[h]
            S.op("dve", lambda e, h=h, ss_t=ss_t: e.scalar_tensor_tensor(out=t1b[h][0], in0=osb[h][0], scalar=ss_t[:, 0:1], in1=gnrep, op0=ALU.mult, op1=ALU.mult),
                 [osb[h][1], o_ss, o_gn], [t1b[h][1]])
            S.op("pool", lambda e, h=h: e.tensor_tensor(out=t1b[h][0], in0=t1b[h][0], in1=zb[:, h * 128:(h + 1) * 128], op=ALU.mult), [t1b[h][1], o_zb], [t1b[h][1]])
            S.op("pool", lambda e, h=h: e.tensor_tensor(out=yv[:, t, h * 128:(h + 1) * 128], in0=t1b[h][0], in1=szb[h][0], op=ALU.mult),
                 [t1b[h][1], szb[h][1]], [o_y[t]])
        yield

    gens = {}

    def adv(tt):
        try:
            next(gens[tt])
        except StopIteration:
            del gens[tt]

    gens[0] = pre_gen(0)
    while 0 in gens:
        adv(0)
        yield
    if ntile > 1:
        gens[1] = pre_gen(1)
        for _ in range(5):
            adv(1)
            yield
    for t in range(ntile):
        if t + 2 < ntile:
            gens[t + 2] = pre_gen(t + 2)
        gr = rec_gen(t)
        rec_done = False
        step = 0
        while True:
            if not rec_done:
                try:
                    next(gr)
                except StopIteration:
                    rec_done = True
            order = (t + 1, t + 2) if step % 2 == 0 else (t + 2, t + 1)
            for tt in order:
                if tt in gens:
                    adv(tt)
                    break
            step += 1
            yield
            if rec_done and (t + 1) not in gens:
                break


def host_inputs(inputs, b):
    cst, bm, kaug, qaug = _host_consts()
    f = lambda a: np.ascontiguousarray(np.asarray(a, dtype=np.float32))
    m = {
        "x": f(inputs["x"][b]),
        "ln1": f(inputs["ln1"]), "ln2": f(inputs["ln2"]), "ln_f": f(inputs["ln_f"]).reshape(1, D),
        "w_in": f(inputs["w_in"]),
        "conv_qkv_r": f(np.asarray(inputs["conv_qkv"]).reshape(DEPTH, 4, 12, 128).transpose(0, 3, 2, 1).reshape(DEPTH, 128, 48)),
        "a_log_r": f(np.tile(np.asarray(inputs["a_log"]), (1, 32)).reshape(DEPTH, 1, 128)),
        "dt_bias_r": f(np.tile(np.asarray(inputs["dt_bias"]), (1, 32)).reshape(DEPTH, 1, 128)),
        "gdn_norm": f(inputs["gdn_norm"]),
        "w_out": f(inputs["w_out"]),
        "w_gate": f(inputs["w_gate"]), "w_up": f(inputs["w_up"]),
        "ffn_conv_r": f(np.asarray(inputs["ffn_conv"]).reshape(DEPTH, 3, NFC, 128).transpose(0, 3, 2, 1).reshape(DEPTH, 128, NFC * 3)),
        "w_down": f(inputs["w_down"]),
        "cst": cst, "cstb": cst[:, 0:256].astype(ml_dtypes.bfloat16), "bm": bm.astype(ml_dtypes.bfloat16),
        "kaug": kaug.astype(ml_dtypes.bfloat16), "qaug": qaug.astype(ml_dtypes.bfloat16),
    }
    return m


_NC_CACHE = {}


def kernel(**inputs):
    n = 8
    if "full" not in _NC_CACHE:
        _NC_CACHE["full"] = build(dict())
    in_maps = [host_inputs(inputs, b) for b in range(n)]
    res = run_bass_kernel_spmd(_NC_CACHE["full"], in_maps, core_ids=list(range(n)))
    return np.stack([r["out"] for r in res.results], axis=0).astype(np.float32)
```

```python
from contextlib import ExitStack
import numpy as np
import ml_dtypes
import concourse.bass as bass
import concourse.mybir as mybir
from concourse.bass_utils import run_bass_kernel_spmd

F32 = mybir.dt.float32
BF16 = mybir.dt.bfloat16
AF = mybir.ActivationFunctionType
ALU = mybir.AluOpType

S_LEN = 4096
D = 1024
DEPTH = 2
NT = 32
IN_COLS = 3592
DFF = 2816
NFC = 22
RMS_EPS = 1e-6
L2_EPS = 1e-6
NEG = -30000.0

ENGS = ("pe", "act", "dve", "pool", "sp")


class Slot:
    __slots__ = ("name", "kind", "sem", "count")

    def __init__(self, name, kind):
        self.name = name
        self.kind = kind
        self.sem = None
        self.count = 0


class Obj:
    __slots__ = ("name", "w_ev", "r_ev", "dq")

    def __init__(self, name):
        self.name = name
        self.w_ev = {}
        self.r_ev = {}
        self.dq = {}


class Op:
    __slots__ = ("eng", "idx", "fn", "waits", "need_inc", "count", "slot", "tag")

    def __init__(self, eng, idx, fn):
        self.eng = eng
        self.idx = idx
        self.fn = fn
        self.waits = []
        self.need_inc = False
        self.count = None
        self.slot = None


def _merge(dst, src):
    for k, v in src.items():
        if dst.get(k, -1) < v:
            dst[k] = v


class Sched:
    def __init__(self, nc):
        self.nc = nc
        self.ops = {e: [] for e in ENGS}
        self.seen = {e: {} for e in ENGS}
        self.slots = []
        self.free = {"hw": [], "sw": []}
        self.phase_slots = []
        self.gen = 0
        self.bar = {}
        self.bar_gen = 0
        self.bar_applied = {e: 0 for e in ENGS}

    def keep(self):
        self.phase_slots = []

    def barrier(self):
        ev = {}
        for e in ENGS:
            n = len(self.ops[e])
            for i in range(n - 1, -1, -1):
                if self.ops[e][i].slot is None:
                    ev[("e", e)] = i
                    break
        for sl in self.slots:
            ev[("d", sl)] = sl.count
        self.bar = ev
        self.bar_gen += 1
        for sl in self.phase_slots:
            self.free[sl.kind].append(sl)
        self.phase_slots = []
        self.gen += 1

    def _slot_for(self, obj, qk):
        ent = obj.dq.get(qk)
        if ent is not None and ent[1] == self.gen:
            return ent[0]
        if self.free[qk]:
            sl = self.free[qk].pop()
        else:
            sl = Slot("%s_%d" % (qk, len(self.slots)), qk)
            self.slots.append(sl)
        self.phase_slots.append(sl)
        obj.dq[qk] = (sl, self.gen)
        return sl

    def _record(self, eng, fn, reads, writes, dma_obj=None):
        lst = self.ops[eng]
        op = Op(eng, len(lst), fn)
        op.tag = getattr(self, 'tag', '')
        need = {}
        mykey = ("e", eng)
        for o in reads:
            _merge(need, o.w_ev)
        for o in writes:
            _merge(need, o.w_ev)
            _merge(need, o.r_ev)
        if self.bar_applied[eng] != self.bar_gen:
            self.bar_applied[eng] = self.bar_gen
            for k, v in self.bar.items():
                if k == mykey:
                    continue
                if need.get(k, -1) < v:
                    need[k] = v
        seen = self.seen[eng]
        for k, v in need.items():
            if k == mykey and dma_obj is None and eng == "pe":
                continue
            if seen.get(k, -1) >= v:
                continue
            seen[k] = v
            if k[0] == "e":
                prod = self.ops[k[1]][v]
                prod.need_inc = True
                op.waits.append(("e", prod))
            else:
                op.waits.append(("d", k[1], v))
        lst.append(op)
        if dma_obj is not None:
            sl = self._slot_for(dma_obj, "sw" if eng == "pool" else "hw")
            sl.count += 1
            op.slot = sl
            ev = {("d", sl): sl.count}
        else:
            ev = {mykey: op.idx}
        for o in reads:
            _merge(o.r_ev, ev)
        for o in writes:
            if o.r_ev:
                o.w_ev = dict(ev)
                o.r_ev = {}
            else:
                _merge(o.w_ev, ev)
        return op

    def op(self, eng, fn, reads=(), writes=()):
        return self._record(eng, fn, list(reads), list(writes))

    def dma(self, eng, out, in_, reads, writes, sb_obj, **kw):
        def fn(e, out=out, in_=in_, kw=kw):
            return e.dma_start(out=out, in_=in_, **kw)
        return self._record(eng, fn, list(reads), list(writes), dma_obj=sb_obj)

    def emit(self, stack, final_wait_objs=()):
        nc = self.nc
        esem = {}
        for e in ENGS:
            if e != "sp":
                esem[e] = stack.enter_context(nc.semaphore("s_" + e))
        for sl in self.slots:
            sl.sem = stack.enter_context(nc.semaphore("d_" + sl.name))
        for e in ENGS:
            c = 0
            for op in self.ops[e]:
                if op.slot is None and op.need_inc:
                    c += 1
                    op.count = c
        block = stack.enter_context(nc.Block())

        def run(engname, e):
            for op in self.ops[engname]:
                for w in op.waits:
                    if w[0] == "e":
                        e.wait_ge(esem[w[1].eng], w[1].count)
                    else:
                        e.wait_ge(w[1].sem, 16 * w[2])
                ins = op.fn(e)
                if op.slot is not None:
                    ins.then_inc(op.slot.sem, 16)
                elif op.need_inc:
                    ins.then_inc(esem[engname], 1)
            if engname == "sp":
                for sl in self.slots:
                    e.wait_ge(sl.sem, 16 * sl.count)

        @block.tensor
        def _(e):
            run("pe", e)

        @block.scalar
        def _(e):
            run("act", e)

        @block.vector
        def _(e):
            run("dve", e)

        @block.gpsimd
        def _(e):
            run("pool", e)

        @block.sync
        def _(e):
            run("sp", e)


CST_NAMES = ["ident", "ones", "mbcT", "msT", "lmT", "bones", "sel0", "sel1"]


def _host_consts():
    i = np.arange(128)
    same = (i[:, None] // 64) == (i[None, :] // 64)
    c = {}
    c["ident"] = np.eye(128, dtype=np.float32)
    c["ones"] = np.ones((128, 128), np.float32)
    c["mbcT"] = np.where(same & (i[:, None] <= i[None, :]), 0.0, NEG).astype(np.float32)
    c["msT"] = (same & (i[:, None] < i[None, :])).astype(np.float32)
    c["lmT"] = (same & (i[:, None] <= i[None, :])).astype(np.float32)
    c["bones"] = same.astype(np.float32)
    s0 = np.zeros((128, 128), np.float32); s0[0:64, :] = 1.0
    s1 = np.zeros((128, 128), np.float32); s1[64:128, :] = 1.0
    c["sel0"] = s0
    c["sel1"] = s1
    cst = np.concatenate([c[n] for n in CST_NAMES], axis=1)
    ki = np.arange(128)[:, None, None]
    o = np.arange(17)[None, :, None]
    qi = np.arange(128)[None, None, :]
    dl = o * 128 + qi - ki
    cnt = ((dl >= 0) & (dl <= 128)).astype(np.int64) + ((dl >= 0) & (dl % 4 == 0) & (dl <= 512)) \
        + ((dl >= 0) & (dl % 16 == 0) & (dl <= 2048))
    bm = np.where(cnt > 0, np.log(np.maximum(cnt, 1).astype(np.float64)), NEG).astype(np.float32)
    bm = bm.reshape(128, 17 * 128)
    t = np.arange(S_LEN)
    slopes = 2.0 ** (-8.0 * np.arange(1, 9) / 8)
    kaug = np.zeros((8, 4, S_LEN), np.float32)
    qaug = np.zeros((8, 4, S_LEN), np.float32)
    for h in range(8):
        kaug[h, 0] = slopes[h] * (t % 128)
        kaug[h, 1] = slopes[h] * 128 * (t // 128)
        kaug[h, 2] = 1.0
        kaug[h, 3] = 1.0
        qaug[h, 0] = 1.0
        qaug[h, 1] = 1.0
        qaug[h, 2] = -slopes[h] * (t % 128)
        qaug[h, 3] = -slopes[h] * 128 * (t // 128)
    return cst, bm, kaug, qaug


class KB:
    ARENA_WORDS = 53200

    def __init__(self, nc, st, cfg):
        self.nc = nc
        self.st = st
        self.cfg = cfg
        self.S = Sched(nc)
        self.arena = st.enter_context(nc.sbuf_tensor("arena", [128, self.ARENA_WORDS], F32))
        self.top = 0
        self.pbank = [st.enter_context(nc.psum_tensor("pb%d" % i, [128, 512], F32)) for i in range(8)]
        self.pobj = [Obj("pb%d" % i) for i in range(8)]
        self.final_objs = []

    def f32(self, name, n):
        off = self.top
        self.top += n + (n & 1)
        assert self.top <= self.ARENA_WORDS, (name, self.top)
        return self.arena[:, off:off + n], Obj(name)

    def bf(self, name, n):
        assert n % 2 == 0
        off = self.top
        self.top += n // 2 + ((n // 2) & 1)
        assert self.top <= self.ARENA_WORDS, (name, self.top)
        return self.arena[:, off:off + n // 2].bitcast(BF16), Obj(name)

    def mark(self):
        return self.top

    def release(self, m):
        self.top = m
        self.S.barrier()

    def dram(self, name, shape, dt, out=False):
        out = out or (name in self.cfg.get("outs", ()))
        kind = "ExternalOutput" if out else "Internal"
        return self.nc.dram_tensor(name, list(shape), dt, kind=kind).ap()


def build(cfg):
    nc = bass.Bass("TRN2", target_bir_lowering=False)
    dbg = cfg.get("dbg", False)
    layers = cfg.get("layers", [0, 1])
    phases = cfg.get("phases", {"p1", "p2", "p3", "p4a", "p4b"})
    final_norm = cfg.get("final", True)

    def inp(name, shape, dt=F32):
        return nc.dram_tensor(name, list(shape), dt, kind="ExternalInput").ap()

    x_in = inp("x", [S_LEN, D])
    ln1 = inp("ln1", [DEPTH, D]); ln2 = inp("ln2", [DEPTH, D]); lnf = inp("ln_f", [1, D])
    w_in = inp("w_in", [DEPTH, D, IN_COLS])
    convq = inp("conv_qkv_r", [DEPTH, 128, 12 * 4])
    alog = inp("a_log_r", [DEPTH, 1, 128]); dtb = inp("dt_bias_r", [DEPTH, 1, 128])
    gdnn = inp("gdn_norm", [DEPTH, 128])
    w_out = inp("w_out", [DEPTH, D, D])
    w_gate = inp("w_gate", [DEPTH, D, DFF]); w_up = inp("w_up", [DEPTH, D, DFF])
    ffnc = inp("ffn_conv_r", [DEPTH, 128, NFC * 3])
    w_down = inp("w_down", [DEPTH, DFF, D])
    cst_d = inp("cst", [128, 8 * 128]); cstb_d = inp("cstb", [128, 2 * 128], BF16); bm_d = inp("bm", [128, 17 * 128], BF16)
    kaug_d = inp("kaug", [8, 4, S_LEN], BF16); qaug_d = inp("qaug", [8, 4, S_LEN], BF16)
    out_d = nc.dram_tensor("out", [S_LEN, D], F32, kind="ExternalOutput").ap()

    with ExitStack() as st:
        kb = KB(nc, st, cfg)
        S = kb.S
        xs = kb.dram("xs", [S_LEN, D], F32, out=(dbg or cfg.get("xs_out", False)))
        gqkv = kb.dram("gqkv", [12, 128, S_LEN], BF16, out=dbg)
        aqk = kb.dram("aqk", [8, 128, S_LEN], BF16, out=dbg)
        zs = kb.dram("zs", [S_LEN, 512], BF16, out=dbg)
        avs = kb.dram("avs", [S_LEN, 512], BF16, out=dbg)
        glog_d = kb.dram("glog", [S_LEN, 8], F32, out=dbg)
        ydbg = kb.dram("ydbg", [S_LEN, D], BF16, out=True) if (dbg or cfg.get("ydump")) else None
        o_xs = [Obj("xs%d" % i) for i in range(NT)]
        o_gqkv = Obj("gqkv"); o_aqk = Obj("aqk"); o_zs = Obj("zs"); o_avs = Obj("avs")

        cst32, o_cst32 = kb.f32("cst32", 8 * 128)
        cstbf, o_cstbf = kb.bf("cstbf", 2 * 128)
        S.dma("sp", cst32, cst_d, [], [o_cst32], o_cst32)
        S.dma("sp", cstbf, cstb_d, [], [o_cstbf], o_cstbf)
        C32 = {n: cst32[:, i * 128:(i + 1) * 128] for i, n in enumerate(CST_NAMES)}
        ident_bf = cstbf[:, 0:128]
        ones_bf = cstbf[:, 128:256]
        kb.C32 = C32; kb.o_cst32 = o_cst32; kb.ident_bf = ident_bf; kb.ones_bf = ones_bf; kb.o_cstbf = o_cstbf
        glog, o_glog = kb.f32("glog", NT * 8)
        kb.glog = glog; kb.o_glog = o_glog

        S.keep()
        base_mark = kb.mark()
        phases_all = phases
        for l in layers:
            phases = cfg.get('phases_by_layer', {}).get(l, phases_all)
            kb.cur_phases = phases
            x_src = x_in if l == layers[0] else xs
            if "p1" in phases:
                m = kb.mark()
                S.tag = 'phase1_' + str(l)
                phase1(kb, l, x_src, ln1, w_in, convq, gqkv, aqk, zs, avs, o_xs, o_gqkv, o_aqk, o_zs, o_avs, glog_d)
                kb.release(m)
            m_y = kb.mark()
            ybuf, _ = kb.bf("ybuf", NT * 1024)
            yv = ybuf.rearrange("p (t c) -> p t c", t=NT)
            o_y = [Obj("y%d" % i) for i in range(NT)]
            if "p2" in phases and "p3" in phases and cfg.get("interleave", False):
                m = kb.mark()
                g2 = phase2_gen(kb, l, gqkv, zs, o_gqkv, o_zs, alog, dtb, gdnn, yv, o_y, banks=(4, 5, 6, 7))
                g3 = phase3_gen(kb, l, aqk, avs, o_aqk, o_avs, bm_d, kaug_d, qaug_d, yv, o_y, ps_banks=(0, 1, 2), pacc_banks=(3,), look=2)
                d2 = d3 = False
                ratio = cfg.get("ratio", 4)
                while not (d2 and d3):
                    if not d2:
                        S.tag = 'phase2_' + str(l)
                        try:
                            next(g2)
                        except StopIteration:
                            d2 = True
                    for _ in range(ratio):
                        if d3:
                            break
                        S.tag = 'phase3_' + str(l)
                        try:
                            next(g3)
                        except StopIteration:
                            d3 = True
                kb.release(m)
            else:
                if "p2" in phases:
                    m = kb.mark()
                    S.tag = 'phase2_' + str(l)
                    phase2(kb, l, gqkv, zs, o_gqkv, o_zs, alog, dtb, gdnn, yv, o_y, banks=cfg.get('p2banks', range(8)))
                    kb.release(m)
                if "p3" in phases:
                    m = kb.mark()
                    S.tag = 'phase3_' + str(l)
                    phase3(kb, l, aqk, avs, o_aqk, o_avs, bm_d, kaug_d, qaug_d, yv, o_y)
                    kb.release(m)
            if (dbg and not cfg.get("lim") and ("p2" in phases or "p3" in phases) and "p4a" not in phases) or cfg.get("ydump"):
                for t in range(NT):
                    S.dma("sp", ydbg[t * 128:(t + 1) * 128, :], yv[:, t, :], [o_y[t]], [], o_y[t])
                    kb.final_objs.append(o_y[t])
            if "p4a" in phases:
                m = kb.mark()
                S.tag = 'phase4a_' + str(l)
                phase4a(kb, l, x_src, xs, o_xs, w_out, yv, o_y)
                kb.release(m)
            kb.release(m_y)
            if "p4b" in phases:
                m = kb.mark()
                last = (l == layers[-1]) and final_norm
                x4 = xs if ("p4a" in phases or l != layers[0]) else x_in
                S.tag = 'phase4b_' + str(l)
                phase4b(kb, l, x4, out_d if last else xs, o_xs, ln2, w_gate, w_up, ffnc, w_down, lnf if last else None)
                kb.release(m)
                if dbg and cfg.get("snap") and l == 0:
                    m = kb.mark()
                    xsnap = kb.dram("xsnap", [S_LEN, D], F32, out=True)
                    bufs = [kb.f32("snap%d" % i, D) for i in range(2)]
                    for t in range(NT):
                        bt, ob = bufs[t % 2]
                        S.dma("sp", bt, xs[t * 128:(t + 1) * 128, :], [o_xs[t]], [ob], ob)
                        S.dma("sp", xsnap[t * 128:(t + 1) * 128, :], bt, [ob], [], ob)
                    kb.release(m)
        npad = cfg.get("pad", 0)
        if npad:
            padt, o_pad = kb.f32("padt", 8)
            for i in range(npad):
                S.op("dve", lambda e: e.memset(padt, 0.0), [], [o_pad])
                S.op("act", lambda e: e.activation(out=padt, in_=padt, func=AF.Copy), [o_pad], [o_pad])
        S.emit(st, final_wait_objs=kb.final_objs)
    return nc


def make_stages(kb, n=3, cap=2048):
    return [kb.f32("wstage%d" % i, cap) for i in range(n)]


def load_weight(kb, name, src2d, kchunks, ncols, stages, eng="sp"):
    S = kb.S
    w, o = kb.bf(name, kchunks * ncols)
    wv = w.rearrange("p (k c) -> p k c", k=kchunks)
    srcv = src2d.rearrange("(k p) c -> p k c", p=128)
    n = getattr(kb, "wstage_n", 0)
    for k in range(kchunks):
        c0 = 0
        while c0 < ncols:
            stg, o_stg = stages[n % len(stages)]
            cap = stg.shape[-1]
            c1 = min(ncols, c0 + cap)
            S.dma(eng, stg[:, 0:c1 - c0], srcv[:, k, c0:c1], [], [o_stg], o_stg)
            dst = wv[:, k, c0:c1]
            ce = ("act", "dve", "pool")[n % 3]
            if ce == "act":
                S.op("act", lambda e, dst=dst, stg=stg, m=c1 - c0: e.activation(out=dst, in_=stg[:, 0:m], func=AF.Copy), [o_stg], [o])
            else:
                S.op(ce, lambda e, dst=dst, stg=stg, m=c1 - c0: e.tensor_copy(out=dst, in_=stg[:, 0:m]), [o_stg], [o])
            n += 1
            c0 = c1
    kb.wstage_n = n
    return wv, o


def load_weight_blocks(kb, wv, src2d, kchunks, blocks, stages, eng="sp"):
    S = kb.S
    srcv = src2d.rearrange("(k p) c -> p k c", p=128)
    n = getattr(kb, "wstage_n", 0)
    objs = []
    for (c0, c1) in blocks:
        o = Obj("wblk%d" % c0)
        objs.append(o)
        width = c1 - c0
        cap = stages[0][0].shape[-1]
        kstep = max(1, min(kchunks, cap // width))
        for k0 in range(0, kchunks, kstep):
            k1 = min(kchunks, k0 + kstep)
            stg, o_stg = stages[n % len(stages)]
            stv = stg[:, 0:(k1 - k0) * width].rearrange("p (k c) -> p k c", k=k1 - k0)
            S.dma(eng, stv, srcv[:, k0:k1, c0:c1], [], [o_stg], o_stg)
            dst = wv[:, k0:k1, c0:c1]
            ce = ("act", "dve")[n % 2]
            if ce == "act":
                S.op("act", lambda e, dst=dst, stv=stv: e.activation(out=dst, in_=stv, func=AF.Copy), [o_stg], [o])
            else:
                S.op(ce, lambda e, dst=dst, stv=stv: e.tensor_copy(out=dst, in_=stv), [o_stg], [o])
            n += 1
    kb.wstage_n = n
    return objs


def rms_stats(kb, ss, o_ss, n, inv_n, eps):
    S = kb.S
    S.op("dve", lambda e: e.tensor_scalar(out=ss, in0=ss, scalar1=inv_n, scalar2=eps, op0=ALU.mult, op1=ALU.add), [o_ss], [o_ss])
    S.op("act", lambda e: e.activation(out=ss, in_=ss, func=AF.Sqrt), [o_ss], [o_ss])
    S.op("dve", lambda e: e.reciprocal(out=ss, in_=ss), [o_ss], [o_ss])


def norm_transpose(kb, xt, o_xt, j, lnrep, o_ln, ss, o_ss, hb, o_hb, junk, o_junk, pT, o_pT, hT, o_hT, col0, evac_eng):
    S = kb.S
    xj = xt[:, j, :]
    S.op("dve", lambda e: e.scalar_tensor_tensor(out=hb, in0=xj, scalar=ss[:, j:j + 1], in1=lnrep, op0=ALU.mult, op1=ALU.mult),
         [o_xt, o_ss, o_ln], [o_hb])
    pTb = pT.bitcast(BF16)
    for k in range(8):
        S.op("pe", lambda e, k=k: e.transpose(pTb[:, k * 128:(k + 1) * 128], hb[:, k * 128:(k + 1) * 128], kb.ident_bf),
             [o_hb, kb.o_cstbf], [o_pT])
    src = pTb.rearrange("p (k t) -> p k t", k=8)
    dst = hT[:, :, col0:col0 + 128]
    if evac_eng == "act":
        S.op("act", lambda e: e.activation(out=dst, in_=src, func=AF.Copy), [o_pT], [o_hT])
    else:
        S.op("dve", lambda e: e.tensor_copy(out=dst, in_=src), [o_pT], [o_hT])


def phase4b(kb, l, x_src, x_dst, o_xs, ln2, w_gate, w_up, ffnc, w_down, lnf):
    S = kb.S
    nc = kb.nc
    actreg, _ = kb.f32("actreg", NFC * 256)
    stages = [(actreg[:, i * 1408:(i + 1) * 1408], Obj("wst%d" % i)) for i in range(4)]
    o_stages = [o for _, o in stages]
    wg_a, _ = kb.bf("wg", 8 * DFF)
    wu_a, _ = kb.bf("wu", 8 * DFF)
    wg = wg_a.rearrange("p (k c) -> p k c", k=8)
    wu = wu_a.rearrange("p (k c) -> p k c", k=8)
    lnrep, o_ln = kb.f32("ln2rep", D)
    S.dma("sp", lnrep, ln2[l:l + 1, :].partition_broadcast(128), [], [o_ln], o_ln)
    if lnf is not None:
        lnfrep, o_lnf = kb.f32("lnfrep", D)
        S.dma("sp", lnfrep, lnf[0:1, :].partition_broadcast(128), [], [o_lnf], o_lnf)
    cw, o_cw = kb.f32("ffncw", NFC * 3)
    S.dma("sp", cw, ffnc[l], [], [o_cw], o_cw)
    cwv = cw.rearrange("p (c j) -> p c j", j=3)
    halo, o_halo = kb.f32("halo", NFC * 2)
    halov = halo.rearrange("p (c j) -> p c j", j=2)
    S.op("pool", lambda e: e.memset(halo, 0.0), [], [o_halo])
    xt_a, o_xt = kb.f32("xt", 4 * D)
    xt = xt_a.rearrange("p (j d) -> p j d", j=4)
    hb, o_hb = kb.bf("hb", D)
    junk, o_junk = kb.bf("junk", D)
    ss, o_ss = kb.f32("ss", 4)
    hT_a, o_hT = kb.bf("hT", 8 * 512)
    hT = hT_a.rearrange("p (k t) -> p k t", k=8)
    actT_a, o_actT = actreg.bitcast(BF16), Obj("actT")
    actT = actT_a.rearrange("p (c t) -> p c t", c=NFC)
    pres = [kb.f32("pre%d" % i, 514) for i in range(2)]
    cvs = [kb.f32("cv%d" % i, 512) for i in range(3)]
    xo_a, o_xo, xo = xt_a, o_xt, xt
    ss2, o_ss2 = kb.f32("ss2", 4)
    pb = kb.pbank; po = kb.pobj
    S.dma("sp", xt, x_src[0:512, :].rearrange("(j p) d -> p j d", p=128), o_xs[0:4], [o_xt], o_xt)
    fblocks = [(c0, min(DFF, c0 + 512)) for c0 in range(0, DFF, 512)]
    o_wgb, o_wub = [], []
    for blk in fblocks:
        o_wgb += load_weight_blocks(kb, wg, w_gate[l], 8, [blk], stages)
        o_wub += load_weight_blocks(kb, wu, w_up[l], 8, [blk], stages)
    wd, o_wd = load_weight(kb, "wd", w_down[l], NFC, D, stages)
    for T in range(8):
        tiles = o_xs[T * 4:(T + 1) * 4]
        src = x_src[T * 512:(T + 1) * 512, :].rearrange("(j p) d -> p j d", p=128)
        if T > 0:
            S.dma("sp", xt, src, tiles, [o_xt], o_xt)
        for j in range(4):
            S.op("act", lambda e, j=j: e.activation(out=junk, in_=xt[:, j, :], func=AF.Square, accum_out=ss[:, j:j + 1]),
                 [o_xt], [o_junk, o_ss])
        rms_stats(kb, ss, o_ss, 4, 1.0 / D, RMS_EPS)
        for j in range(4):
            norm_transpose(kb, xt, o_xt, j, lnrep, o_ln, ss, o_ss, hb, o_hb, junk, o_junk, pb[6 + (j % 2)], po[6 + (j % 2)],
                           hT, o_hT, j * 128, "act" if j % 2 == 0 else "dve")
        def ffn_a(fc):
            pg, opg = pb[(2 * fc) % 6], po[(2 * fc) % 6]
            pu, opu = pb[(2 * fc + 1) % 6], po[(2 * fc + 1) % 6]
            for k in range(8):
                S.op("pe", lambda e, k=k, fc=fc, pg=pg: e.matmul(pg[:, :], lhsT=wg[:, k, fc * 128:(fc + 1) * 128], rhs=hT[:, k, :],
                                                                 start=(k == 0), stop=(k == 7)), [o_wgb[fc // 4], o_hT], [opg])
            for k in range(8):
                S.op("pe", lambda e, k=k, fc=fc, pu=pu: e.matmul(pu[:, :], lhsT=wu[:, k, fc * 128:(fc + 1) * 128], rhs=hT[:, k, :],
                                                                 start=(k == 0), stop=(k == 7)), [o_wub[fc // 4], o_hT], [opu])
            pre, o_pre = pres[fc % 2]
            cv, o_cv = cvs[fc % 3]
            S.op("pool", lambda e, pre=pre, fc=fc: e.tensor_copy(out=pre[:, 0:2], in_=halov[:, fc, :]), [o_halo], [o_pre])
            S.op("act", lambda e, pre=pre, pg=pg: e.activation(out=pre[:, 2:514], in_=pg[:, :], func=AF.Copy), [opg], [o_pre])
            S.op("pool", lambda e, pre=pre, fc=fc: e.tensor_copy(out=halov[:, fc, :], in_=pre[:, 512:514]), [o_pre], [o_halo])
            S.op("dve", lambda e, pre=pre, cv=cv, fc=fc: e.tensor_scalar(out=cv, in0=pre[:, 2:514], scalar1=cwv[:, fc, 2:3], scalar2=None,
                                                                         op0=ALU.mult), [o_pre, o_cw], [o_cv])
            S.op("dve", lambda e, pre=pre, cv=cv, fc=fc: e.scalar_tensor_tensor(out=cv, in0=pre[:, 1:513], scalar=cwv[:, fc, 1:2], in1=cv,
                                                                                op0=ALU.mult, op1=ALU.add), [o_pre, o_cw, o_cv], [o_cv])
            S.op("dve", lambda e, pre=pre, cv=cv, fc=fc: e.scalar_tensor_tensor(out=cv, in0=pre[:, 0:512], scalar=cwv[:, fc, 0:1], in1=cv,
                                                                                op0=ALU.mult, op1=ALU.add), [o_pre, o_cw, o_cv], [o_cv])

        def ffn_b(fc):
            pu, opu = pb[(2 * fc + 1) % 6], po[(2 * fc + 1) % 6]
            cv, o_cv = cvs[fc % 3]
            S.op("act", lambda e, cv=cv: e.activation(out=cv, in_=cv, func=AF.Silu), [o_cv], [o_cv])
            S.op("dve", lambda e, cv=cv, pu=pu, fc=fc: e.tensor_tensor(out=actT[:, fc, :], in0=cv, in1=pu[:, :], op=ALU.mult),
                 [o_cv, opu], [o_actT] + (o_stages if T == 0 else []))

        for s_ in range(NFC + 1):
            if s_ < NFC:
                ffn_a(s_)
            if s_ >= 1:
                ffn_b(s_ - 1)
        for j in range(4):
            for nh in range(2):
                pi = (j * 2 + nh) % 6
                pd, opd = pb[pi], po[pi]
                for fc in range(NFC):
                    S.op("pe", lambda e, fc=fc, j=j, nh=nh, pd=pd: e.matmul(pd[:, :], lhsT=actT[:, fc, j * 128:(j + 1) * 128],
                                                                          rhs=wd[:, fc, nh * 512:(nh + 1) * 512],
                                                                          start=(fc == 0), stop=(fc == NFC - 1)), [o_actT, o_wd], [opd])
                S.op("dve", lambda e, j=j, nh=nh, pd=pd: e.tensor_tensor(out=xo[:, j, nh * 512:(nh + 1) * 512],
                                                                       in0=xt[:, j, nh * 512:(nh + 1) * 512], in1=pd[:, :], op=ALU.add),
                     [o_xt, opd], [o_xo])
        dst = x_dst[T * 512:(T + 1) * 512, :].rearrange("(j p) d -> p j d", p=128)
        if lnf is not None:
            for j in range(4):
                S.op("act", lambda e, j=j: e.activation(out=junk, in_=xo[:, j, :], func=AF.Square, accum_out=ss2[:, j:j + 1]),
                     [o_xo], [o_junk, o_ss2])
            rms_stats(kb, ss2, o_ss2, 4, 1.0 / D, RMS_EPS)
            for j in range(4):
                S.op("dve", lambda e, j=j: e.scalar_tensor_tensor(out=xo[:, j, :], in0=xo[:, j, :], scalar=ss2[:, j:j + 1], in1=lnfrep,
                                                                  op0=ALU.mult, op1=ALU.mult), [o_xo, o_ss2, o_lnf], [o_xo])
            S.dma("sp", dst, xo_a.rearrange("p (j d) -> p j d", j=4), [o_xo], [], o_xo)
        else:
            S.dma("sp", dst, xo_a.rearrange("p (j d) -> p j d", j=4), [o_xo], tiles, o_xo)
        if o_xo not in kb.final_objs:
            kb.final_objs.append(o_xo)


def phase1(kb, l, x_src, ln1, w_in, convq, gqkv, aqk, zs, avs, o_xs, o_gqkv, o_aqk, o_zs, o_avs, glog_d):
    S = kb.S
    w_a, _ = kb.bf("win", 8 * IN_COLS)
    w = w_a.rearrange("p (k c) -> p k c", k=8)
    lnrep, o_ln = kb.f32("ln1rep", D)
    S.dma("sp", lnrep, ln1[l:l + 1, :].partition_broadcast(128), [], [o_ln], o_ln)
    cw, o_cw = kb.f32("qkvcw", 48)
    S.dma("sp", cw, convq[l], [], [o_cw], o_cw)
    cwv = cw.rearrange("p (c j) -> p c j", j=4)
    pre_a, _ = kb.f32("qkvpre", 12 * 516)
    prev = pre_a.rearrange("p (c t) -> p c t", c=12)
    o_pre = [Obj("pre%d" % c) for c in range(12)]
    S.op("pool", lambda e: e.memset(pre_a, 0.0), [], o_pre)
    xts = [kb.f32("xt%d" % i, 4 * D) for i in range(2)]
    hb, o_hb = kb.bf("hb", D)
    junk, o_junk = kb.bf("junk", D)
    ss, o_ss = kb.f32("ss", 4)
    hTs = [kb.bf("hT%d" % i, 8 * 512) for i in range(2)]
    cvs = [kb.f32("cv%d" % i, 512) for i in range(3)]
    sqs = [kb.bf("sq%d" % i, 512) for i in range(2)]
    rns = [kb.f32("rn%d" % i, 512) for i in range(2)]
    stg = [kb.bf("stg%d" % i, 512) for i in range(4)]
    zst_a, o_zst = kb.bf("zst", 4 * 512)
    zst = zst_a.rearrange("p (j c) -> p j c", j=4)
    avst_a, o_avst = kb.bf("avst", 4 * 512)
    avst = avst_a.rearrange("p (j c) -> p j c", j=4)
    glv = kb.glog.rearrange("p (t c) -> p t c", c=8)
    pb = kb.pbank; po = kb.pobj
    nstg = 0
    npb = 0
    def load_x(T):
        xt_a, o_xt = xts[T % 2]
        src = x_src[T * 512:(T + 1) * 512, :].rearrange("(j p) d -> p j d", p=128)
        S.dma("sp", xt_a.rearrange("p (j d) -> p j d", j=4), src, o_xs[T * 4:(T + 1) * 4], [o_xt], o_xt)

    load_x(0)
    wblocks = [(0, 512), (512, 1024), (1024, 1536), (2056, 2568), (2568, 3080), (1536, 2048), (3080, 3592), (2048, 2056)]
    wobjs = load_weight_blocks(kb, w, w_in[l], 8, wblocks, make_stages(kb, n=4, cap=2048))
    o_wcol = {}
    for (c0, c1), o in zip(wblocks, wobjs):
        for c in range(c0, c1, 8):
            o_wcol[c] = o

    def o_wc(c0):
        return o_wcol[c0 - (c0 % 8)]
    for T in range(8):
        tiles = o_xs[T * 4:(T + 1) * 4]
        xt_a, o_xt = xts[T % 2]
        xt = xt_a.rearrange("p (j d) -> p j d", j=4)
        hT_a, o_hT = hTs[T % 2]
        hT = hT_a.rearrange("p (k t) -> p k t", k=8)
        if T + 1 < 8:
            load_x(T + 1)
        for j in range(4):
            S.op("act", lambda e, j=j, xt=xt: e.activation(out=junk, in_=xt[:, j, :], func=AF.Square, accum_out=ss[:, j:j + 1]),
                 [o_xt], [o_junk, o_ss])
        rms_stats(kb, ss, o_ss, 4, 1.0 / D, RMS_EPS)
        for j in range(4):
            norm_transpose(kb, xt, o_xt, j, lnrep, o_ln, ss, o_ss, hb, o_hb, junk, o_junk, pb[6 + (j % 2)], po[6 + (j % 2)],
                           hT, o_hT, j * 128, "act" if j % 2 == 0 else "dve")
        tsl = slice(T * 512, (T + 1) * 512)
        def stage_a(c):
            pg, opg = pb[c % 4], po[c % 4]
            for k in range(8):
                S.op("pe", lambda e, k=k, c=c, pg=pg, hT=hT: e.matmul(pg[:, :], lhsT=w[:, k, c * 128:(c + 1) * 128], rhs=hT[:, k, :],
                                                                      start=(k == 0), stop=(k == 7)), [o_wc(c * 128), o_hT], [opg])
            opre = o_pre[c]
            S.op("act", lambda e, c=c, pg=pg: e.activation(out=prev[:, c, 3:515], in_=pg[:, :], func=AF.Copy), [opg], [opre])
            cv, o_cv = cvs[c % 3]
            S.op("act", lambda e, c=c, cv=cv, pg=pg: e.activation(out=cv, in_=pg[:, :], func=AF.Copy, scale=cwv[:, c, 3:4]), [opg, o_cw], [o_cv])
            for jt in (2, 1, 0):
                S.op("dve", lambda e, c=c, cv=cv, jt=jt: e.scalar_tensor_tensor(out=cv, in0=prev[:, c, jt:jt + 512], scalar=cwv[:, c, jt:jt + 1],
                                                                                in1=cv, op0=ALU.mult, op1=ALU.add), [opre, o_cw, o_cv], [o_cv])
            S.op("pool", lambda e, c=c: e.tensor_copy(out=prev[:, c, 0:3], in_=prev[:, c, 512:515]), [opre], [opre])

        def stage_b(c):
            cv, o_cv = cvs[c % 3]
            if c >= 8:
                st_t, o_st = stg[c % 4]
                S.op("act", lambda e, cv=cv, st_t=st_t: e.activation(out=st_t, in_=cv, func=AF.Silu), [o_cv], [o_st])
                S.dma("sp", gqkv[c][:, tsl], st_t, [o_st], [o_gqkv], o_st)
            else:
                sq, o_sq = sqs[c % 2]
                rn, o_rn = rns[c % 2]
                pn, opn = pb[4 + (c % 2)], po[4 + (c % 2)]
                S.op("act", lambda e, cv=cv: e.activation(out=cv, in_=cv, func=AF.Silu), [o_cv], [o_cv])
                S.op("pool", lambda e, cv=cv, sq=sq: e.tensor_tensor(out=sq, in0=cv, in1=cv, op=ALU.mult), [o_cv], [o_sq])
                S.op("pe", lambda e, sq=sq, pn=pn: e.matmul(pn[:, :], lhsT=kb.ones_bf, rhs=sq, start=True, stop=True), [kb.o_cstbf, o_sq], [opn])

        def stage_c(c):
            if c >= 8:
                return
            cv, o_cv = cvs[c % 3]
            rn, o_rn = rns[c % 2]
            st_t, o_st = stg[c % 4]
            pn, opn = pb[4 + (c % 2)], po[4 + (c % 2)]
            S.op("act", lambda e, rn=rn, pn=pn: e.activation(out=rn, in_=pn[:, :], func=AF.Sqrt, bias=L2_EPS), [opn], [o_rn])
            S.op("dve", lambda e, rn=rn: e.reciprocal(out=rn, in_=rn), [o_rn], [o_rn])
            sc = (128 ** -0.5) if c < 4 else 1.0
            S.op("dve", lambda e, cv=cv, rn=rn, st_t=st_t, sc=sc: e.scalar_tensor_tensor(out=st_t, in0=cv, scalar=sc, in1=rn,
                                                                                         op0=ALU.mult, op1=ALU.mult), [o_cv, o_rn], [o_st])
            S.dma("sp", gqkv[c][:, tsl], st_t, [o_st], [o_gqkv], o_st)

        for s_ in range(12 + 2):
            if s_ < 12:
                stage_a(s_)
            if 0 <= s_ - 1 < 12:
                stage_b(s_ - 1)
            if 0 <= s_ - 2 < 12:
                stage_c(s_ - 2)
        npb = 0
        nstg = 0
        for c in range(8):
            pg, opg = pb[npb % 4], po[npb % 4]; npb += 1
            col = 2056 + c * 128
            for k in range(8):
                S.op("pe", lambda e, k=k, col=col, pg=pg, hT=hT: e.matmul(pg[:, :], lhsT=w[:, k, col:col + 128], rhs=hT[:, k, :],
                                                                   start=(k == 0), stop=(k == 7)), [o_wc(col), o_hT], [opg])
            st_t, o_st = stg[nstg % 4]; nstg += 1
            sc = 0.125 if c < 4 else 1.0
            if c % 2 == 0:
                S.op("act", lambda e, pg=pg, st_t=st_t, sc=sc: e.activation(out=st_t, in_=pg[:, :], func=AF.Copy, scale=sc), [opg], [o_st])
            else:
                S.op("dve", lambda e, pg=pg, st_t=st_t, sc=sc: e.tensor_scalar(out=st_t, in0=pg[:, :], scalar1=sc, scalar2=None, op0=ALU.mult),
                     [opg], [o_st])
            S.dma("sp", aqk[c][:, tsl], st_t, [o_st], [o_aqk], o_st)
        for j in range(4):
            for which, col, dst_t, o_dst in ((0, 1536, zst, o_zst), (1, 3080, avst, o_avst)):
                pg, opg = pb[npb % 4], po[npb % 4]; npb += 1
                for k in range(8):
                    S.op("pe", lambda e, k=k, j=j, col=col, pg=pg, hT=hT: e.matmul(pg[:, :], lhsT=hT[:, k, j * 128:(j + 1) * 128], rhs=w[:, k, col:col + 512],
                                                                          start=(k == 0), stop=(k == 7)), [o_wc(col), o_hT], [opg])
                if which == 0:
                    S.op("act", lambda e, pg=pg, dst_t=dst_t, j=j: e.activation(out=dst_t[:, j, :], in_=pg[:, :], func=AF.Copy), [opg], [o_dst])
                else:
                    S.op("dve", lambda e, pg=pg, dst_t=dst_t, j=j: e.tensor_copy(out=dst_t[:, j, :], in_=pg[:, :]), [opg], [o_dst])
            pl, opl = pb[4 + (j % 2)], po[4 + (j % 2)]
            for k in range(8):
                S.op("pe", lambda e, k=k, j=j, pl=pl, hT=hT: e.matmul(pl[:, 0:8], lhsT=hT[:, k, j * 128:(j + 1) * 128], rhs=w[:, k, 2048:2056],
                                                               start=(k == 0), stop=(k == 7)), [o_wc(2048), o_hT], [opl])
            S.op("dve", lambda e, pl=pl, j=j, T=T: e.tensor_copy(out=glv[:, T * 4 + j, :], in_=pl[:, 0:8]), [opl], [kb.o_glog])
        S.dma("sp", zs[tsl, :].rearrange("(j p) c -> p j c", p=128), zst, [o_zst], [o_zs], o_zst)
        S.dma("sp", avs[tsl, :].rearrange("(j p) c -> p j c", p=128), avst, [o_avst], [o_avs], o_avst)
    if kb.cfg.get("dbg"):
        S.dma("sp", glog_d.rearrange("(t p) c -> p t c", p=128), glv, [kb.o_glog], [], kb.o_glog)
        kb.final_objs.append(kb.o_glog)


def phase4a(kb, l, x_src, xs, o_xs, w_out, yv, o_y):
    S = kb.S
    wo, o_wo = load_weight(kb, "wo", w_out[l], 8, D, make_stages(kb))
    xts = [kb.f32("xa%d" % i, D) for i in range(2)]
    yTs = [kb.bf("yT%d" % i, 8 * 128) for i in range(2)]
    pb = kb.pbank; po = kb.pobj
    S.dma("sp", xts[0][0], x_src[0:128, :], [o_xs[0]], [xts[0][1]], xts[0][1])
    for t in range(NT):
        xt, o_xt = xts[t % 2]
        yT_a, o_yT = yTs[t % 2]
        yT = yT_a.rearrange("p (k t) -> p k t", k=8)
        if t + 1 < NT:
            xn, o_xn = xts[(t + 1) % 2]
            S.dma("sp", xn, x_src[(t + 1) * 128:(t + 2) * 128, :], [o_xs[t + 1]], [o_xn], o_xn)
        pT, o_pT = pb[6 + (t % 2)], po[6 + (t % 2)]
        pTb = pT.bitcast(BF16)
        for k in range(8):
            S.op("pe", lambda e, k=k, t=t, pTb=pTb: e.transpose(pTb[:, k * 128:(k + 1) * 128], yv[:, t, k * 128:(k + 1) * 128], kb.ident_bf),
                 [o_y[t], kb.o_cstbf], [o_pT])
        if t % 2 == 0:
            S.op("act", lambda e, pTb=pTb, yT_a=yT_a: e.activation(out=yT_a, in_=pTb[:, :], func=AF.Copy), [o_pT], [o_yT])
        else:
            S.op("dve", lambda e, pTb=pTb, yT_a=yT_a: e.tensor_copy(out=yT_a, in_=pTb[:, :]), [o_pT], [o_yT])
        for nh in range(2):
            pi = (t * 2 + nh) % 6
            pd, opd = pb[pi], po[pi]
            for k in range(8):
                S.op("pe", lambda e, k=k, nh=nh, pd=pd, yT=yT: e.matmul(pd[:, :], lhsT=yT[:, k, :], rhs=wo[:, k, nh * 512:(nh + 1) * 512],
                                                                      start=(k == 0), stop=(k == 7)), [o_yT, o_wo], [opd])
            S.op("dve", lambda e, nh=nh, pd=pd, xt=xt: e.tensor_tensor(out=xt[:, nh * 512:(nh + 1) * 512], in0=xt[:, nh * 512:(nh + 1) * 512],
                                                                     in1=pd[:, :], op=ALU.add), [o_xt, opd], [o_xt])
        S.dma("sp", xs[t * 128:(t + 1) * 128, :], xt, [o_xt], [o_xs[t]], o_xt)
        if o_xt not in kb.final_objs:
            kb.final_objs.append(o_xt)


def phase3(*a, **k):
    for _ in phase3_gen(*a, **k):
        pass


def phase3_gen(kb, l, aqk, avs, o_aqk, o_avs, bm_d, kaug_d, qaug_d, yv, o_y, ps_banks=(0, 1, 2, 3, 4), pacc_banks=(5, 6, 7), look=2):
    S = kb.S
    bm, o_bm = kb.bf("bm", 17 * 128)
    S.dma("sp", bm, bm_d, [], [o_bm], o_bm)
    KTs = [kb.bf("KT%d" % i, S_LEN) for i in range(2)]
    QTs = [kb.bf("QT%d" % i, S_LEN) for i in range(2)]
    VAs = [kb.bf("VA%d" % i, NT * 66) for i in range(2)]
    PTs = [kb.bf("PT%d" % i, 512) for i in range(3)]
    rd, o_rd = kb.f32("rden", 2)
    for i in range(2):
        S.op("pool", lambda e, i=i: e.memset(KTs[i][0][64:128, :], 0.0), [], [KTs[i][1]])
        S.op("pool", lambda e, i=i: e.memset(QTs[i][0][64:128, :], 0.0), [], [QTs[i][1]])
        S.op("pool", lambda e, i=i: e.memset(VAs[i][0], 1.0), [], [VAs[i][1]])
    pb = kb.pbank; po = kb.pobj
    lim = kb.cfg.get('lim', False)
    PTs = PTs + [kb.bf("PT%d" % i, 512) for i in range(3, 5)]
    groups = []
    for h in range(2 if lim else 8):
        for qb in (list(range(3)) + [20] if lim else range(NT)):
            nkb = min(qb, 16) + 1
            for o0 in range(0, nkb, 4):
                groups.append((h, qb, o0, min(4, nkb - o0), nkb))
    loaded = set()
    state = {}

    def load_head(h):
        KT, o_KT = KTs[h % 2]
        QT, o_QT = QTs[h % 2]
        VA_a, o_VA = VAs[h % 2]
        VA = VA_a.rearrange("p (t c) -> p t c", c=66)
        r0 = (h % 2) * 64
        S.dma("sp", KT[0:64, :], aqk[4 + h // 2][r0:r0 + 64, :], [o_aqk], [o_KT], o_KT)
        S.dma("sp", QT[0:64, :], aqk[h // 2][r0:r0 + 64, :], [o_aqk], [o_QT], o_QT)
        S.dma("sp", KT[64:68, :], kaug_d[h], [], [o_KT], o_KT)
        S.dma("sp", QT[64:68, :], qaug_d[h], [], [o_QT], o_QT)
        S.dma("sp", VA[:, :, 0:64], avs[:, h * 64:(h + 1) * 64].rearrange("(t p) c -> p t c", p=128), [o_avs], [o_VA], o_VA)

    def emit_qk(gi):
        h, qb, o0, n, nkb = groups[gi]
        if h not in loaded:
            loaded.add(h)
            load_head(h)
        KT, o_KT = KTs[h % 2]
        QT, o_QT = QTs[h % 2]
        ps, ops_ = pb[ps_banks[gi % len(ps_banks)]], po[ps_banks[gi % len(ps_banks)]]
        PT, o_PT = PTs[gi % 5]
        S.op("pe", lambda e, ps=ps, o0=o0, n=n: e.matmul(ps[:, 0:n * 128], lhsT=kb.ident_bf, rhs=bm[:, o0 * 128:(o0 + n) * 128],
                                                         start=True, stop=False), [kb.o_cstbf, o_bm], [ops_])
        for o in range(o0, o0 + n):
            kbk = qb - o
            S.op("pe", lambda e, ps=ps, o=o, o0=o0, n=n, kbk=kbk, qb=qb, KT=KT, QT=QT: e.matmul(
                ps[:, (o - o0) * 128:(o - o0 + 1) * 128], lhsT=KT[:, kbk * 128:(kbk + 1) * 128], rhs=QT[:, qb * 128:(qb + 1) * 128],
                start=False, stop=(o == o0 + n - 1)), [o_KT, o_QT], [ops_])
        S.op("act", lambda e, ps=ps, PT=PT, n=n: e.activation(out=PT[:, 0:n * 128], in_=ps[:, 0:n * 128], func=AF.Exp), [ops_], [o_PT])

    def emit_pv(gi):
        h, qb, o0, n, nkb = groups[gi]
        VA_a, o_VA = VAs[h % 2]
        VA = VA_a.rearrange("p (t c) -> p t c", c=66)
        PT, o_PT = PTs[gi % 5]
        if o0 == 0:
            state["npo"] = state.get("npo", 0) + 1
        pi = pacc_banks[state["npo"] % len(pacc_banks)]
        pacc, opacc = pb[pi], po[pi]
        for o in range(o0, o0 + n):
            kbk = qb - o
            S.op("pe", lambda e, pacc=pacc, PT=PT, o=o, o0=o0, kbk=kbk, VA=VA, nkb=nkb: e.matmul(
                pacc[:, 0:65], lhsT=PT[:, (o - o0) * 128:(o - o0 + 1) * 128], rhs=VA[:, kbk, 0:65],
                start=(o == 0), stop=(o == nkb - 1)), [o_PT, o_VA], [opacc])
        if o0 + n == nkb:
            rdc = rd[:, (qb % 2):(qb % 2) + 1]
            S.op("dve", lambda e, pacc=pacc, rdc=rdc: e.reciprocal(out=rdc, in_=pacc[:, 64:65]), [opacc], [o_rd])
            S.op("dve", lambda e, pacc=pacc, rdc=rdc, qb=qb, h=h: e.tensor_scalar(out=yv[:, qb, 512 + h * 64:512 + (h + 1) * 64], in0=pacc[:, 0:64],
                                                                                  scalar1=rdc, scalar2=None, op0=ALU.mult), [opacc, o_rd], [o_y[qb]])

    LOOK = look
    G = len(groups)
    for gi in range(G + LOOK):
        if gi < G:
            emit_qk(gi)
        if gi - LOOK >= 0:
            emit_pv(gi - LOOK)
        yield


class SlotPool:
    def __init__(self, kb, banks=range(8)):
        self.banks = [(kb.pbank[b], kb.pobj[b]) for b in banks]
        self.n = 0

    def bank(self):
        b = self.banks[self.n % len(self.banks)]
        self.n += 1
        return b


def _slot(bank, h):
    return bank[0][:, h * 128:(h + 1) * 128], bank[1]


def phase2(*a, **k):
    for _ in phase2_gen(*a, **k):
        pass


def phase2_gen(kb, l, gqkv, zs, o_gqkv, o_zs, alog, dtb, gdnn, yv, o_y, banks=range(8)):
    from itertools import zip_longest
    S = kb.S
    C32 = kb.C32
    o_c32 = kb.o_cst32
    ident32 = C32["ident"]; ones32 = C32["ones"]; mbcT = C32["mbcT"]; msT = C32["msT"]; lmT = C32["lmT"]
    sel = [C32["sel0"], C32["sel1"]]
    ident_bf = kb.ident_bf; o_cbf = kb.o_cstbf
    sp = SlotPool(kb, banks)
    lim = kb.cfg.get("lim", False)
    import os
    ntile = int(os.environ.get('P2NT', '3')) if lim else NT

    def t128(name):
        return kb.f32(name, 128)

    dtbr, o_dtbr = t128("dtbr"); algr, o_algr = t128("algr"); gnrep, o_gn = t128("gnrep")
    S.dma("sp", dtbr, dtb[l].partition_broadcast(128), [], [o_dtbr], o_dtbr)
    S.dma("sp", algr, alog[l].partition_broadcast(128), [], [o_algr], o_algr)
    S.dma("sp", gnrep, gdnn[l:l + 1, :].partition_broadcast(128), [], [o_gn], o_gn)
    glv = kb.glog.rearrange("p (t c) -> p t c", c=8)
    o_gl = kb.o_glog
    g, o_g = t128("g"); bet, o_bet = t128("bet"); nbet, o_nbet = t128("nbet")
    if 'p1' not in kb.cur_phases:
        S.op("pool", lambda e: e.memset(kb.glog, 0.1), [], [o_gl])
    gc, o_gc = t128("gc"); ngc, o_ngc = t128("ngc"); egc, o_egc = t128("egc"); negc, o_negc = t128("negc")
    edl, o_edl = t128("edl"); tmp, o_tmp = t128("tmp")
    dlr = [t128("dlr0"), t128("dlr1")]
    v3 = lambda ap: ap.rearrange("p (t h) -> p t h", h=4)
    S.op("dve", lambda e: e.tensor_tensor(out=v3(tmp), in0=glv[:, :, 4:8], in1=v3(dtbr), op=ALU.add), [o_gl, o_dtbr], [o_tmp])
    S.op("act", lambda e: e.activation(out=tmp, in_=tmp, func=AF.Exp), [o_tmp], [o_tmp])
    S.op("dve", lambda e: e.tensor_scalar(out=tmp, in0=tmp, scalar1=1.0, scalar2=None, op0=ALU.add), [o_tmp], [o_tmp])
    S.op("act", lambda e: e.activation(out=tmp, in_=tmp, func=AF.Ln), [o_tmp], [o_tmp])
    S.op("act", lambda e: e.activation(out=algr, in_=algr, func=AF.Exp), [o_algr], [o_algr])
    S.op("dve", lambda e: e.scalar_tensor_tensor(out=g, in0=tmp, scalar=-1.0, in1=algr, op0=ALU.mult, op1=ALU.mult), [o_tmp, o_algr], [o_g])
    S.op("act", lambda e: e.activation(out=v3(bet), in_=glv[:, :, 0:4], func=AF.Sigmoid), [o_gl], [o_bet])
    S.op("dve", lambda e: e.tensor_scalar(out=nbet, in0=bet, scalar1=-1.0, scalar2=None, op0=ALU.mult), [o_bet], [o_nbet])
    pgc, opgc = _slot(sp.bank(), 0)
    S.op("pe", lambda e: e.matmul(pgc, lhsT=lmT, rhs=g, start=True, stop=True), [o_c32, o_g], [opgc])
    S.op("dve", lambda e: e.tensor_copy(out=gc, in_=pgc), [opgc], [o_gc])
    S.op("dve", lambda e: e.tensor_scalar(out=ngc, in0=gc, scalar1=-1.0, scalar2=None, op0=ALU.mult), [o_gc], [o_ngc])
    S.op("act", lambda e: e.activation(out=egc, in_=gc, func=AF.Exp), [o_gc], [o_egc])
    S.op("dve", lambda e: e.tensor_scalar(out=negc, in0=egc, scalar1=-1.0, scalar2=None, op0=ALU.mult), [o_egc], [o_negc])
    for c in range(2):
        pd, opd = _slot(sp.bank(), 0)
        dl_t, o_dl = dlr[c]
        rc = slice(c * 64, c * 64 + 64)
        S.op("pe", lambda e, pd=pd, c=c: e.matmul(pd, lhsT=sel[c], rhs=g, start=True, stop=True), [o_c32, o_g], [opd])
        S.op("dve", lambda e, pd=pd, rc=rc: e.tensor_tensor(out=edl[rc, :], in0=pd[rc, :], in1=gc[rc, :], op=ALU.subtract), [opd, o_gc], [o_edl])
        S.op("act", lambda e, pd=pd, dl_t=dl_t: e.activation(out=dl_t, in_=pd, func=AF.Exp), [opd], [o_dl])
    S.op("act", lambda e: e.activation(out=edl, in_=edl, func=AF.Exp), [o_edl], [o_edl])

    def bf128(name):
        return kb.bf(name, 128)
    ld = [[kb.bf("ld%d_%d" % (par, c), 512) for c in range(12)] for par in range(2)]
    zt = [kb.bf("zt%d" % i, 512) for i in range(3)]
    PB = [[{nm: [bf128("%s%d%d_%d" % (nm, par, h, i)) for i in range(2)] for nm in ("B", "Bt", "Q")} for h in range(4)] for par in range(3)]
    PX = [[{nm: bf128("%s%d%d" % (nm, par, h)) for nm in ("aqkT", "kdec", "vtok")} for h in range(4)] for par in range(3)]
    W32x = [[{nm: t128("%s%d_%d" % (nm, h, i)) for nm in ("dg", "dgn", "E3", "E3s")} for h in range(4)] for i in range(2)]
    Sst = [t128("S%d" % h) for h in range(4)]
    Sbf = [bf128("Sbf%d" % h) for h in range(4)]
    rp = [bf128("rp%d" % h) for h in range(4)]
    vnew = [bf128("vnew%d" % h) for h in range(4)]
    qs = [t128("qs%d" % h) for h in range(4)]
    osb = [t128("osb%d" % h) for h in range(4)]
    szb = [t128("sz%d" % h) for h in range(4)]
    t1b = [t128("t1%d" % h) for h in range(4)]
    junk, o_junk = t128("junk2")
    ssn = [kb.f32("ssn%d" % h, 2) for h in range(4)]
    for h in range(4):
        S.op("pool", lambda e, h=h: e.memset(Sst[h][0], 0.0), [], [Sst[h][1]])
        S.op("pool", lambda e, h=h: e.memset(Sbf[h][0], 0.0), [], [Sbf[h][1]])

    def loads(t):
        if t % 4 == 0:
            par = (t // 4) % 2
            for c in range(12):
                buf, o_b = ld[par][c]
                S.dma("sp", buf, gqkv[c][:, t * 128:t * 128 + 512], [o_gqkv], [o_b], o_b)
        zb, o_zb = zt[t % 3]
        S.dma("sp", zb, zs[t * 128:(t + 1) * 128, :], [o_zs], [o_zb], o_zb)

    def opnd(t, h):
        par = (t // 4) % 2
        off = (t % 4) * 128
        q = (ld[par][h][0][:, off:off + 128], ld[par][h][1])
        k = (ld[par][4 + h][0][:, off:off + 128], ld[par][4 + h][1])
        v = (ld[par][8 + h][0][:, off:off + 128], ld[par][8 + h][1])
        return q, k, v

    def pre_gen(t):
        par = t % 3
        W32 = W32x[t % 2]
        loads(t)
        hs = range(4)
        pkk = {}; pkq = {}
        for h in hs:
            (qT, o_q), (kT, o_k), (vT, o_v) = opnd(t, h)
            n = t * 4 + h
            dg, o_dg = W32[h]["dg"]; dgn, o_dgn = W32[h]["dgn"]
            S.op("dve", lambda e, dg=dg, n=n: e.tensor_scalar(out=dg, in0=ident32, scalar1=gc[:, n:n + 1], scalar2=None, op0=ALU.mult),
                 [o_c32, o_gc], [o_dg])
            S.op("act", lambda e, dgn=dgn, n=n: e.activation(out=dgn, in_=ident32, func=AF.Copy, scale=ngc[:, n:n + 1]),
                 [o_c32, o_ngc], [o_dgn])
        bt1 = sp.bank(); bt2 = sp.bank()
        for h in hs:
            n = t * 4 + h
            (qT, o_q), (kT, o_k), (vT, o_v) = opnd(t, h)
            kdec, o_kdec = PX[par][h]["kdec"]; vtok, o_vtok = PX[par][h]["vtok"]
            p, op_ = _slot(bt1, h); pb16 = p.bitcast(BF16)[:, 0:128]
            S.op("pe", lambda e, pb16=pb16, kT=kT: e.transpose(pb16, kT, ident_bf), [o_k, o_cbf], [op_])
            S.op("act", lambda e, pb16=pb16, kdec=kdec, n=n: e.activation(out=kdec, in_=pb16, func=AF.Copy, scale=edl[:, n:n + 1]), [op_, o_edl], [o_kdec])
            p2, op2 = _slot(bt2, h); p2b = p2.bitcast(BF16)[:, 0:128]
            S.op("pe", lambda e, p2b=p2b, vT=vT: e.transpose(p2b, vT, ident_bf), [o_v, o_cbf], [op2])
            S.op("dve", lambda e, p2b=p2b, vtok=vtok: e.tensor_copy(out=vtok, in_=p2b), [op2], [o_vtok])
        yield
        pdl = {}
        bdl = sp.bank()
        for h in hs:
            dg, o_dg = W32[h]["dg"]; dgn, o_dgn = W32[h]["dgn"]
            pdl[h] = _slot(bdl, h)
            p, op_ = pdl[h]
            S.op("pe", lambda e, p=p, dg=dg: e.matmul(p, lhsT=ones32, rhs=dg, start=True, stop=False), [o_c32, o_dg], [op_])
            S.op("pe", lambda e, p=p, dgn=dgn: e.matmul(p, lhsT=dgn, rhs=ones32, start=False, stop=False), [o_c32, o_dgn], [op_])
            S.op("pe", lambda e, p=p: e.matmul(p, lhsT=ident32, rhs=mbcT, start=False, stop=True), [o_c32], [op_])
            E3, o_E3 = W32[h]["E3"]
            S.op("act", lambda e, p=p, E3=E3: e.activation(out=E3, in_=p, func=AF.Exp), [op_], [o_E3])
        yield
        bkk = sp.bank(); bkq = sp.bank()
        for h in hs:
            (qT, o_q), (kT, o_k), (vT, o_v) = opnd(t, h)
            pkk[h] = _slot(bkk, h); pkq[h] = _slot(bkq, h)
            S.op("pe", lambda e, kT=kT, p=pkk[h][0]: e.matmul(p, lhsT=kT, rhs=kT, start=True, stop=True), [o_k], [pkk[h][1]])
            S.op("pe", lambda e, kT=kT, qT=qT, p=pkq[h][0]: e.matmul(p, lhsT=kT, rhs=qT, start=True, stop=True), [o_k, o_q], [pkq[h][1]])
        for h in hs:
            n = t * 4 + h
            E3, o_E3 = W32[h]["E3"]; E3s, o_E3s = W32[h]["E3s"]
            aqkT, o_aqkT = PX[par][h]["aqkT"]
            S.op("dve", lambda e, p=pkq[h][0], E3=E3, aqkT=aqkT: e.tensor_tensor(out=aqkT, in0=p, in1=E3, op=ALU.mult), [pkq[h][1], o_E3], [o_aqkT])
            S.op("pool", lambda e, E3=E3, E3s=E3s: e.tensor_tensor(out=E3s, in0=E3, in1=msT, op=ALU.mult), [o_E3, o_c32], [o_E3s])
            Bt0, o_Bt0 = PB[par][h]["Bt"][0]
            S.op("dve", lambda e, p=pkk[h][0], E3s=E3s, Bt0=Bt0, n=n: e.scalar_tensor_tensor(out=Bt0, in0=p, scalar=nbet[:, n:n + 1], in1=E3s,
                                                                                            op0=ALU.mult, op1=ALU.mult), [pkk[h][1], o_nbet, o_E3s], [o_Bt0])
        yield
        btr = sp.bank()
        for h in hs:
            Bt0, o_Bt0 = PB[par][h]["Bt"][0]
            B0, o_B0 = PB[par][h]["B"][0]
            Q0, o_Q0 = PB[par][h]["Q"][0]
            p, op_ = _slot(btr, h)
            pb16 = p.bitcast(BF16)[:, 0:128]
            S.op("pe", lambda e, pb16=pb16, Bt0=Bt0: e.transpose(pb16, Bt0, ident_bf), [o_Bt0, o_cbf], [op_])
            S.op("act", lambda e, pb16=pb16, B0=B0: e.activation(out=B0, in_=pb16, func=AF.Copy), [op_], [o_B0])
            S.op("pool", lambda e, Bt0=Bt0, Q0=Q0: e.tensor_tensor(out=Q0, in0=Bt0, in1=ident_bf, op=ALU.add), [o_Bt0, o_cbf], [o_Q0])
        yield
        bB = sp.bank(); bBt = sp.bank()
        for h in hs:
            B0, o_B0 = PB[par][h]["B"][0]
            Bt0, o_Bt0 = PB[par][h]["Bt"][0]
            p1, op1 = _slot(bB, h); p2, op2 = _slot(bBt, h)
            S.op("pe", lambda e, p=p1, Bt0=Bt0, B0=B0: e.matmul(p, lhsT=Bt0, rhs=B0, start=True, stop=True), [o_Bt0, o_B0], [op1])
            S.op("pe", lambda e, p=p2, Bt0=Bt0, B0=B0: e.matmul(p, lhsT=B0, rhs=Bt0, start=True, stop=True), [o_Bt0, o_B0], [op2])
        for h in hs:
            B1, o_B1 = PB[par][h]["B"][1]
            Bt1, o_Bt1 = PB[par][h]["Bt"][1]
            p1, op1 = _slot(bB, h); p2, op2 = _slot(bBt, h)
            S.op("act", lambda e, p=p1, B1=B1: e.activation(out=B1, in_=p, func=AF.Copy), [op1], [o_B1])
            S.op("dve", lambda e, p=p2, Bt1=Bt1: e.tensor_copy(out=Bt1, in_=p), [op2], [o_Bt1])
        yield
        for it in range(1, 6):
            bQ = sp.bank()
            bB = sp.bank() if it <= 4 else None
            bBt = sp.bank() if it <= 3 else None
            for h in hs:
                Bk, o_Bk = PB[par][h]["B"][it % 2]
                Btk, o_Btk = PB[par][h]["Bt"][it % 2]
                Qp, o_Qp = PB[par][h]["Q"][(it - 1) % 2]
                p, op_ = _slot(bQ, h)
                S.op("pe", lambda e, p=p, Bk=Bk, Qp=Qp: e.matmul(p, lhsT=Bk, rhs=Qp, start=True, stop=True), [o_Bk, o_Qp], [op_])
                if it <= 4:
                    p1, op1 = _slot(bB, h)
                    S.op("pe", lambda e, p=p1, Btk=Btk, Bk=Bk: e.matmul(p, lhsT=Btk, rhs=Bk, start=True, stop=True), [o_Btk, o_Bk], [op1])
                if it <= 3:
                    p2, op2 = _slot(bBt, h)
                    S.op("pe", lambda e, p=p2, Btk=Btk, Bk=Bk: e.matmul(p, lhsT=Bk, rhs=Btk, start=True, stop=True), [o_Btk, o_Bk], [op2])
            for h in hs:
                Qp, o_Qp = PB[par][h]["Q"][(it - 1) % 2]
                Qn, o_Qn = PB[par][h]["Q"][it % 2]
                p, op_ = _slot(bQ, h)
                S.op("dve", lambda e, p=p, Qp=Qp, Qn=Qn: e.tensor_tensor(out=Qn, in0=Qp, in1=p, op=ALU.add), [op_, o_Qp], [o_Qn])
                if it <= 4:
                    Bn, o_Bn = PB[par][h]["B"][(it + 1) % 2]
                    p1, op1 = _slot(bB, h)
                    if it % 2 == 0:
                        S.op("dve", lambda e, p=p1, Bn=Bn: e.tensor_copy(out=Bn, in_=p), [op1], [o_Bn])
                    else:
                        S.op("act", lambda e, p=p1, Bn=Bn: e.activation(out=Bn, in_=p, func=AF.Copy), [op1], [o_Bn])
                if it <= 3:
                    Btn, o_Btn = PB[par][h]["Bt"][(it + 1) % 2]
                    p2, op2 = _slot(bBt, h)
                    S.op("dve", lambda e, p=p2, Btn=Btn: e.tensor_copy(out=Btn, in_=p), [op2], [o_Btn])
            yield

    def rec_gen(t):
        par = t % 3
        hs = range(4)
        zb, o_zb = zt[t % 3]
        for c in range(2):
            rc = slice(c * 64, c * 64 + 64)
            pk = {}; pq = {}
            bpk = sp.bank(); bpq = sp.bank()
            for h in hs:
                (qT, o_q), (kT, o_k), (vT, o_v) = opnd(t, h)
                pk[h] = _slot(bpk, h); pq[h] = _slot(bpq, h)
                S.op("pe", lambda e, p=pk[h][0], kT=kT, h=h: e.matmul(p, lhsT=kT, rhs=Sbf[h][0], start=True, stop=True), [o_k, Sbf[h][1]], [pk[h][1]])
                S.op("pe", lambda e, p=pq[h][0], qT=qT, h=h: e.matmul(p, lhsT=qT, rhs=Sbf[h][0], start=True, stop=True), [o_q, Sbf[h][1]], [pq[h][1]])
            for h in hs:
                n = t * 4 + h
                vtok, o_vtok = PX[par][h]["vtok"]
                S.op("dve", lambda e, p=pk[h][0], h=h, n=n, vtok=vtok, rc=rc: e.scalar_tensor_tensor(out=rp[h][0][rc, :], in0=p[rc, :], scalar=negc[rc, n:n + 1],
                                                                                             in1=vtok[rc, :], op0=ALU.mult, op1=ALU.add),
                     [pk[h][1], o_negc, o_vtok], [rp[h][1]])
                S.op("dve", lambda e, p=pq[h][0], h=h, n=n, rc=rc: e.tensor_scalar(out=qs[h][0][rc, :], in0=p[rc, :], scalar1=egc[rc, n:n + 1], scalar2=None, op0=ALU.mult),
                     [pq[h][1], o_egc], [qs[h][1]])
            yield
            pv = {}
            bpv = sp.bank()
            for h in hs:
                Q5, o_Q5 = PB[par][h]["Q"][1]
                pv[h] = _slot(bpv, h)
                S.op("pe", lambda e, p=pv[h][0], Q5=Q5, h=h, rc=rc: e.matmul(p, lhsT=Q5[rc, :], rhs=rp[h][0][rc, :], start=True, stop=True), [o_Q5, rp[h][1]], [pv[h][1]])
            for h in hs:
                n = t * 4 + h
                S.op("act", lambda e, p=pv[h][0], h=h, n=n, rc=rc: e.activation(out=vnew[h][0][rc, :], in_=p[rc, :], func=AF.Copy, scale=bet[rc, n:n + 1]),
                     [pv[h][1], o_bet], [vnew[h][1]])
            yield
            pS = {}; po_ = {}
            bpS = sp.bank(); bpo = sp.bank()
            for h in hs:
                kdec, o_kdec = PX[par][h]["kdec"]; aqkT, o_aqkT = PX[par][h]["aqkT"]
                pS[h] = _slot(bpS, h); po_[h] = _slot(bpo, h)
                S.op("pe", lambda e, p=pS[h][0], kdec=kdec, h=h, rc=rc: e.matmul(p, lhsT=kdec[rc, :], rhs=vnew[h][0][rc, :], start=True, stop=True),
                     [o_kdec, vnew[h][1]], [pS[h][1]])
                S.op("pe", lambda e, p=po_[h][0], aqkT=aqkT, h=h, rc=rc: e.matmul(p, lhsT=aqkT[rc, :], rhs=vnew[h][0][rc, :], start=True, stop=True),
                     [o_aqkT, vnew[h][1]], [po_[h][1]])
            for h in hs:
                n = t * 4 + h
                dl_t, o_dl = dlr[c]
                S.op("dve", lambda e, p=pS[h][0], h=h, n=n, dl_t=dl_t: e.scalar_tensor_tensor(out=Sst[h][0], in0=Sst[h][0], scalar=dl_t[:, n:n + 1], in1=p,
                                                                                             op0=ALU.mult, op1=ALU.add), [pS[h][1], o_dl, Sst[h][1]], [Sst[h][1]])
                S.op("act", lambda e, h=h: e.activation(out=Sbf[h][0], in_=Sst[h][0], func=AF.Copy), [Sst[h][1]], [Sbf[h][1]])
                S.op("dve", lambda e, p=po_[h][0], h=h, rc=rc: e.tensor_tensor(out=osb[h][0][rc, :], in0=p[rc, :], in1=qs[h][0][rc, :], op=ALU.add),
                     [po_[h][1], qs[h][1]], [osb[h][1]])
            yield
        for h in hs:
            ss_t, o_ss = ssn[h]
            S.op("act", lambda e, h=h, ss_t=ss_t: e.activation(out=junk, in_=osb[h][0], func=AF.Square, accum_out=ss_t[:, 0:1]), [osb[h][1]], [o_junk, o_ss])
            S.op("act", lambda e, ss_t=ss_t: e.activation(out=ss_t[:, 0:1], in_=ss_t[:, 0:1], func=AF.Ln, scale=1.0 / 128, bias=RMS_EPS), [o_ss], [o_ss])
            S.op("act", lambda e, ss_t=ss_t: e.activation(out=ss_t[:, 0:1], in_=ss_t[:, 0:1], func=AF.Exp, scale=-0.5), [o_ss], [o_ss])
            S.op("act", lambda e, h=h: e.activation(out=szb[h][0], in_=zb[:, h * 128:(h + 1) * 128], func=AF.Exp, scale=-1.0), [o_zb], [szb[h][1]])
        yield
        for h in hs:
            S.op("act", lambda e, h=h: e.activation(out=szb[h][0], in_=szb[h][0], func=AF.Ln, bias=1.0), [szb[h][1]], [szb[h][1]])
            S.op("act", lambda e, h=h: e.activation(out=szb[h][0], in_=szb[h][0], func=AF.Exp, scale=-1.0), [szb[h][1]], [szb[h][1]])
        yield
        for h in hs:
            ss_t, o_ss = ssn[h]
            S.op("dve", lambda e, h=h, ss_t=ss_t: e.scalar_tensor_tensor(out=t1b[h][0], in0=osb[h][0], scalar=ss_t[:, 0:1], in1=gnrep, op0=ALU.mult, op1=ALU.mult),
                 [osb[h][1], o_ss, o_gn], [t1b[h][1]])
            S.op("pool", lambda e, h=h: e.tensor_tensor(out=t1b[h][0], in0=t1b[h][0], in1=zb[:, h * 128:(h + 1) * 128], op=ALU.mult), [t1b[h][1], o_zb], [t1b[h][1]])
            S.op("pool", lambda e, h=h: e.tensor_tensor(out=yv[:, t, h * 128:(h + 1) * 128], in0=t1b[h][0], in1=szb[h][0], op=ALU.mult),
                 [t1b[h][1], szb[h][1]], [o_y[t]])
        yield

    gens = {}

    def adv(tt):
        try:
            next(gens[tt])
        except StopIteration:
            del gens[tt]

    gens[0] = pre_gen(0)
    while 0 in gens:
        adv(0)
        yield
    if ntile > 1:
        gens[1] = pre_gen(1)
        for _ in range(5):
            adv(1)
            yield
    for t in range(ntile):
        if t + 2 < ntile:
            gens[t + 2] = pre_gen(t + 2)
        gr = rec_gen(t)
        rec_done = False
        step = 0
        while True:
            if not rec_done:
                try:
                    next(gr)
                except StopIteration:
                    rec_done = True
            order = (t + 1, t + 2) if step % 2 == 0 else (t + 2, t + 1)
            for tt in order:
                if tt in gens:
                    adv(tt)
                    break
            step += 1
            yield
            if rec_done and (t + 1) not in gens:
                break


def host_inputs(inputs, b):
    cst, bm, kaug, qaug = _host_consts()
    f = lambda a: np.ascontiguousarray(np.asarray(a, dtype=np.float32))
    m = {
        "x": f(inputs["x"][b]),
        "ln1": f(inputs["ln1"]), "ln2": f(inputs["ln2"]), "ln_f": f(inputs["ln_f"]).reshape(1, D),
        "w_in": f(inputs["w_in"]),
        "conv_qkv_r": f(np.asarray(inputs["conv_qkv"]).reshape(DEPTH, 4, 12, 128).transpose(0, 3, 2, 1).reshape(DEPTH, 128, 48)),
        "a_log_r": f(np.tile(np.asarray(inputs["a_log"]), (1, 32)).reshape(DEPTH, 1, 128)),
        "dt_bias_r": f(np.tile(np.asarray(inputs["dt_bias"]), (1, 32)).reshape(DEPTH, 1, 128)),
        "gdn_norm": f(inputs["gdn_norm"]),
        "w_out": f(inputs["w_out"]),
        "w_gate": f(inputs["w_gate"]), "w_up": f(inputs["w_up"]),
        "ffn_conv_r": f(np.asarray(inputs["ffn_conv"]).reshape(DEPTH, 3, NFC, 128).transpose(0, 3, 2, 1).reshape(DEPTH, 128, NFC * 3)),
        "w_down": f(inputs["w_down"]),
        "cst": cst, "cstb": cst[:, 0:256].astype(ml_dtypes.bfloat16), "bm": bm.astype(ml_dtypes.bfloat16),
        "kaug": kaug.astype(ml_dtypes.bfloat16), "qaug": qaug.astype(ml_dtypes.bfloat16),
    }
    return m


_NC_CACHE = {}


def kernel(**inputs):
    n = 8
    if "full" not in _NC_CACHE:
        _NC_CACHE["full"] = build(dict())
    in_maps = [host_inputs(inputs, b) for b in range(n)]
    res = run_bass_kernel_spmd(_NC_CACHE["full"], in_maps, core_ids=list(range(n)))
    return np.stack([r["out"] for r in res.results], axis=0).astype(np.float32)
```

```python
from contextlib import ExitStack
import numpy as np
import ml_dtypes
import concourse.bass as bass
import concourse.mybir as mybir
from concourse.bass_utils import run_bass_kernel_spmd

F32 = mybir.dt.float32
BF16 = mybir.dt.bfloat16
AF = mybir.ActivationFunctionType
ALU = mybir.AluOpType

S_LEN = 4096
D = 1024
DEPTH = 2
NT = 32
IN_COLS = 3592
DFF = 2816
NFC = 22
RMS_EPS = 1e-6
L2_EPS = 1e-6
NEG = -30000.0

ENGS = ("pe", "act", "dve", "pool", "sp")


class Slot:
    __slots__ = ("name", "kind", "sem", "count")

    def __init__(self, name, kind):
        self.name = name
        self.kind = kind
        self.sem = None
        self.count = 0


class Obj:
    __slots__ = ("name", "w_ev", "r_ev", "dq")

    def __init__(self, name):
        self.name = name
        self.w_ev = {}
        self.r_ev = {}
        self.dq = {}


class Op:
    __slots__ = ("eng", "idx", "fn", "waits", "need_inc", "count", "slot", "tag")

    def __init__(self, eng, idx, fn):
        self.eng = eng
        self.idx = idx
        self.fn = fn
        self.waits = []
        self.need_inc = False
        self.count = None
        self.slot = None


def _merge(dst, src):
    for k, v in src.items():
        if dst.get(k, -1) < v:
            dst[k] = v


class Sched:
    def __init__(self, nc):
        self.nc = nc
        self.ops = {e: [] for e in ENGS}
        self.seen = {e: {} for e in ENGS}
        self.slots = []
        self.free = {"hw": [], "sw": []}
        self.phase_slots = []
        self.gen = 0
        self.bar = {}
        self.bar_gen = 0
        self.bar_applied = {e: 0 for e in ENGS}

    def keep(self):
        self.phase_slots = []

    def barrier(self):
        ev = {}
        for e in ENGS:
            n = len(self.ops[e])
            for i in range(n - 1, -1, -1):
                if self.ops[e][i].slot is None:
                    ev[("e", e)] = i
                    break
        for sl in self.slots:
            ev[("d", sl)] = sl.count
        self.bar = ev
        self.bar_gen += 1
        for sl in self.phase_slots:
            self.free[sl.kind].append(sl)
        self.phase_slots = []
        self.gen += 1

    def _slot_for(self, obj, qk):
        ent = obj.dq.get(qk)
        if ent is not None and ent[1] == self.gen:
            return ent[0]
        if self.free[qk]:
            sl = self.free[qk].pop()
        else:
            sl = Slot("%s_%d" % (qk, len(self.slots)), qk)
            self.slots.append(sl)
        self.phase_slots.append(sl)
        obj.dq[qk] = (sl, self.gen)
        return sl

    def _record(self, eng, fn, reads, writes, dma_obj=None):
        lst = self.ops[eng]
        op = Op(eng, len(lst), fn)
        op.tag = getattr(self, 'tag', '')
        need = {}
        mykey = ("e", eng)
        for o in reads:
            _merge(need, o.w_ev)
        for o in writes:
            _merge(need, o.w_ev)
            _merge(need, o.r_ev)
        if self.bar_applied[eng] != self.bar_gen:
            self.bar_applied[eng] = self.bar_gen
            for k, v in self.bar.items():
                if k == mykey:
                    continue
                if need.get(k, -1) < v:
                    need[k] = v
        seen = self.seen[eng]
        for k, v in need.items():
            if k == mykey and dma_obj is None and eng == "pe":
                continue
            if seen.get(k, -1) >= v:
                continue
            seen[k] = v
            if k[0] == "e":
                prod = self.ops[k[1]][v]
                prod.need_inc = True
                op.waits.append(("e", prod))
            else:
                op.waits.append(("d", k[1], v))
        lst.append(op)
        if dma_obj is not None:
            sl = self._slot_for(dma_obj, "sw" if eng == "pool" else "hw")
            sl.count += 1
            op.slot = sl
            ev = {("d", sl): sl.count}
        else:
            ev = {mykey: op.idx}
        for o in reads:
            _merge(o.r_ev, ev)
        for o in writes:
            if o.r_ev:
                o.w_ev = dict(ev)
                o.r_ev = {}
            else:
                _merge(o.w_ev, ev)
        return op

    def op(self, eng, fn, reads=(), writes=()):
        return self._record(eng, fn, list(reads), list(writes))

    def dma(self, eng, out, in_, reads, writes, sb_obj, **kw):
        def fn(e, out=out, in_=in_, kw=kw):
            return e.dma_start(out=out, in_=in_, **kw)
        return self._record(eng, fn, list(reads), list(writes), dma_obj=sb_obj)

    def emit(self, stack, final_wait_objs=()):
        nc = self.nc
        esem = {}
        for e in ENGS:
            if e != "sp":
                esem[e] = stack.enter_context(nc.semaphore("s_" + e))
        for sl in self.slots:
            sl.sem = stack.enter_context(nc.semaphore("d_" + sl.name))
        for e in ENGS:
            c = 0
            for op in self.ops[e]:
                if op.slot is None and op.need_inc:
                    c += 1
                    op.count = c
        block = stack.enter_context(nc.Block())

        def run(engname, e):
            for op in self.ops[engname]:
                for w in op.waits:
                    if w[0] == "e":
                        e.wait_ge(esem[w[1].eng], w[1].count)
                    else:
                        e.wait_ge(w[1].sem, 16 * w[2])
                ins = op.fn(e)
                if op.slot is not None:
                    ins.then_inc(op.slot.sem, 16)
                elif op.need_inc:
                    ins.then_inc(esem[engname], 1)
            if engname == "sp":
                for sl in self.slots:
                    e.wait_ge(sl.sem, 16 * sl.count)

        @block.tensor
        def _(e):
            run("pe", e)

        @block.scalar
        def _(e):
            run("act", e)

        @block.vector
        def _(e):
            run("dve", e)

        @block.gpsimd
        def _(e):
            run("pool", e)

        @block.sync
        def _(e):
            run("sp", e)


CST_NAMES = ["ident", "ones", "mbcT", "msT", "lmT", "bones", "sel0", "sel1"]


def _host_consts():
    i = np.arange(128)
    same = (i[:, None] // 64) == (i[None, :] // 64)
    c = {}
    c["ident"] = np.eye(128, dtype=np.float32)
    c["ones"] = np.ones((128, 128), np.float32)
    c["mbcT"] = np.where(same & (i[:, None] <= i[None, :]), 0.0, NEG).astype(np.float32)
    c["msT"] = (same & (i[:, None] < i[None, :])).astype(np.float32)
    c["lmT"] = (same & (i[:, None] <= i[None, :])).astype(np.float32)
    c["bones"] = same.astype(np.float32)
    s0 = np.zeros((128, 128), np.float32); s0[0:64, :] = 1.0
    s1 = np.zeros((128, 128), np.float32); s1[64:128, :] = 1.0
    c["sel0"] = s0
    c["sel1"] = s1
    cst = np.concatenate([c[n] for n in CST_NAMES], axis=1)
    ki = np.arange(128)[:, None, None]
    o = np.arange(17)[None, :, None]
    qi = np.arange(128)[None, None, :]
    dl = o * 128 + qi - ki
    cnt = ((dl >= 0) & (dl <= 128)).astype(np.int64) + ((dl >= 0) & (dl % 4 == 0) & (dl <= 512)) \
        + ((dl >= 0) & (dl % 16 == 0) & (dl <= 2048))
    bm = np.where(cnt > 0, np.log(np.maximum(cnt, 1).astype(np.float64)), NEG).astype(np.float32)
    bm = bm.reshape(128, 17 * 128)
    t = np.arange(S_LEN)
    slopes = 2.0 ** (-8.0 * np.arange(1, 9) / 8)
    kaug = np.zeros((8, 4, S_LEN), np.float32)
    qaug = np.zeros((8, 4, S_LEN), np.float32)
    for h in range(8):
        kaug[h, 0] = slopes[h] * (t % 128)
        kaug[h, 1] = slopes[h] * 128 * (t // 128)
        kaug[h, 2] = 1.0
        kaug[h, 3] = 1.0
        qaug[h, 0] = 1.0
        qaug[h, 1] = 1.0
        qaug[h, 2] = -slopes[h] * (t % 128)
        qaug[h, 3] = -slopes[h] * 128 * (t // 128)
    return cst, bm, kaug, qaug


class KB:
    ARENA_WORDS = 53200

    def __init__(self, nc, st, cfg):
        self.nc = nc
        self.st = st
        self.cfg = cfg
        self.S = Sched(nc)
        self.arena = st.enter_context(nc.sbuf_tensor("arena", [128, self.ARENA_WORDS], F32))
        self.top = 0
        self.pbank = [st.enter_context(nc.psum_tensor("pb%d" % i, [128, 512], F32)) for i in range(8)]
        self.pobj = [Obj("pb%d" % i) for i in range(8)]
        self.final_objs = []

    def f32(self, name, n):
        off = self.top
        self.top += n + (n & 1)
        assert self.top <= self.ARENA_WORDS, (name, self.top)
        return self.arena[:, off:off + n], Obj(name)

    def bf(self, name, n):
        assert n % 2 == 0
        off = self.top
        self.top += n // 2 + ((n // 2) & 1)
        assert self.top <= self.ARENA_WORDS, (name, self.top)
        return self.arena[:, off:off + n // 2].bitcast(BF16), Obj(name)

    def mark(self):
        return self.top

    def release(self, m):
        self.top = m
        self.S.barrier()

    def dram(self, name, shape, dt, out=False):
        out = out or (name in self.cfg.get("outs", ()))
        kind = "ExternalOutput" if out else "Internal"
        return self.nc.dram_tensor(name, list(shape), dt, kind=kind).ap()


def build(cfg):
    nc = bass.Bass("TRN2", target_bir_lowering=False)
    dbg = cfg.get("dbg", False)
    layers = cfg.get("layers", [0, 1])
    phases = cfg.get("phases", {"p1", "p2", "p3", "p4a", "p4b"})
    final_norm = cfg.get("final", True)

    def inp(name, shape, dt=F32):
        return nc.dram_tensor(name, list(shape), dt, kind="ExternalInput").ap()

    x_in = inp("x", [S_LEN, D])
    ln1 = inp("ln1", [DEPTH, D]); ln2 = inp("ln2", [DEPTH, D]); lnf = inp("ln_f", [1, D])
    w_in = inp("w_in", [DEPTH, D, IN_COLS])
    convq = inp("conv_qkv_r", [DEPTH, 128, 12 * 4])
    alog = inp("a_log_r", [DEPTH, 1, 128]); dtb = inp("dt_bias_r", [DEPTH, 1, 128])
    gdnn = inp("gdn_norm", [DEPTH, 128])
    w_out = inp("w_out", [DEPTH, D, D])
    w_gate = inp("w_gate", [DEPTH, D, DFF]); w_up = inp("w_up", [DEPTH, D, DFF])
    ffnc = inp("ffn_conv_r", [DEPTH, 128, NFC * 3])
    w_down = inp("w_down", [DEPTH, DFF, D])
    cst_d = inp("cst", [128, 8 * 128]); cstb_d = inp("cstb", [128, 2 * 128], BF16); bm_d = inp("bm", [128, 17 * 128], BF16)
    kaug_d = inp("kaug", [8, 4, S_LEN], BF16); qaug_d = inp("qaug", [8, 4, S_LEN], BF16)
    out_d = nc.dram_tensor("out", [S_LEN, D], F32, kind="ExternalOutput").ap()

    with ExitStack() as st:
        kb = KB(nc, st, cfg)
        S = kb.S
        xs = kb.dram("xs", [S_LEN, D], F32, out=(dbg or cfg.get("xs_out", False)))
        gqkv = kb.dram("gqkv", [12, 128, S_LEN], BF16, out=dbg)
        aqk = kb.dram("aqk", [8, 128, S_LEN], BF16, out=dbg)
        zs = kb.dram("zs", [S_LEN, 512], BF16, out=dbg)
        avs = kb.dram("avs", [S_LEN, 512], BF16, out=dbg)
        glog_d = kb.dram("glog", [S_LEN, 8], F32, out=dbg)
        ydbg = kb.dram("ydbg", [S_LEN, D], BF16, out=True) if (dbg or cfg.get("ydump")) else None
        o_xs = [Obj("xs%d" % i) for i in range(NT)]
        o_gqkv = Obj("gqkv"); o_aqk = Obj("aqk"); o_zs = Obj("zs"); o_avs = Obj("avs")

        cst32, o_cst32 = kb.f32("cst32", 8 * 128)
        cstbf, o_cstbf = kb.bf("cstbf", 2 * 128)
        S.dma("sp", cst32, cst_d, [], [o_cst32], o_cst32)
        S.dma("sp", cstbf, cstb_d, [], [o_cstbf], o_cstbf)
        C32 = {n: cst32[:, i * 128:(i + 1) * 128] for i, n in enumerate(CST_NAMES)}
        ident_bf = cstbf[:, 0:128]
        ones_bf = cstbf[:, 128:256]
        kb.C32 = C32; kb.o_cst32 = o_cst32; kb.ident_bf = ident_bf; kb.ones_bf = ones_bf; kb.o_cstbf = o_cstbf
        glog, o_glog = kb.f32("glog", NT * 8)
        kb.glog = glog; kb.o_glog = o_glog

        S.keep()
        base_mark = kb.mark()
        phases_all = phases
        for l in layers:
            phases = cfg.get('phases_by_layer', {}).get(l, phases_all)
            kb.cur_phases = phases
            x_src = x_in if l == layers[0] else xs
            if "p1" in phases:
                m = kb.mark()
                S.tag = 'phase1_' + str(l)
                phase1(kb, l, x_src, ln1, w_in, convq, gqkv, aqk, zs, avs, o_xs, o_gqkv, o_aqk, o_zs, o_avs, glog_d)
                kb.release(m)
            m_y = kb.mark()
            ybuf, _ = kb.bf("ybuf", NT * 1024)
            yv = ybuf.rearrange("p (t c) -> p t c", t=NT)
            o_y = [Obj("y%d" % i) for i in range(NT)]
            if "p2" in phases and "p3" in phases and cfg.get("interleave", False):
                m = kb.mark()
                g2 = phase2_gen(kb, l, gqkv, zs, o_gqkv, o_zs, alog, dtb, gdnn, yv, o_y, banks=(4, 5, 6, 7))
                g3 = phase3_gen(kb, l, aqk, avs, o_aqk, o_avs, bm_d, kaug_d, qaug_d, yv, o_y, ps_banks=(0, 1, 2), pacc_banks=(3,), look=2)
                d2 = d3 = False
                ratio = cfg.get("ratio", 4)
                while not (d2 and d3):
                    if not d2:
                        S.tag = 'phase2_' + str(l)
                        try:
                            next(g2)
                        except StopIteration:
                            d2 = True
                    for _ in range(ratio):
                        if d3:
                            break
                        S.tag = 'phase3_' + str(l)
                        try:
                            next(g3)
                        except StopIteration:
                            d3 = True
                kb.release(m)
            else:
                if "p2" in phases:
                    m = kb.mark()
                    S.tag = 'phase2_' + str(l)
                    phase2(kb, l, gqkv, zs, o_gqkv, o_zs, alog, dtb, gdnn, yv, o_y, banks=cfg.get('p2banks', range(8)))
                    kb.release(m)
                if "p3" in phases:
                    m = kb.mark()
                    S.tag = 'phase3_' + str(l)
                    phase3(kb, l, aqk, avs, o_aqk, o_avs, bm_d, kaug_d, qaug_d, yv, o_y)
                    kb.release(m)
            if (dbg and not cfg.get("lim") and ("p2" in phases or "p3" in phases) and "p4a" not in phases) or cfg.get("ydump"):
                for t in range(NT):
                    S.dma("sp", ydbg[t * 128:(t + 1) * 128, :], yv[:, t, :], [o_y[t]], [], o_y[t])
                    kb.final_objs.append(o_y[t])
            if "p4a" in phases:
                m = kb.mark()
                S.tag = 'phase4a_' + str(l)
                phase4a(kb, l, x_src, xs, o_xs, w_out, yv, o_y)
                kb.release(m)
            kb.release(m_y)
            if "p4b" in phases:
                m = kb.mark()
                last = (l == layers[-1]) and final_norm
                x4 = xs if ("p4a" in phases or l != layers[0]) else x_in
                S.tag = 'phase4b_' + str(l)
                phase4b(kb, l, x4, out_d if last else xs, o_xs, ln2, w_gate, w_up, ffnc, w_down, lnf if last else None)
                kb.release(m)
                if dbg and cfg.get("snap") and l == 0:
                    m = kb.mark()
                    xsnap = kb.dram("xsnap", [S_LEN, D], F32, out=True)
                    bufs = [kb.f32("snap%d" % i, D) for i in range(2)]
                    for t in range(NT):
                        bt, ob = bufs[t % 2]
                        S.dma("sp", bt, xs[t * 128:(t + 1) * 128, :], [o_xs[t]], [ob], ob)
                        S.dma("sp", xsnap[t * 128:(t + 1) * 128, :], bt, [ob], [], ob)
                    kb.release(m)
        npad = cfg.get("pad", 0)
        if npad:
            padt, o_pad = kb.f32("padt", 8)
            for i in range(npad):
                S.op("dve", lambda e: e.memset(padt, 0.0), [], [o_pad])
                S.op("act", lambda e: e.activation(out=padt, in_=padt, func=AF.Copy), [o_pad], [o_pad])
        S.emit(st, final_wait_objs=kb.final_objs)
    return nc


def make_stages(kb, n=3, cap=2048):
    return [kb.f32("wstage%d" % i, cap) for i in range(n)]


def load_weight(kb, name, src2d, kchunks, ncols, stages, eng="sp"):
    S = kb.S
    w, o = kb.bf(name, kchunks * ncols)
    wv = w.rearrange("p (k c) -> p k c", k=kchunks)
    srcv = src2d.rearrange("(k p) c -> p k c", p=128)
    n = getattr(kb, "wstage_n", 0)
    for k in range(kchunks):
        c0 = 0
        while c0 < ncols:
            stg, o_stg = stages[n % len(stages)]
            cap = stg.shape[-1]
            c1 = min(ncols, c0 + cap)
            S.dma(eng, stg[:, 0:c1 - c0], srcv[:, k, c0:c1], [], [o_stg], o_stg)
            dst = wv[:, k, c0:c1]
            ce = ("act", "dve")[n % 2]
            if ce == "act":
                S.op("act", lambda e, dst=dst, stg=stg, m=c1 - c0: e.activation(out=dst, in_=stg[:, 0:m], func=AF.Copy), [o_stg], [o])
            else:
                S.op(ce, lambda e, dst=dst, stg=stg, m=c1 - c0: e.tensor_copy(out=dst, in_=stg[:, 0:m]), [o_stg], [o])
            n += 1
            c0 = c1
    kb.wstage_n = n
    return wv, o


def load_weight_blocks(kb, wv, src2d, kchunks, blocks, stages, eng="sp"):
    S = kb.S
    srcv = src2d.rearrange("(k p) c -> p k c", p=128)
    n = getattr(kb, "wstage_n", 0)
    objs = []
    for (c0, c1) in blocks:
        o = Obj("wblk%d" % c0)
        objs.append(o)
        width = c1 - c0
        cap = stages[0][0].shape[-1]
        kstep = max(1, min(kchunks, cap // width))
        for k0 in range(0, kchunks, kstep):
            k1 = min(kchunks, k0 + kstep)
            stg, o_stg = stages[n % len(stages)]
            stv = stg[:, 0:(k1 - k0) * width].rearrange("p (k c) -> p k c", k=k1 - k0)
            S.dma(eng, stv, srcv[:, k0:k1, c0:c1], [], [o_stg], o_stg)
            dst = wv[:, k0:k1, c0:c1]
            ce = ("act", "dve")[n % 2]
            if ce == "act":
                S.op("act", lambda e, dst=dst, stv=stv: e.activation(out=dst, in_=stv, func=AF.Copy), [o_stg], [o])
            else:
                S.op(ce, lambda e, dst=dst, stv=stv: e.tensor_copy(out=dst, in_=stv), [o_stg], [o])
            n += 1
    kb.wstage_n = n
    return objs


def rms_stats(kb, ss, o_ss, n, inv_n, eps):
    S = kb.S
    S.op("dve", lambda e: e.tensor_scalar(out=ss, in0=ss, scalar1=inv_n, scalar2=eps, op0=ALU.mult, op1=ALU.add), [o_ss], [o_ss])
    S.op("act", lambda e: e.activation(out=ss, in_=ss, func=AF.Sqrt), [o_ss], [o_ss])
    S.op("dve", lambda e: e.reciprocal(out=ss, in_=ss), [o_ss], [o_ss])


def norm_transpose(kb, xt, o_xt, j, lnrep, o_ln, ss, o_ss, hb, o_hb, junk, o_junk, pT, o_pT, hT, o_hT, col0, evac_eng):
    S = kb.S
    xj = xt[:, j, :]
    S.op("dve", lambda e: e.scalar_tensor_tensor(out=hb, in0=xj, scalar=ss[:, j:j + 1], in1=lnrep, op0=ALU.mult, op1=ALU.mult),
         [o_xt, o_ss, o_ln], [o_hb])
    pTb = pT.bitcast(BF16)
    for k in range(8):
        S.op("pe", lambda e, k=k: e.transpose(pTb[:, k * 128:(k + 1) * 128], hb[:, k * 128:(k + 1) * 128], kb.ident_bf),
             [o_hb, kb.o_cstbf], [o_pT])
    src = pTb.rearrange("p (k t) -> p k t", k=8)
    dst = hT[:, :, col0:col0 + 128]
    if evac_eng == "act":
        S.op("act", lambda e: e.activation(out=dst, in_=src, func=AF.Copy), [o_pT], [o_hT])
    else:
        S.op("dve", lambda e: e.tensor_copy(out=dst, in_=src), [o_pT], [o_hT])


def phase4b(kb, l, x_src, x_dst, o_xs, ln2, w_gate, w_up, ffnc, w_down, lnf):
    S = kb.S
    nc = kb.nc
    actreg, _ = kb.f32("actreg", NFC * 256)
    stages = [(actreg[:, i * 1408:(i + 1) * 1408], Obj("wst%d" % i)) for i in range(4)]
    o_stages = [o for _, o in stages]
    wg_a, _ = kb.bf("wg", 8 * DFF)
    wu_a, _ = kb.bf("wu", 8 * DFF)
    wg = wg_a.rearrange("p (k c) -> p k c", k=8)
    wu = wu_a.rearrange("p (k c) -> p k c", k=8)
    lnrep, o_ln = kb.f32("ln2rep", D)
    S.dma("sp", lnrep, ln2[l:l + 1, :].partition_broadcast(128), [], [o_ln], o_ln)
    if lnf is not None:
        lnfrep, o_lnf = kb.f32("lnfrep", D)
        S.dma("sp", lnfrep, lnf[0:1, :].partition_broadcast(128), [], [o_lnf], o_lnf)
    cw, o_cw = kb.f32("ffncw", NFC * 3)
    S.dma("sp", cw, ffnc[l], [], [o_cw], o_cw)
    cwv = cw.rearrange("p (c j) -> p c j", j=3)
    halo, o_halo = kb.f32("halo", NFC * 2)
    halov = halo.rearrange("p (c j) -> p c j", j=2)
    S.op("pool", lambda e: e.memset(halo, 0.0), [], [o_halo])
    xt_a, o_xt = kb.f32("xt", 4 * D)
    xt = xt_a.rearrange("p (j d) -> p j d", j=4)
    hb, o_hb = kb.bf("hb", D)
    junk, o_junk = kb.bf("junk", D)
    ss, o_ss = kb.f32("ss", 4)
    hT_a, o_hT = kb.bf("hT", 8 * 512)
    hT = hT_a.rearrange("p (k t) -> p k t", k=8)
    actT_a, o_actT = actreg.bitcast(BF16), Obj("actT")
    actT = actT_a.rearrange("p (c t) -> p c t", c=NFC)
    pres = [kb.f32("pre%d" % i, 514) for i in range(2)]
    cvs = [kb.f32("cv%d" % i, 512) for i in range(3)]
    xo_a, o_xo, xo = xt_a, o_xt, xt
    ss2, o_ss2 = kb.f32("ss2", 4)
    pb = kb.pbank; po = kb.pobj
    S.dma("sp", xt, x_src[0:512, :].rearrange("(j p) d -> p j d", p=128), o_xs[0:4], [o_xt], o_xt)
    fblocks = [(c0, min(DFF, c0 + 512)) for c0 in range(0, DFF, 512)]
    o_wgb, o_wub = [], []
    for blk in fblocks:
        o_wgb += load_weight_blocks(kb, wg, w_gate[l], 8, [blk], stages)
        o_wub += load_weight_blocks(kb, wu, w_up[l], 8, [blk], stages)
    wd, o_wd = load_weight(kb, "wd", w_down[l], NFC, D, stages)
    for T in range(8):
        tiles = o_xs[T * 4:(T + 1) * 4]
        src = x_src[T * 512:(T + 1) * 512, :].rearrange("(j p) d -> p j d", p=128)
        if T > 0:
            S.dma("sp", xt, src, tiles, [o_xt], o_xt)
        for j in range(4):
            S.op("act", lambda e, j=j: e.activation(out=junk, in_=xt[:, j, :], func=AF.Square, accum_out=ss[:, j:j + 1]),
                 [o_xt], [o_junk, o_ss])
        rms_stats(kb, ss, o_ss, 4, 1.0 / D, RMS_EPS)
        for j in range(4):
            norm_transpose(kb, xt, o_xt, j, lnrep, o_ln, ss, o_ss, hb, o_hb, junk, o_junk, pb[6 + (j % 2)], po[6 + (j % 2)],
                           hT, o_hT, j * 128, "act" if j % 2 == 0 else "dve")
        def ffn_a(fc):
            pg, opg = pb[(2 * fc) % 6], po[(2 * fc) % 6]
            pu, opu = pb[(2 * fc + 1) % 6], po[(2 * fc + 1) % 6]
            for k in range(8):
                S.op("pe", lambda e, k=k, fc=fc, pg=pg: e.matmul(pg[:, :], lhsT=wg[:, k, fc * 128:(fc + 1) * 128], rhs=hT[:, k, :],
                                                                 start=(k == 0), stop=(k == 7)), [o_wgb[fc // 4], o_hT], [opg])
            for k in range(8):
                S.op("pe", lambda e, k=k, fc=fc, pu=pu: e.matmul(pu[:, :], lhsT=wu[:, k, fc * 128:(fc + 1) * 128], rhs=hT[:, k, :],
                                                                 start=(k == 0), stop=(k == 7)), [o_wub[fc // 4], o_hT], [opu])
            pre, o_pre = pres[fc % 2]
            cv, o_cv = cvs[fc % 3]
            S.op("pool", lambda e, pre=pre, fc=fc: e.tensor_copy(out=pre[:, 0:2], in_=halov[:, fc, :]), [o_halo], [o_pre])
            S.op("act", lambda e, pre=pre, pg=pg: e.activation(out=pre[:, 2:514], in_=pg[:, :], func=AF.Copy), [opg], [o_pre])
            S.op("pool", lambda e, pre=pre, fc=fc: e.tensor_copy(out=halov[:, fc, :], in_=pre[:, 512:514]), [o_pre], [o_halo])
            S.op("dve", lambda e, pre=pre, cv=cv, fc=fc: e.tensor_scalar(out=cv, in0=pre[:, 2:514], scalar1=cwv[:, fc, 2:3], scalar2=None,
                                                                         op0=ALU.mult), [o_pre, o_cw], [o_cv])
            S.op("dve", lambda e, pre=pre, cv=cv, fc=fc: e.scalar_tensor_tensor(out=cv, in0=pre[:, 1:513], scalar=cwv[:, fc, 1:2], in1=cv,
                                                                                op0=ALU.mult, op1=ALU.add), [o_pre, o_cw, o_cv], [o_cv])
            S.op("dve", lambda e, pre=pre, cv=cv, fc=fc: e.scalar_tensor_tensor(out=cv, in0=pre[:, 0:512], scalar=cwv[:, fc, 0:1], in1=cv,
                                                                                op0=ALU.mult, op1=ALU.add), [o_pre, o_cw, o_cv], [o_cv])

        def ffn_b(fc):
            pu, opu = pb[(2 * fc + 1) % 6], po[(2 * fc + 1) % 6]
            cv, o_cv = cvs[fc % 3]
            S.op("act", lambda e, cv=cv: e.activation(out=cv, in_=cv, func=AF.Silu), [o_cv], [o_cv])
            S.op("dve", lambda e, cv=cv, pu=pu, fc=fc: e.tensor_tensor(out=actT[:, fc, :], in0=cv, in1=pu[:, :], op=ALU.mult),
                 [o_cv, opu], [o_actT] + (o_stages if T == 0 else []))

        for s_ in range(NFC + 1):
            if s_ < NFC:
                ffn_a(s_)
            if s_ >= 1:
                ffn_b(s_ - 1)
        for j in range(4):
            for nh in range(2):
                pi = (j * 2 + nh) % 6
                pd, opd = pb[pi], po[pi]
                for fc in range(NFC):
                    S.op("pe", lambda e, fc=fc, j=j, nh=nh, pd=pd: e.matmul(pd[:, :], lhsT=actT[:, fc, j * 128:(j + 1) * 128],
                                                                          rhs=wd[:, fc, nh * 512:(nh + 1) * 512],
                                                                          start=(fc == 0), stop=(fc == NFC - 1)), [o_actT, o_wd], [opd])
                S.op("dve", lambda e, j=j, nh=nh, pd=pd: e.tensor_tensor(out=xo[:, j, nh * 512:(nh + 1) * 512],
                                                                       in0=xt[:, j, nh * 512:(nh + 1) * 512], in1=pd[:, :], op=ALU.add),
                     [o_xt, opd], [o_xo])
        dst = x_dst[T * 512:(T + 1) * 512, :].rearrange("(j p) d -> p j d", p=128)
        if lnf is not None:
            for j in range(4):
                S.op("act", lambda e, j=j: e.activation(out=junk, in_=xo[:, j, :], func=AF.Square, accum_out=ss2[:, j:j + 1]),
                     [o_xo], [o_junk, o_ss2])
            rms_stats(kb, ss2, o_ss2, 4, 1.0 / D, RMS_EPS)
            for j in range(4):
                S.op("dve", lambda e, j=j: e.scalar_tensor_tensor(out=xo[:, j, :], in0=xo[:, j, :], scalar=ss2[:, j:j + 1], in1=lnfrep,
                                                                  op0=ALU.mult, op1=ALU.mult), [o_xo, o_ss2, o_lnf], [o_xo])
            S.dma("sp", dst, xo_a.rearrange("p (j d) -> p j d", j=4), [o_xo], [], o_xo)
        else:
            S.dma("sp", dst, xo_a.rearrange("p (j d) -> p j d", j=4), [o_xo], tiles, o_xo)
        if o_xo not in kb.final_objs:
            kb.final_objs.append(o_xo)


def phase1(kb, l, x_src, ln1, w_in, convq, gqkv, aqk, zs, avs, o_xs, o_gqkv, o_aqk, o_zs, o_avs, glog_d):
    S = kb.S
    w_a, _ = kb.bf("win", 8 * IN_COLS)
    w = w_a.rearrange("p (k c) -> p k c", k=8)
    lnrep, o_ln = kb.f32("ln1rep", D)
    S.dma("sp", lnrep, ln1[l:l + 1, :].partition_broadcast(128), [], [o_ln], o_ln)
    cw, o_cw = kb.f32("qkvcw", 48)
    S.dma("sp", cw, convq[l], [], [o_cw], o_cw)
    cwv = cw.rearrange("p (c j) -> p c j", j=4)
    pre_a, _ = kb.f32("qkvpre", 12 * 516)
    prev = pre_a.rearrange("p (c t) -> p c t", c=12)
    o_pre = [Obj("pre%d" % c) for c in range(12)]
    S.op("pool", lambda e: e.memset(pre_a, 0.0), [], o_pre)
    xts = [kb.f32("xt%d" % i, 4 * D) for i in range(2)]
    hb, o_hb = kb.bf("hb", D)
    junk, o_junk = kb.bf("junk", D)
    ss, o_ss = kb.f32("ss", 4)
    hTs = [kb.bf("hT%d" % i, 8 * 512) for i in range(2)]
    cvs = [kb.f32("cv%d" % i, 512) for i in range(3)]
    sqs = [kb.bf("sq%d" % i, 512) for i in range(2)]
    rns = [kb.f32("rn%d" % i, 512) for i in range(2)]
    stg = [kb.bf("stg%d" % i, 512) for i in range(4)]
    zst_a, o_zst = kb.bf("zst", 4 * 512)
    zst = zst_a.rearrange("p (j c) -> p j c", j=4)
    avst_a, o_avst = kb.bf("avst", 4 * 512)
    avst = avst_a.rearrange("p (j c) -> p j c", j=4)
    glv = kb.glog.rearrange("p (t c) -> p t c", c=8)
    pb = kb.pbank; po = kb.pobj
    nstg = 0
    npb = 0
    def load_x(T):
        xt_a, o_xt = xts[T % 2]
        src = x_src[T * 512:(T + 1) * 512, :].rearrange("(j p) d -> p j d", p=128)
        S.dma("sp", xt_a.rearrange("p (j d) -> p j d", j=4), src, o_xs[T * 4:(T + 1) * 4], [o_xt], o_xt)

    load_x(0)
    wblocks = [(0, 512), (512, 1024), (1024, 1536), (2056, 2568), (2568, 3080), (1536, 2048), (3080, 3592), (2048, 2056)]
    wobjs = load_weight_blocks(kb, w, w_in[l], 8, wblocks, make_stages(kb, n=4, cap=2048))
    o_wcol = {}
    for (c0, c1), o in zip(wblocks, wobjs):
        for c in range(c0, c1, 8):
            o_wcol[c] = o

    def o_wc(c0):
        return o_wcol[c0 - (c0 % 8)]
    for T in range(8):
        tiles = o_xs[T * 4:(T + 1) * 4]
        xt_a, o_xt = xts[T % 2]
        xt = xt_a.rearrange("p (j d) -> p j d", j=4)
        hT_a, o_hT = hTs[T % 2]
        hT = hT_a.rearrange("p (k t) -> p k t", k=8)
        if T + 1 < 8:
            load_x(T + 1)
        for j in range(4):
            S.op("act", lambda e, j=j, xt=xt: e.activation(out=junk, in_=xt[:, j, :], func=AF.Square, accum_out=ss[:, j:j + 1]),
                 [o_xt], [o_junk, o_ss])
        rms_stats(kb, ss, o_ss, 4, 1.0 / D, RMS_EPS)
        for j in range(4):
            norm_transpose(kb, xt, o_xt, j, lnrep, o_ln, ss, o_ss, hb, o_hb, junk, o_junk, pb[6 + (j % 2)], po[6 + (j % 2)],
                           hT, o_hT, j * 128, "act" if j % 2 == 0 else "dve")
        tsl = slice(T * 512, (T + 1) * 512)
        def stage_a(c):
            pg, opg = pb[c % 4], po[c % 4]
            for k in range(8):
                S.op("pe", lambda e, k=k, c=c, pg=pg, hT=hT: e.matmul(pg[:, :], lhsT=w[:, k, c * 128:(c + 1) * 128], rhs=hT[:, k, :],
                                                                      start=(k == 0), stop=(k == 7)), [o_wc(c * 128), o_hT], [opg])
            opre = o_pre[c]
            S.op("act", lambda e, c=c, pg=pg: e.activation(out=prev[:, c, 3:515], in_=pg[:, :], func=AF.Copy), [opg], [opre])
            cv, o_cv = cvs[c % 3]
            S.op("act", lambda e, c=c, cv=cv, pg=pg: e.activation(out=cv, in_=pg[:, :], func=AF.Copy, scale=cwv[:, c, 3:4]), [opg, o_cw], [o_cv])
            for jt in (2, 1, 0):
                S.op("dve", lambda e, c=c, cv=cv, jt=jt: e.scalar_tensor_tensor(out=cv, in0=prev[:, c, jt:jt + 512], scalar=cwv[:, c, jt:jt + 1],
                                                                                in1=cv, op0=ALU.mult, op1=ALU.add), [opre, o_cw, o_cv], [o_cv])
            S.op("pool", lambda e, c=c: e.tensor_copy(out=prev[:, c, 0:3], in_=prev[:, c, 512:515]), [opre], [opre])

        def stage_b(c):
            cv, o_cv = cvs[c % 3]
            if c >= 8:
                st_t, o_st = stg[c % 4]
                S.op("act", lambda e, cv=cv, st_t=st_t: e.activation(out=st_t, in_=cv, func=AF.Silu), [o_cv], [o_st])
                S.dma("sp", gqkv[c][:, tsl], st_t, [o_st], [o_gqkv], o_st)
            else:
                sq, o_sq = sqs[c % 2]
                rn, o_rn = rns[c % 2]
                pn, opn = pb[4 + (c % 2)], po[4 + (c % 2)]
                S.op("act", lambda e, cv=cv: e.activation(out=cv, in_=cv, func=AF.Silu), [o_cv], [o_cv])
                S.op("pool", lambda e, cv=cv, sq=sq: e.tensor_tensor(out=sq, in0=cv, in1=cv, op=ALU.mult), [o_cv], [o_sq])
                S.op("pe", lambda e, sq=sq, pn=pn: e.matmul(pn[:, :], lhsT=kb.ones_bf, rhs=sq, start=True, stop=True), [kb.o_cstbf, o_sq], [opn])

        def stage_c(c):
            if c >= 8:
                return
            cv, o_cv = cvs[c % 3]
            rn, o_rn = rns[c % 2]
            st_t, o_st = stg[c % 4]
            pn, opn = pb[4 + (c % 2)], po[4 + (c % 2)]
            S.op("act", lambda e, rn=rn, pn=pn: e.activation(out=rn, in_=pn[:, :], func=AF.Sqrt, bias=L2_EPS), [opn], [o_rn])
            S.op("dve", lambda e, rn=rn: e.reciprocal(out=rn, in_=rn), [o_rn], [o_rn])
            sc = (128 ** -0.5) if c < 4 else 1.0
            S.op("dve", lambda e, cv=cv, rn=rn, st_t=st_t, sc=sc: e.scalar_tensor_tensor(out=st_t, in0=cv, scalar=sc, in1=rn,
                                                                                         op0=ALU.mult, op1=ALU.mult), [o_cv, o_rn], [o_st])
            S.dma("sp", gqkv[c][:, tsl], st_t, [o_st], [o_gqkv], o_st)

        for s_ in range(12 + 2):
            if s_ < 12:
                stage_a(s_)
            if 0 <= s_ - 1 < 12:
                stage_b(s_ - 1)
            if 0 <= s_ - 2 < 12:
                stage_c(s_ - 2)
        npb = 0
        nstg = 0
        for c in range(8):
            pg, opg = pb[npb % 4], po[npb % 4]; npb += 1
            col = 2056 + c * 128
            for k in range(8):
                S.op("pe", lambda e, k=k, col=col, pg=pg, hT=hT: e.matmul(pg[:, :], lhsT=w[:, k, col:col + 128], rhs=hT[:, k, :],
                                                                   start=(k == 0), stop=(k == 7)), [o_wc(col), o_hT], [opg])
            st_t, o_st = stg[nstg % 4]; nstg += 1
            sc = 0.125 if c < 4 else 1.0
            if c % 2 == 0:
                S.op("act", lambda e, pg=pg, st_t=st_t, sc=sc: e.activation(out=st_t, in_=pg[:, :], func=AF.Copy, scale=sc), [opg], [o_st])
            else:
                S.op("dve", lambda e, pg=pg, st_t=st_t, sc=sc: e.tensor_scalar(out=st_t, in0=pg[:, :], scalar1=sc, scalar2=None, op0=ALU.mult),
                     [opg], [o_st])
            S.dma("sp", aqk[c][:, tsl], st_t, [o_st], [o_aqk], o_st)
        for j in range(4):
            for which, col, dst_t, o_dst in ((0, 1536, zst, o_zst), (1, 3080, avst, o_avst)):
                pg, opg = pb[npb % 4], po[npb % 4]; npb += 1
                for k in range(8):
                    S.op("pe", lambda e, k=k, j=j, col=col, pg=pg, hT=hT: e.matmul(pg[:, :], lhsT=hT[:, k, j * 128:(j + 1) * 128], rhs=w[:, k, col:col + 512],
                                                                          start=(k == 0), stop=(k == 7)), [o_wc(col), o_hT], [opg])
                if which == 0:
                    S.op("act", lambda e, pg=pg, dst_t=dst_t, j=j: e.activation(out=dst_t[:, j, :], in_=pg[:, :], func=AF.Copy), [opg], [o_dst])
                else:
                    S.op("dve", lambda e, pg=pg, dst_t=dst_t, j=j: e.tensor_copy(out=dst_t[:, j, :], in_=pg[:, :]), [opg], [o_dst])
            pl, opl = pb[4 + (j % 2)], po[4 + (j % 2)]
            for k in range(8):
                S.op("pe", lambda e, k=k, j=j, pl=pl, hT=hT: e.matmul(pl[:, 0:8], lhsT=hT[:, k, j * 128:(j + 1) * 128], rhs=w[:, k, 2048:2056],
                                                               start=(k == 0), stop=(k == 7)), [o_wc(2048), o_hT], [opl])
            S.op("dve", lambda e, pl=pl, j=j, T=T: e.tensor_copy(out=glv[:, T * 4 + j, :], in_=pl[:, 0:8]), [opl], [kb.o_glog])
        S.dma("sp", zs[tsl, :].rearrange("(j p) c -> p j c", p=128), zst, [o_zst], [o_zs], o_zst)
        S.dma("sp", avs[tsl, :].rearrange("(j p) c -> p j c", p=128), avst, [o_avst], [o_avs], o_avst)
    if kb.cfg.get("dbg"):
        S.dma("sp", glog_d.rearrange("(t p) c -> p t c", p=128), glv, [kb.o_glog], [], kb.o_glog)
        kb.final_objs.append(kb.o_glog)


def phase4a(kb, l, x_src, xs, o_xs, w_out, yv, o_y):
    S = kb.S
    xts = [kb.f32("xa%d" % i, D) for i in range(2)]
    yTs = [kb.bf("yT%d" % i, 8 * 128) for i in range(2)]
    pb = kb.pbank; po = kb.pobj
    S.dma("sp", xts[0][0], x_src[0:128, :], [o_xs[0]], [xts[0][1]], xts[0][1])
    wo, o_wo = load_weight(kb, "wo", w_out[l], 8, D, make_stages(kb, n=4, cap=1024))
    for t in range(NT):
        xt, o_xt = xts[t % 2]
        yT_a, o_yT = yTs[t % 2]
        yT = yT_a.rearrange("p (k t) -> p k t", k=8)
        if t + 1 < NT:
            xn, o_xn = xts[(t + 1) % 2]
            S.dma("sp", xn, x_src[(t + 1) * 128:(t + 2) * 128, :], [o_xs[t + 1]], [o_xn], o_xn)
        pT, o_pT = pb[6 + (t % 2)], po[6 + (t % 2)]
        pTb = pT.bitcast(BF16)
        for k in range(8):
            S.op("pe", lambda e, k=k, t=t, pTb=pTb: e.transpose(pTb[:, k * 128:(k + 1) * 128], yv[:, t, k * 128:(k + 1) * 128], kb.ident_bf),
                 [o_y[t], kb.o_cstbf], [o_pT])
        if t % 2 == 0:
            S.op("act", lambda e, pTb=pTb, yT_a=yT_a: e.activation(out=yT_a, in_=pTb[:, :], func=AF.Copy), [o_pT], [o_yT])
        else:
            S.op("dve", lambda e, pTb=pTb, yT_a=yT_a: e.tensor_copy(out=yT_a, in_=pTb[:, :]), [o_pT], [o_yT])
        for nh in range(2):
            pi = (t * 2 + nh) % 6
            pd, opd = pb[pi], po[pi]
            for k in range(8):
                S.op("pe", lambda e, k=k, nh=nh, pd=pd, yT=yT: e.matmul(pd[:, :], lhsT=yT[:, k, :], rhs=wo[:, k, nh * 512:(nh + 1) * 512],
                                                                      start=(k == 0), stop=(k == 7)), [o_yT, o_wo], [opd])
            S.op("dve", lambda e, nh=nh, pd=pd, xt=xt: e.tensor_tensor(out=xt[:, nh * 512:(nh + 1) * 512], in0=xt[:, nh * 512:(nh + 1) * 512],
                                                                     in1=pd[:, :], op=ALU.add), [o_xt, opd], [o_xt])
        S.dma("sp", xs[t * 128:(t + 1) * 128, :], xt, [o_xt], [o_xs[t]], o_xt)
        if o_xt not in kb.final_objs:
            kb.final_objs.append(o_xt)


def phase3(*a, **k):
    for _ in phase3_gen(*a, **k):
        pass


def phase3_gen(kb, l, aqk, avs, o_aqk, o_avs, bm_d, kaug_d, qaug_d, yv, o_y, ps_banks=(0, 1, 2, 3, 4), pacc_banks=(5, 6, 7), look=2):
    S = kb.S
    bm, o_bm = kb.bf("bm", 17 * 128)
    S.dma("sp", bm, bm_d, [], [o_bm], o_bm)
    KTs = [kb.bf("KT%d" % i, S_LEN) for i in range(2)]
    QTs = [kb.bf("QT%d" % i, S_LEN) for i in range(2)]
    VAs = [kb.bf("VA%d" % i, NT * 66) for i in range(2)]
    PTs = [kb.bf("PT%d" % i, 512) for i in range(3)]
    rd, o_rd = kb.f32("rden", 2)
    for i in range(2):
        S.op("pool", lambda e, i=i: e.memset(KTs[i][0][64:128, :], 0.0), [], [KTs[i][1]])
        S.op("pool", lambda e, i=i: e.memset(QTs[i][0][64:128, :], 0.0), [], [QTs[i][1]])
        S.op("pool", lambda e, i=i: e.memset(VAs[i][0], 1.0), [], [VAs[i][1]])
    pb = kb.pbank; po = kb.pobj
    lim = kb.cfg.get('lim', False)
    PTs = PTs + [kb.bf("PT%d" % i, 512) for i in range(3, 5)]
    groups = []
    for h in range(2 if lim else 8):
        for qb in (list(range(3)) + [20] if lim else range(NT)):
            nkb = min(qb, 16) + 1
            for o0 in range(0, nkb, 4):
                groups.append((h, qb, o0, min(4, nkb - o0), nkb))
    loaded = set()
    state = {}

    def load_head(h):
        KT, o_KT = KTs[h % 2]
        QT, o_QT = QTs[h % 2]
        VA_a, o_VA = VAs[h % 2]
        VA = VA_a.rearrange("p (t c) -> p t c", c=66)
        r0 = (h % 2) * 64
        S.dma("sp", KT[0:64, :], aqk[4 + h // 2][r0:r0 + 64, :], [o_aqk], [o_KT], o_KT)
        S.dma("sp", QT[0:64, :], aqk[h // 2][r0:r0 + 64, :], [o_aqk], [o_QT], o_QT)
        S.dma("sp", KT[64:68, :], kaug_d[h], [], [o_KT], o_KT)
        S.dma("sp", QT[64:68, :], qaug_d[h], [], [o_QT], o_QT)
        S.dma("sp", VA[:, :, 0:64], avs[:, h * 64:(h + 1) * 64].rearrange("(t p) c -> p t c", p=128), [o_avs], [o_VA], o_VA)

    def emit_qk(gi):
        h, qb, o0, n, nkb = groups[gi]
        if h not in loaded:
            loaded.add(h)
            load_head(h)
        KT, o_KT = KTs[h % 2]
        QT, o_QT = QTs[h % 2]
        ps, ops_ = pb[ps_banks[gi % len(ps_banks)]], po[ps_banks[gi % len(ps_banks)]]
        PT, o_PT = PTs[gi % 5]
        S.op("pe", lambda e, ps=ps, o0=o0, n=n: e.matmul(ps[:, 0:n * 128], lhsT=kb.ident_bf, rhs=bm[:, o0 * 128:(o0 + n) * 128],
                                                         start=True, stop=False), [kb.o_cstbf, o_bm], [ops_])
        for o in range(o0, o0 + n):
            kbk = qb - o
            S.op("pe", lambda e, ps=ps, o=o, o0=o0, n=n, kbk=kbk, qb=qb, KT=KT, QT=QT: e.matmul(
                ps[:, (o - o0) * 128:(o - o0 + 1) * 128], lhsT=KT[:, kbk * 128:(kbk + 1) * 128], rhs=QT[:, qb * 128:(qb + 1) * 128],
                start=False, stop=(o == o0 + n - 1)), [o_KT, o_QT], [ops_])
        S.op("act", lambda e, ps=ps, PT=PT, n=n: e.activation(out=PT[:, 0:n * 128], in_=ps[:, 0:n * 128], func=AF.Exp), [ops_], [o_PT])

    def emit_pv(gi):
        h, qb, o0, n, nkb = groups[gi]
        VA_a, o_VA = VAs[h % 2]
        VA = VA_a.rearrange("p (t c) -> p t c", c=66)
        PT, o_PT = PTs[gi % 5]
        if o0 == 0:
            state["npo"] = state.get("npo", 0) + 1
        pi = pacc_banks[state["npo"] % len(pacc_banks)]
        pacc, opacc = pb[pi], po[pi]
        for o in range(o0, o0 + n):
            kbk = qb - o
            S.op("pe", lambda e, pacc=pacc, PT=PT, o=o, o0=o0, kbk=kbk, VA=VA, nkb=nkb: e.matmul(
                pacc[:, 0:65], lhsT=PT[:, (o - o0) * 128:(o - o0 + 1) * 128], rhs=VA[:, kbk, 0:65],
                start=(o == 0), stop=(o == nkb - 1)), [o_PT, o_VA], [opacc])
        if o0 + n == nkb:
            rdc = rd[:, (qb % 2):(qb % 2) + 1]
            S.op("dve", lambda e, pacc=pacc, rdc=rdc: e.reciprocal(out=rdc, in_=pacc[:, 64:65]), [opacc], [o_rd])
            S.op("dve", lambda e, pacc=pacc, rdc=rdc, qb=qb, h=h: e.tensor_scalar(out=yv[:, qb, 512 + h * 64:512 + (h + 1) * 64], in0=pacc[:, 0:64],
                                                                                  scalar1=rdc, scalar2=None, op0=ALU.mult), [opacc, o_rd], [o_y[qb]])

    LOOK = look
    G = len(groups)
    for gi in range(G + LOOK):
        if gi < G:
            emit_qk(gi)
        if gi - LOOK >= 0:
            emit_pv(gi - LOOK)
        yield


class SlotPool:
    def __init__(self, kb, banks=range(8)):
        self.banks = [(kb.pbank[b], kb.pobj[b]) for b in banks]
        self.n = 0

    def bank(self):
        b = self.banks[self.n % len(self.banks)]
        self.n += 1
        return b


def _slot(bank, h):
    return bank[0][:, h * 128:(h + 1) * 128], bank[1]


def phase2(*a, **k):
    for _ in phase2_gen(*a, **k):
        pass


def phase2_gen(kb, l, gqkv, zs, o_gqkv, o_zs, alog, dtb, gdnn, yv, o_y, banks=range(8)):
    from itertools import zip_longest
    S = kb.S
    C32 = kb.C32
    o_c32 = kb.o_cst32
    ident32 = C32["ident"]; ones32 = C32["ones"]; mbcT = C32["mbcT"]; msT = C32["msT"]; lmT = C32["lmT"]
    sel = [C32["sel0"], C32["sel1"]]
    ident_bf = kb.ident_bf; o_cbf = kb.o_cstbf
    sp = SlotPool(kb, banks)
    lim = kb.cfg.get("lim", False)
    import os
    ntile = int(os.environ.get('P2NT', '3')) if lim else NT

    def t128(name):
        return kb.f32(name, 128)

    dtbr, o_dtbr = t128("dtbr"); algr, o_algr = t128("algr"); gnrep, o_gn = t128("gnrep")
    S.dma("sp", dtbr, dtb[l].partition_broadcast(128), [], [o_dtbr], o_dtbr)
    S.dma("sp", algr, alog[l].partition_broadcast(128), [], [o_algr], o_algr)
    S.dma("sp", gnrep, gdnn[l:l + 1, :].partition_broadcast(128), [], [o_gn], o_gn)
    glv = kb.glog.rearrange("p (t c) -> p t c", c=8)
    o_gl = kb.o_glog
    g, o_g = t128("g"); bet, o_bet = t128("bet"); nbet, o_nbet = t128("nbet")
    if 'p1' not in kb.cur_phases:
        S.op("pool", lambda e: e.memset(kb.glog, 0.1), [], [o_gl])
    gc, o_gc = t128("gc"); ngc, o_ngc = t128("ngc"); egc, o_egc = t128("egc"); negc, o_negc = t128("negc")
    edl, o_edl = t128("edl"); tmp, o_tmp = t128("tmp")
    dlr = [t128("dlr0"), t128("dlr1")]
    v3 = lambda ap: ap.rearrange("p (t h) -> p t h", h=4)
    S.op("dve", lambda e: e.tensor_tensor(out=v3(tmp), in0=glv[:, :, 4:8], in1=v3(dtbr), op=ALU.add), [o_gl, o_dtbr], [o_tmp])
    S.op("act", lambda e: e.activation(out=tmp, in_=tmp, func=AF.Exp), [o_tmp], [o_tmp])
    S.op("dve", lambda e: e.tensor_scalar(out=tmp, in0=tmp, scalar1=1.0, scalar2=None, op0=ALU.add), [o_tmp], [o_tmp])
    S.op("act", lambda e: e.activation(out=tmp, in_=tmp, func=AF.Ln), [o_tmp], [o_tmp])
    S.op("act", lambda e: e.activation(out=algr, in_=algr, func=AF.Exp), [o_algr], [o_algr])
    S.op("dve", lambda e: e.scalar_tensor_tensor(out=g, in0=tmp, scalar=-1.0, in1=algr, op0=ALU.mult, op1=ALU.mult), [o_tmp, o_algr], [o_g])
    S.op("act", lambda e: e.activation(out=v3(bet), in_=glv[:, :, 0:4], func=AF.Sigmoid), [o_gl], [o_bet])
    S.op("dve", lambda e: e.tensor_scalar(out=nbet, in0=bet, scalar1=-1.0, scalar2=None, op0=ALU.mult), [o_bet], [o_nbet])
    pgc, opgc = _slot(sp.bank(), 0)
    S.op("pe", lambda e: e.matmul(pgc, lhsT=lmT, rhs=g, start=True, stop=True), [o_c32, o_g], [opgc])
    S.op("dve", lambda e: e.tensor_copy(out=gc, in_=pgc), [opgc], [o_gc])
    S.op("dve", lambda e: e.tensor_scalar(out=ngc, in0=gc, scalar1=-1.0, scalar2=None, op0=ALU.mult), [o_gc], [o_ngc])
    S.op("act", lambda e: e.activation(out=egc, in_=gc, func=AF.Exp), [o_gc], [o_egc])
    S.op("dve", lambda e: e.tensor_scalar(out=negc, in0=egc, scalar1=-1.0, scalar2=None, op0=ALU.mult), [o_egc], [o_negc])
    for c in range(2):
        pd, opd = _slot(sp.bank(), 0)
        dl_t, o_dl = dlr[c]
        rc = slice(c * 64, c * 64 + 64)
        S.op("pe", lambda e, pd=pd, c=c: e.matmul(pd, lhsT=sel[c], rhs=g, start=True, stop=True), [o_c32, o_g], [opd])
        S.op("dve", lambda e, pd=pd, rc=rc: e.tensor_tensor(out=edl[rc, :], in0=pd[rc, :], in1=gc[rc, :], op=ALU.subtract), [opd, o_gc], [o_edl])
        S.op("act", lambda e, pd=pd, dl_t=dl_t: e.activation(out=dl_t, in_=pd, func=AF.Exp), [opd], [o_dl])
    S.op("act", lambda e: e.activation(out=edl, in_=edl, func=AF.Exp), [o_edl], [o_edl])

    def bf128(name):
        return kb.bf(name, 128)
    ld = [[kb.bf("ld%d_%d" % (par, c), 512) for c in range(12)] for par in range(2)]
    zt = [kb.bf("zt%d" % i, 512) for i in range(3)]
    PB = [[{nm: [bf128("%s%d%d_%d" % (nm, par, h, i)) for i in range(2)] for nm in ("B", "Bt", "Q")} for h in range(4)] for par in range(3)]
    PX = [[{nm: bf128("%s%d%d" % (nm, par, h)) for nm in ("aqkT", "kdec", "vtok")} for h in range(4)] for par in range(3)]
    W32x = [[{nm: t128("%s%d_%d" % (nm, h, i)) for nm in ("dg", "dgn", "E3", "E3s")} for h in range(4)] for i in range(2)]
    Sst = [t128("S%d" % h) for h in range(4)]
    Sbf = [bf128("Sbf%d" % h) for h in range(4)]
    rp = [bf128("rp%d" % h) for h in range(4)]
    vnew = [bf128("vnew%d" % h) for h in range(4)]
    qs = [t128("qs%d" % h) for h in range(4)]
    osb = [t128("osb%d" % h) for h in range(4)]
    szb = [t128("sz%d" % h) for h in range(4)]
    t1b = [t128("t1%d" % h) for h in range(4)]
    junk, o_junk = t128("junk2")
    ssn = [kb.f32("ssn%d" % h, 2) for h in range(4)]
    for h in range(4):
        S.op("pool", lambda e, h=h: e.memset(Sst[h][0], 0.0), [], [Sst[h][1]])
        S.op("pool", lambda e, h=h: e.memset(Sbf[h][0], 0.0), [], [Sbf[h][1]])

    def loads(t):
        if t % 4 == 0:
            par = (t // 4) % 2
            for c in range(12):
                buf, o_b = ld[par][c]
                S.dma("sp", buf, gqkv[c][:, t * 128:t * 128 + 512], [o_gqkv], [o_b], o_b)
        zb, o_zb = zt[t % 3]
        S.dma("sp", zb, zs[t * 128:(t + 1) * 128, :], [o_zs], [o_zb], o_zb)

    def opnd(t, h):
        par = (t // 4) % 2
        off = (t % 4) * 128
        q = (ld[par][h][0][:, off:off + 128], ld[par][h][1])
        k = (ld[par][4 + h][0][:, off:off + 128], ld[par][4 + h][1])
        v = (ld[par][8 + h][0][:, off:off + 128], ld[par][8 + h][1])
        return q, k, v

    def pre_gen(t):
        par = t % 3
        W32 = W32x[t % 2]
        loads(t)
        hs = range(4)
        pkk = {}; pkq = {}
        for h in hs:
            (qT, o_q), (kT, o_k), (vT, o_v) = opnd(t, h)
            n = t * 4 + h
            dg, o_dg = W32[h]["dg"]; dgn, o_dgn = W32[h]["dgn"]
            S.op("dve", lambda e, dg=dg, n=n: e.tensor_scalar(out=dg, in0=ident32, scalar1=gc[:, n:n + 1], scalar2=None, op0=ALU.mult),
                 [o_c32, o_gc], [o_dg])
            S.op("act", lambda e, dgn=dgn, n=n: e.activation(out=dgn, in_=ident32, func=AF.Copy, scale=ngc[:, n:n + 1]),
                 [o_c32, o_ngc], [o_dgn])
        bt1 = sp.bank(); bt2 = sp.bank()
        for h in hs:
            n = t * 4 + h
            (qT, o_q), (kT, o_k), (vT, o_v) = opnd(t, h)
            kdec, o_kdec = PX[par][h]["kdec"]; vtok, o_vtok = PX[par][h]["vtok"]
            p, op_ = _slot(bt1, h); pb16 = p.bitcast(BF16)[:, 0:128]
            S.op("pe", lambda e, pb16=pb16, kT=kT: e.transpose(pb16, kT, ident_bf), [o_k, o_cbf], [op_])
            S.op("act", lambda e, pb16=pb16, kdec=kdec, n=n: e.activation(out=kdec, in_=pb16, func=AF.Copy, scale=edl[:, n:n + 1]), [op_, o_edl], [o_kdec])
            p2, op2 = _slot(bt2, h); p2b = p2.bitcast(BF16)[:, 0:128]
            S.op("pe", lambda e, p2b=p2b, vT=vT: e.transpose(p2b, vT, ident_bf), [o_v, o_cbf], [op2])
            S.op("dve", lambda e, p2b=p2b, vtok=vtok: e.tensor_copy(out=vtok, in_=p2b), [op2], [o_vtok])
        yield
        pdl = {}
        bdl = sp.bank()
        for h in hs:
            dg, o_dg = W32[h]["dg"]; dgn, o_dgn = W32[h]["dgn"]
            pdl[h] = _slot(bdl, h)
            p, op_ = pdl[h]
            S.op("pe", lambda e, p=p, dg=dg: e.matmul(p, lhsT=ones32, rhs=dg, start=True, stop=False), [o_c32, o_dg], [op_])
            S.op("pe", lambda e, p=p, dgn=dgn: e.matmul(p, lhsT=dgn, rhs=ones32, start=False, stop=False), [o_c32, o_dgn], [op_])
            S.op("pe", lambda e, p=p: e.matmul(p, lhsT=ident32, rhs=mbcT, start=False, stop=True), [o_c32], [op_])
            E3, o_E3 = W32[h]["E3"]
            S.op("act", lambda e, p=p, E3=E3: e.activation(out=E3, in_=p, func=AF.Exp), [op_], [o_E3])
        yield
        bkk = sp.bank(); bkq = sp.bank()
        for h in hs:
            (qT, o_q), (kT, o_k), (vT, o_v) = opnd(t, h)
            pkk[h] = _slot(bkk, h); pkq[h] = _slot(bkq, h)
            S.op("pe", lambda e, kT=kT, p=pkk[h][0]: e.matmul(p, lhsT=kT, rhs=kT, start=True, stop=True), [o_k], [pkk[h][1]])
            S.op("pe", lambda e, kT=kT, qT=qT, p=pkq[h][0]: e.matmul(p, lhsT=kT, rhs=qT, start=True, stop=True), [o_k, o_q], [pkq[h][1]])
        for h in hs:
            n = t * 4 + h
            E3, o_E3 = W32[h]["E3"]; E3s, o_E3s = W32[h]["E3s"]
            aqkT, o_aqkT = PX[par][h]["aqkT"]
            S.op("dve", lambda e, p=pkq[h][0], E3=E3, aqkT=aqkT: e.tensor_tensor(out=aqkT, in0=p, in1=E3, op=ALU.mult), [pkq[h][1], o_E3], [o_aqkT])
            S.op("pool", lambda e, E3=E3, E3s=E3s: e.tensor_tensor(out=E3s, in0=E3, in1=msT, op=ALU.mult), [o_E3, o_c32], [o_E3s])
            Bt0, o_Bt0 = PB[par][h]["Bt"][0]
            S.op("dve", lambda e, p=pkk[h][0], E3s=E3s, Bt0=Bt0, n=n: e.scalar_tensor_tensor(out=Bt0, in0=p, scalar=nbet[:, n:n + 1], in1=E3s,
                                                                                            op0=ALU.mult, op1=ALU.mult), [pkk[h][1], o_nbet, o_E3s], [o_Bt0])
        yield
        btr = sp.bank()
        for h in hs:
            Bt0, o_Bt0 = PB[par][h]["Bt"][0]
            B0, o_B0 = PB[par][h]["B"][0]
            Q0, o_Q0 = PB[par][h]["Q"][0]
            p, op_ = _slot(btr, h)
            pb16 = p.bitcast(BF16)[:, 0:128]
            S.op("pe", lambda e, pb16=pb16, Bt0=Bt0: e.transpose(pb16, Bt0, ident_bf), [o_Bt0, o_cbf], [op_])
            S.op("act", lambda e, pb16=pb16, B0=B0: e.activation(out=B0, in_=pb16, func=AF.Copy), [op_], [o_B0])
            S.op("pool", lambda e, Bt0=Bt0, Q0=Q0: e.tensor_tensor(out=Q0, in0=Bt0, in1=ident_bf, op=ALU.add), [o_Bt0, o_cbf], [o_Q0])
        yield
        bB = sp.bank(); bBt = sp.bank()
        for h in hs:
            B0, o_B0 = PB[par][h]["B"][0]
            Bt0, o_Bt0 = PB[par][h]["Bt"][0]
            p1, op1 = _slot(bB, h); p2, op2 = _slot(bBt, h)
            S.op("pe", lambda e, p=p1, Bt0=Bt0, B0=B0: e.matmul(p, lhsT=Bt0, rhs=B0, start=True, stop=True), [o_Bt0, o_B0], [op1])
            S.op("pe", lambda e, p=p2, Bt0=Bt0, B0=B0: e.matmul(p, lhsT=B0, rhs=Bt0, start=True, stop=True), [o_Bt0, o_B0], [op2])
        for h in hs:
            B1, o_B1 = PB[par][h]["B"][1]
            Bt1, o_Bt1 = PB[par][h]["Bt"][1]
            p1, op1 = _slot(bB, h); p2, op2 = _slot(bBt, h)
            S.op("act", lambda e, p=p1, B1=B1: e.activation(out=B1, in_=p, func=AF.Copy), [op1], [o_B1])
            S.op("dve", lambda e, p=p2, Bt1=Bt1: e.tensor_copy(out=Bt1, in_=p), [op2], [o_Bt1])
        yield
        for it in range(1, 6):
            bQ = sp.bank()
            bB = sp.bank() if it <= 4 else None
            bBt = sp.bank() if it <= 3 else None
            for h in hs:
                Bk, o_Bk = PB[par][h]["B"][it % 2]
                Btk, o_Btk = PB[par][h]["Bt"][it % 2]
                Qp, o_Qp = PB[par][h]["Q"][(it - 1) % 2]
                p, op_ = _slot(bQ, h)
                S.op("pe", lambda e, p=p, Bk=Bk, Qp=Qp: e.matmul(p, lhsT=Bk, rhs=Qp, start=True, stop=True), [o_Bk, o_Qp], [op_])
                if it <= 4:
                    p1, op1 = _slot(bB, h)
                    S.op("pe", lambda e, p=p1, Btk=Btk, Bk=Bk: e.matmul(p, lhsT=Btk, rhs=Bk, start=True, stop=True), [o_Btk, o_Bk], [op1])
                if it <= 3:
                    p2, op2 = _slot(bBt, h)
                    S.op("pe", lambda e, p=p2, Btk=Btk, Bk=Bk: e.matmul(p, lhsT=Bk, rhs=Btk, start=True, stop=True), [o_Btk, o_Bk], [op2])
            for h in hs:
                Qp, o_Qp = PB[par][h]["Q"][(it - 1) % 2]
                Qn, o_Qn = PB[par][h]["Q"][it % 2]
                p, op_ = _slot(bQ, h)
                S.op("dve", lambda e, p=p, Qp=Qp, Qn=Qn: e.tensor_tensor(out=Qn, in0=Qp, in1=p, op=ALU.add), [op_, o_Qp], [o_Qn])
                if it <= 4:
                    Bn, o_Bn = PB[par][h]["B"][(it + 1) % 2]
                    p1, op1 = _slot(bB, h)
                    if it % 2 == 0:
                        S.op("dve", lambda e, p=p1, Bn=Bn: e.tensor_copy(out=Bn, in_=p), [op1], [o_Bn])
                    else:
                        S.op("act", lambda e, p=p1, Bn=Bn: e.activation(out=Bn, in_=p, func=AF.Copy), [op1], [o_Bn])
                if it <= 3:
                    Btn, o_Btn = PB[par][h]["Bt"][(it + 1) % 2]
                    p2, op2 = _slot(bBt, h)
                    S.op("dve", lambda e, p=p2, Btn=Btn: e.tensor_copy(out=Btn, in_=p), [op2], [o_Btn])
            yield

    def rec_gen(t):
        par = t % 3
        hs = range(4)
        zb, o_zb = zt[t % 3]
        for c in range(2):
            rc = slice(c * 64, c * 64 + 64)
            pk = {}; pq = {}
            bpk = sp.bank(); bpq = sp.bank()
            for h in hs:
                (qT, o_q), (kT, o_k), (vT, o_v) = opnd(t, h)
                pk[h] = _slot(bpk, h); pq[h] = _slot(bpq, h)
                S.op("pe", lambda e, p=pk[h][0], kT=kT, h=h: e.matmul(p, lhsT=kT, rhs=Sbf[h][0], start=True, stop=True), [o_k, Sbf[h][1]], [pk[h][1]])
                S.op("pe", lambda e, p=pq[h][0], qT=qT, h=h: e.matmul(p, lhsT=qT, rhs=Sbf[h][0], start=True, stop=True), [o_q, Sbf[h][1]], [pq[h][1]])
            for h in hs:
                n = t * 4 + h
                vtok, o_vtok = PX[par][h]["vtok"]
                S.op("dve", lambda e, p=pk[h][0], h=h, n=n, vtok=vtok, rc=rc: e.scalar_tensor_tensor(out=rp[h][0][rc, :], in0=p[rc, :], scalar=negc[rc, n:n + 1],
                                                                                             in1=vtok[rc, :], op0=ALU.mult, op1=ALU.add),
                     [pk[h][1], o_negc, o_vtok], [rp[h][1]])
                S.op("dve", lambda e, p=pq[h][0], h=h, n=n, rc=rc: e.tensor_scalar(out=qs[h][0][rc, :], in0=p[rc, :], scalar1=egc[rc, n:n + 1], scalar2=None, op0=ALU.mult),
                     [pq[h][1], o_egc], [qs[h][1]])
            yield
            pv = {}
            bpv = sp.bank()
            for h in hs:
                Q5, o_Q5 = PB[par][h]["Q"][1]
                pv[h] = _slot(bpv, h)
                S.op("pe", lambda e, p=pv[h][0], Q5=Q5, h=h, rc=rc: e.matmul(p, lhsT=Q5[rc, :], rhs=rp[h][0][rc, :], start=True, stop=True), [o_Q5, rp[h][1]], [pv[h][1]])
            for h in hs:
                n = t * 4 + h
                S.op("act", lambda e, p=pv[h][0], h=h, n=n, rc=rc: e.activation(out=vnew[h][0][rc, :], in_=p[rc, :], func=AF.Copy, scale=bet[rc, n:n + 1]),
                     [pv[h][1], o_bet], [vnew[h][1]])
            yield
            pS = {}; po_ = {}
            bpS = sp.bank(); bpo = sp.bank()
            for h in hs:
                kdec, o_kdec = PX[par][h]["kdec"]; aqkT, o_aqkT = PX[par][h]["aqkT"]
                pS[h] = _slot(bpS, h); po_[h] = _slot(bpo, h)
                S.op("pe", lambda e, p=pS[h][0], kdec=kdec, h=h, rc=rc: e.matmul(p, lhsT=kdec[rc, :], rhs=vnew[h][0][rc, :], start=True, stop=True),
                     [o_kdec, vnew[h][1]], [pS[h][1]])
                S.op("pe", lambda e, p=po_[h][0], aqkT=aqkT, h=h, rc=rc: e.matmul(p, lhsT=aqkT[rc, :], rhs=vnew[h][0][rc, :], start=True, stop=True),
                     [o_aqkT, vnew[h][1]], [po_[h][1]])
            for h in hs:
                n = t * 4 + h
                dl_t, o_dl = dlr[c]
                S.op("dve", lambda e, p=pS[h][0], h=h, n=n, dl_t=dl_t: e.scalar_tensor_tensor(out=Sst[h][0], in0=Sst[h][0], scalar=dl_t[:, n:n + 1], in1=p,
                                                                                             op0=ALU.mult, op1=ALU.add), [pS[h][1], o_dl, Sst[h][1]], [Sst[h][1]])
                S.op("act", lambda e, h=h: e.activation(out=Sbf[h][0], in_=Sst[h][0], func=AF.Copy), [Sst[h][1]], [Sbf[h][1]])
                S.op("dve", lambda e, p=po_[h][0], h=h, rc=rc: e.tensor_tensor(out=osb[h][0][rc, :], in0=p[rc, :], in1=qs[h][0][rc, :], op=ALU.add),
                     [po_[h][1], qs[h][1]], [osb[h][1]])
            yield
        for h in hs:
            ss_t, o_ss = ssn[h]
            S.op("act", lambda e, h=h, ss_t=ss_t: e.activation(out=junk, in_=osb[h][0], func=AF.Square, accum_out=ss_t[:, 0:1]), [osb[h][1]], [o_junk, o_ss])
            S.op("act", lambda e, ss_t=ss_t: e.activation(out=ss_t[:, 0:1], in_=ss_t[:, 0:1], func=AF.Ln, scale=1.0 / 128, bias=RMS_EPS), [o_ss], [o_ss])
            S.op("act", lambda e, ss_t=ss_t: e.activation(out=ss_t[:, 0:1], in_=ss_t[:, 0:1], func=AF.Exp, scale=-0.5), [o_ss], [o_ss])
            S.op("act", lambda e, h=h: e.activation(out=szb[h][0], in_=zb[:, h * 128:(h + 1) * 128], func=AF.Exp, scale=-1.0), [o_zb], [szb[h][1]])
        yield
        for h in hs:
            S.op("act", lambda e, h=h: e.activation(out=szb[h][0], in_=szb[h][0], func=AF.Ln, bias=1.0), [szb[h][1]], [szb[h][1]])
            S.op("act", lambda e, h=h: e.activation(out=szb[h][0], in_=szb[h][0], func=AF.Exp, scale=-1.0), [szb[h][1]], [szb[h][1]])
        yield
        for h in hs:
            ss_t, o_ss = ssn[h]
            S.op("dve", lambda e, h=h, ss_t=ss_t: e.scalar_tensor_tensor(out=t1b[h][0], in0=osb[h][0], scalar=ss_t[:, 0:1], in1=gnrep, op0=ALU.mult, op1=ALU.mult),
                 [osb[h][1], o_ss, o_gn], [t1b[h][1]])
            S.op("pool", lambda e, h=h: e.tensor_tensor(out=t1b[h][0], in0=t1b[h][0], in1=zb[:, h * 128:(h + 1) * 128], op=ALU.mult), [t1b[h][1], o_zb], [t1b[h][1]])
            S.op("pool", lambda e, h=h: e.tensor_tensor(out=yv[:, t, h * 128:(h + 1) * 128], in0=t1b[h][0], in1=szb[h][0], op=ALU.mult),
                 [t1b[h][1], szb[h][1]], [o_y[t]])
        yield

    gens = {}

    def adv(tt):
        try:
            next(gens[tt])
        except StopIteration:
            del gens[tt]

    gens[0] = pre_gen(0)
    while 0 in gens:
        adv(0)
        yield
    if ntile > 1:
        gens[1] = pre_gen(1)
        for _ in range(5):
            adv(1)
            yield
    for t in range(ntile):
        if t + 2 < ntile:
            gens[t + 2] = pre_gen(t + 2)
        gr = rec_gen(t)
        rec_done = False
        step = 0
        while True:
            if not rec_done:
                try:
                    next(gr)
                except StopIteration:
                    rec_done = True
            order = (t + 1, t + 2) if step % 2 == 0 else (t + 2, t + 1)
            for tt in order:
                if tt in gens:
                    adv(tt)
                    break
            step += 1
            yield
            if rec_done and (t + 1) not in gens:
                break


def host_inputs(inputs, b):
    cst, bm, kaug, qaug = _host_consts()
    f = lambda a: np.ascontiguousarray(np.asarray(a, dtype=np.float32))
    m = {
        "x": f(inputs["x"][b]),
        "ln1": f(inputs["ln1"]), "ln2": f(inputs["ln2"]), "ln_f": f(inputs["ln_f"]).reshape(1, D),
        "w_in": f(inputs["w_in"]),
        "conv_qkv_r": f(np.asarray(inputs["conv_qkv"]).reshape(DEPTH, 4, 12, 128).transpose(0, 3, 2, 1).reshape(DEPTH, 128, 48)),
        "a_log_r": f(np.tile(np.asarray(inputs["a_log"]), (1, 32)).reshape(DEPTH, 1, 128)),
        "dt_bias_r": f(np.tile(np.asarray(inputs["dt_bias"]), (1, 32)).reshape(DEPTH, 1, 128)),
        "gdn_norm": f(inputs["gdn_norm"]),
        "w_out": f(inputs["w_out"]),
        "w_gate": f(inputs["w_gate"]), "w_up": f(inputs["w_up"]),
        "ffn_conv_r": f(np.asarray(inputs["ffn_conv"]).reshape(DEPTH, 3, NFC, 128).transpose(0, 3, 2, 1).reshape(DEPTH, 128, NFC * 3)),
        "w_down": f(inputs["w_down"]),
        "cst": cst, "cstb": cst[:, 0:256].astype(ml_dtypes.bfloat16), "bm": bm.astype(ml_dtypes.bfloat16),
        "kaug": kaug.astype(ml_dtypes.bfloat16), "qaug": qaug.astype(ml_dtypes.bfloat16),
    }
    return m


_NC_CACHE = {}


def kernel(**inputs):
    n = 8
    if "full" not in _NC_CACHE:
        _NC_CACHE["full"] = build(dict())
    in_maps = [host_inputs(inputs, b) for b in range(n)]
    res = run_bass_kernel_spmd(_NC_CACHE["full"], in_maps, core_ids=list(range(n)))
    return np.stack([r["out"] for r in res.results], axis=0).astype(np.float32)
```
